# Optimizing a Trainium2 kernel written in Bass

```python
import jax
import jax.numpy as jnp
from jax import lax
import numpy as np

D_MODEL = 2048
BATCH = 4
SEQ = 4096
DEPTH = 4

GRID_W = 64
CTX_LEN = 256

HEAD_DIM = 128
N_HEADS = 8
N_KV_HEADS = 2
Q_PER_KV = N_HEADS // N_KV_HEADS
ATTN_W = N_HEADS * HEAD_DIM
KV_W = N_KV_HEADS * HEAD_DIM
Q_BLOCK = 128
ROPE_THETA = 10000.0
ROPE_AXIS_DIM = HEAD_DIM // 2
ATTN_SCALE = HEAD_DIM ** -0.5

FOURIER_GROUPS = 4
FOURIER_W = FOURIER_GROUPS * HEAD_DIM

CONV_GROUPS = 4
CONV_W = CONV_GROUPS * HEAD_DIM
CONV_K = 3

GMLP_GROUPS = 4
GMLP_HEAD = HEAD_DIM
GMLP_W = GMLP_GROUPS * GMLP_HEAD
CHUNK = 128

MIX_W = ATTN_W + FOURIER_W + CONV_W + GMLP_W
SPLIT_SIZES = (ATTN_W, KV_W, KV_W, FOURIER_W, CONV_W, CONV_W, CONV_W, GMLP_W, GMLP_W)
IN_W = sum(SPLIT_SIZES)
SPLIT_POINTS = tuple(int(s) for s in np.cumsum(SPLIT_SIZES)[:-1])
OFF_K = ATTN_W
OFF_V = OFF_K + KV_W
OFF_F = OFF_V + KV_W

D_FF = 5632
FFN_CONV_K = 3

N_MOD = 6
EPS = 1e-6

kernel_name = "hybrid_parallel_groups_diffusion_block"


def rms_norm(x, g):
    xf = x.astype(jnp.float32)
    y = xf * lax.rsqrt(jnp.mean(xf * xf, axis=-1, keepdims=True) + EPS)
    return (y * g.astype(jnp.float32)).astype(x.dtype)


def layer_norm(x, g, b):
    xf = x.astype(jnp.float32)
    xc = xf - jnp.mean(xf, axis=-1, keepdims=True)
    y = xc * lax.rsqrt(jnp.mean(xc * xc, axis=-1, keepdims=True) + EPS)
    return (y * g.astype(jnp.float32) + b.astype(jnp.float32)).astype(x.dtype)


def modulate(h, shift, scale):
    return h * (1 + scale) + shift


def dwconv3(x, w):
    xp = jnp.pad(x, ((0, 0), (1, 1), (0, 0)))
    return xp[:, :-2] * w[0] + xp[:, 1:-1] * w[1] + xp[:, 2:] * w[2]


def axial_rope_tables(rows):
    row = jnp.repeat(jnp.arange(rows, dtype=jnp.float32), GRID_W)
    col = jnp.tile(jnp.arange(GRID_W, dtype=jnp.float32), rows)
    freqs = ROPE_THETA ** (-jnp.arange(0, ROPE_AXIS_DIM, 2, dtype=jnp.float32) / ROPE_AXIS_DIM)
    ang_r = row[:, None] * freqs[None, :]
    ang_c = col[:, None] * freqs[None, :]
    return (jnp.cos(ang_r), jnp.sin(ang_r), jnp.cos(ang_c), jnp.sin(ang_c))


def _rotate(xp, cos, sin):
    half = xp.shape[-1] // 2
    x1, x2 = xp[..., :half], xp[..., half:]
    cos = cos[None, :, None, :].astype(xp.dtype)
    sin = sin[None, :, None, :].astype(xp.dtype)
    return jnp.concatenate([x1 * cos - x2 * sin, x2 * cos + x1 * sin], axis=-1)


def rope_2d(x, tables):
    cos_r, sin_r, cos_c, sin_c = tables
    return jnp.concatenate([_rotate(x[..., :ROPE_AXIS_DIM], cos_r, sin_r),
                            _rotate(x[..., ROPE_AXIS_DIM:], cos_c, sin_c)], axis=-1)


def attn_kv(pk, pv, k_g):
    bsz, n = pk.shape[:2]
    k = rms_norm(pk.reshape(bsz, n, N_KV_HEADS, HEAD_DIM), k_g)
    v = pv.reshape(bsz, n, N_KV_HEADS, HEAD_DIM)
    return k, v


def gqa_attend(q, k, v):
    bsz, nq = q.shape[:2]
    qg = q.reshape(bsz, nq, N_KV_HEADS, Q_PER_KV, HEAD_DIM)
    s = jnp.einsum('bqhgd,bkhd->bhgqk', qg, k).astype(jnp.float32) * ATTN_SCALE
    p = jax.nn.softmax(s, axis=-1).astype(v.dtype)
    o = jnp.einsum('bhgqk,bkhd->bqhgd', p, v)
    return o.reshape(bsz, nq, ATTN_W)


def blocked_attention(q, k_all, v_all):
    bsz, n = q.shape[:2]
    nb = n // Q_BLOCK
    qb = q.reshape(bsz, nb, Q_BLOCK, N_HEADS, HEAD_DIM).swapaxes(0, 1)
    o = lax.map(lambda qq: gqa_attend(qq, k_all, v_all), qb)
    return o.swapaxes(0, 1).reshape(bsz, n, ATTN_W)


def fourier_mix(f):
    bsz, n = f.shape[:2]
    z = f.astype(jnp.float32).reshape(bsz, n, FOURIER_GROUPS, FOURIER_W // FOURIER_GROUPS)
    y = jnp.fft.fft2(z, axes=(1, 3), norm='ortho').real
    return y.reshape(bsz, n, FOURIER_W).astype(f.dtype)


def spatial_gating(pu, pv, ln_g, ln_b, ws, bs):
    u = jax.nn.gelu(pu)
    v = layer_norm(jax.nn.gelu(pv), ln_g, ln_b)
    bsz, n = v.shape[:2]
    vc = v.reshape(bsz, n // CHUNK, CHUNK, GMLP_GROUPS, GMLP_HEAD)
    s = jnp.einsum('gqp,bcpgd->bcqgd', ws, vc) + bs.T[None, None, :, :, None]
    return u * s.reshape(bsz, n, GMLP_W)


def token_mix(h, w_in, q_g, k_g, conv_w, gm_ln_g, gm_ln_b, gm_ws, gm_b, rope, ctx_kv):
    bsz, n = h.shape[:2]
    p = h @ w_in
    pq, pk, pv, pf, pcb, pcc, pch, pgu, pgv = jnp.split(p, SPLIT_POINTS, axis=-1)
    q = rms_norm(pq.reshape(bsz, n, N_HEADS, HEAD_DIM), q_g)
    k, v = attn_kv(pk, pv, k_g)
    if ctx_kv is None:
        attn = gqa_attend(q, k, v)
    else:
        q = rope_2d(q, rope)
        k = rope_2d(k, rope)
        k_all = jnp.concatenate([ctx_kv[0], k], axis=1)
        v_all = jnp.concatenate([ctx_kv[1], v], axis=1)
        attn = blocked_attention(q, k_all, v_all)
    four = fourier_mix(pf)
    conv = pcb * dwconv3(pcc * pch, conv_w)
    gm = spatial_gating(pgu, pgv, gm_ln_g, gm_ln_b, gm_ws, gm_b)
    mix = jnp.concatenate([attn, four, conv, gm], axis=-1)
    return mix, k, v


def conv_ffn(h, w_up, conv_w, conv_b, w_down):
    g, u = jnp.split(h @ w_up, 2, axis=-1)
    return (jax.nn.silu(dwconv3(g, conv_w) + conv_b) * u) @ w_down


def setup_inputs(seed: int = 0) -> dict:
    key = jax.random.key(seed)
    ks = jax.random.split(key, 24)
    f32 = jnp.float32

    def nrm(k, shape, s):
        return jax.random.normal(k, shape, f32) * s

    return {
        'x': nrm(ks[0], (BATCH, SEQ, D_MODEL), 1.0),
        'c': nrm(ks[1], (BATCH, D_MODEL), 1.0),
        'ctx': nrm(ks[2], (BATCH, CTX_LEN, D_MODEL), 1.0),
        'c_ctx': nrm(ks[3], (D_MODEL,), 1.0),
        'w_mod': nrm(ks[4], (DEPTH, D_MODEL, N_MOD * D_MODEL), 0.5 * D_MODEL ** -0.5),
        'b_mod': nrm(ks[5], (DEPTH, N_MOD * D_MODEL), 0.02),
        'norm1_g': 1.0 + nrm(ks[6], (DEPTH, D_MODEL), 0.02),
        'norm2_g': 1.0 + nrm(ks[7], (DEPTH, D_MODEL), 0.02),
        'w_in': nrm(ks[8], (DEPTH, D_MODEL, IN_W), D_MODEL ** -0.5),
        'q_norm_g': 1.0 + nrm(ks[9], (DEPTH, HEAD_DIM), 0.02),
        'k_norm_g': 1.0 + nrm(ks[10], (DEPTH, HEAD_DIM), 0.02),
        'conv_w': nrm(ks[11], (DEPTH, CONV_K, CONV_W), CONV_K ** -0.5),
        'gm_ln_g': 1.0 + nrm(ks[12], (DEPTH, GMLP_W), 0.02),
        'gm_ln_b': nrm(ks[13], (DEPTH, GMLP_W), 0.02),
        'gm_ws': nrm(ks[14], (DEPTH, GMLP_GROUPS, CHUNK, CHUNK), CHUNK ** -0.5),
        'gm_b': 1.0 + nrm(ks[15], (DEPTH, GMLP_GROUPS, CHUNK), 0.1),
        'w_out': nrm(ks[16], (DEPTH, MIX_W, D_MODEL), MIX_W ** -0.5),
        'w_up': nrm(ks[17], (DEPTH, D_MODEL, 2 * D_FF), D_MODEL ** -0.5),
        'ffn_conv_w': nrm(ks[18], (DEPTH, FFN_CONV_K, D_FF), FFN_CONV_K ** -0.5),
        'ffn_conv_b': nrm(ks[19], (DEPTH, D_FF), 0.02),
        'w_down': nrm(ks[20], (DEPTH, D_FF, D_MODEL), D_FF ** -0.5),
        'final_norm_g': 1.0 + nrm(ks[21], (D_MODEL,), 0.02),
    }


def reference(x, c, ctx, c_ctx, w_mod, b_mod, norm1_g, norm2_g, w_in, q_norm_g, k_norm_g,
              conv_w, gm_ln_g, gm_ln_b, gm_ws, gm_b, w_out, w_up, ffn_conv_w, ffn_conv_b,
              w_down, final_norm_g):
    n_lat = x.shape[1]
    rows = n_lat // GRID_W
    rope = axial_rope_tables(rows)
    silu_c = jax.nn.silu(c)
    silu_cc = jax.nn.silu(c_ctx)
    for l in range(DEPTH):
        mod_x = silu_c @ w_mod[l] + b_mod[l]
        mod_c = silu_cc @ w_mod[l] + b_mod[l]
        sh1, sc1, ga1, sh2, sc2, ga2 = jnp.split(mod_x[:, None, :], N_MOD, axis=-1)
        csh1, csc1, cga1, csh2, csc2, cga2 = jnp.split(mod_c, N_MOD, axis=-1)
        hc = modulate(rms_norm(ctx, norm1_g[l]), csh1, csc1)
        hx = modulate(rms_norm(x, norm1_g[l]), sh1, sc1)
        mixer_w = (w_in[l], q_norm_g[l], k_norm_g[l], conv_w[l],
                   gm_ln_g[l], gm_ln_b[l], gm_ws[l], gm_b[l])
        if l < DEPTH - 1:
            mix_c, k_c, v_c = token_mix(hc, *mixer_w, None, None)
            ctx_next = ctx + cga1 * (mix_c @ w_out[l])
            hc2 = modulate(rms_norm(ctx_next, norm2_g[l]), csh2, csc2)
            ctx_next = ctx_next + cga2 * conv_ffn(hc2, w_up[l], ffn_conv_w[l], ffn_conv_b[l], w_down[l])
        else:
            k_c, v_c = attn_kv(hc @ w_in[l][:, OFF_K:OFF_V], hc @ w_in[l][:, OFF_V:OFF_F], k_norm_g[l])
            ctx_next = ctx
        mix_x, _, _ = token_mix(hx, *mixer_w, rope, (k_c, v_c))
        x = x + ga1 * (mix_x @ w_out[l])
        hx2 = modulate(rms_norm(x, norm2_g[l]), sh2, sc2)
        x = x + ga2 * conv_ffn(hx2, w_up[l], ffn_conv_w[l], ffn_conv_b[l], w_down[l])
        ctx = ctx_next
    return rms_norm(x, final_norm_g)
```

```python
from contextlib import ExitStack
import math
import numpy as np
import concourse.bass as bass
import concourse.mybir as mybir
from concourse.bass_utils import run_bass_kernel_spmd

F32 = mybir.dt.float32
BF16 = mybir.dt.bfloat16
AF = mybir.ActivationFunctionType
ALU = mybir.AluOpType

EPOCH = 30000
DMA_EPOCH = 1800
ENGS = ('sp', 'act', 'pe', 'dve', 'pool')

D = 2048
KD = 16
INW = 4608
MIXW = 2560
DFF = 5632
NFF = 44
EPS = 1e-6
TC = 256
NCORES = 4


class Op:
    __slots__ = ('eng', 'fn', 'deps', 'need_inc', 'pos', 'key', 'dn', 'is_dma')

    def __init__(self, eng, fn, is_dma=False, key=None):
        self.eng = eng
        self.fn = fn
        self.deps = ()
        self.need_inc = False
        self.pos = -1
        self.key = key
        self.dn = -1
        self.is_dma = is_dma


class Prog:
    def __init__(self, nc):
        self.nc = nc
        self.es = ExitStack()
        self.streams = {e: [] for e in ENGS}
        self.res = {}
        self.dma_count = {}
        self.dma_since = {}
        self.last_compute = {}
        self.n_ops = 0

    def sbuf(self, name, shape, dt):
        return self.es.enter_context(self.nc.sbuf_tensor(name, list(shape), dt))

    def psum(self, name, shape, dt):
        return self.es.enter_context(self.nc.psum_tensor(name, list(shape), dt))

    def _track(self, o, reads, writes):
        deps = {}
        res = self.res
        for r in reads:
            st = res.get(r)
            if st is not None and st[0] is not None:
                deps[id(st[0])] = st[0]
        for w in writes:
            st = res.get(w)
            if st is not None:
                if st[0] is not None:
                    deps[id(st[0])] = st[0]
                for d in st[1].values():
                    deps[id(d)] = d
                for d in st[2]:
                    deps[id(d)] = d
        for r in reads:
            st = res.get(r)
            if st is None:
                st = res[r] = [None, {}, []]
            if o.is_dma:
                st[2].append(o)
            else:
                st[1][o.eng] = o
        for w in writes:
            res[w] = [o, {}, []]
        dl = []
        for d in deps.values():
            if d is o:
                continue
            if (not d.is_dma) and d.eng == 'pe' and o.eng == 'pe' and not o.is_dma:
                continue
            d.need_inc = True
            dl.append(d)
        o.deps = dl

    def add(self, eng, fn, reads=(), writes=()):
        o = Op(eng, fn)
        self._track(o, reads, writes)
        self.streams[eng].append(o)
        self.last_compute[eng] = o
        self.n_ops += 1
        return o

    def dma(self, eng, out, in_, reads=(), writes=(), key=None, fn=None):
        assert key is not None
        if fn is None:
            fn = lambda e: e.dma_start(out=out, in_=in_)
        o = Op(eng, fn, is_dma=True, key=key)
        n = self.dma_count.get(key, 0)
        o.dn = n
        self.dma_count[key] = n + 1
        self._track(o, reads, writes)
        self.streams[eng].append(o)
        self.dma_since[key] = o
        self.n_ops += 1
        return o

    def barrier(self):
        deps = list(self.last_compute.values()) + list(self.dma_since.values())
        for d in deps:
            d.need_inc = True
        for e in ENGS:
            o = Op(e, None)
            o.deps = [d for d in deps if d.is_dma or d.eng != e or e != 'pe']
            self.streams[e].append(o)
        self.dma_since = {}
        self.res = {}

    def emit(self, final_waits=()):
        nc = self.nc
        es = self.es
        npos = {}
        for e in ENGS:
            p = 0
            for o in self.streams[e]:
                if (not o.is_dma) and o.need_inc:
                    o.pos = p
                    p += 1
            npos[e] = p
        eng_sems = {}
        nsem = 0
        for e in ENGS:
            k = max(1, (npos[e] + EPOCH - 1) // EPOCH)
            eng_sems[e] = [es.enter_context(nc.semaphore(f"se_{e}_{i}")) for i in range(k)]
            nsem += k
        dma_sems = {}
        for key, cnt in self.dma_count.items():
            k = max(1, (cnt + DMA_EPOCH - 1) // DMA_EPOCH)
            dma_sems[key] = [es.enter_context(nc.semaphore(f"sd_{len(dma_sems)}_{i}")) for i in range(k)]
            nsem += k
        self.nsem = nsem
        block = es.enter_context(nc.Block())
        streams = self.streams
        final_waits = list(final_waits)

        def make_body(e):
            def body(eng):
                known = {x: -1 for x in ENGS}
                known_d = {}

                def wait_for(d):
                    if d.is_dma:
                        ep = d.dn // DMA_EPOCH
                        kk = (d.key, ep)
                        v = (d.dn % DMA_EPOCH + 1) * 16
                        if known_d.get(kk, 0) >= v:
                            return
                        known_d[kk] = v
                        eng.wait_ge(dma_sems[d.key][ep], v)
                    else:
                        if known[d.eng] >= d.pos:
                            return
                        known[d.eng] = d.pos
                        ep = d.pos // EPOCH
                        eng.wait_ge(eng_sems[d.eng][ep], d.pos % EPOCH + 1)

                for o in streams[e]:
                    for d in o.deps:
                        wait_for(d)
                    if o.fn is None:
                        continue
                    ins = o.fn(eng)
                    if o.is_dma:
                        ep = o.dn // DMA_EPOCH
                        ins.then_inc(dma_sems[o.key][ep], 16)
                    elif o.need_inc:
                        ep = o.pos // EPOCH
                        ins.then_inc(eng_sems[e][ep], 1)
                if e == 'sp':
                    for d in final_waits:
                        wait_for(d)
            return body

        block.sync(make_body('sp'))
        block.scalar(make_body('act'))
        block.tensor(make_body('pe'))
        block.vector(make_body('dve'))
        block.gpsimd(make_body('pool'))
        es.close()


class Arena:
    def __init__(self, P, nbytes):
        self.t = P.sbuf("arena", [128, nbytes // 4], F32)
        self.off = 0
        self.size = nbytes

    def alloc(self, shape, dt):
        n = 1
        for s in shape:
            n *= s
        nb = n * (4 if dt == F32 else 2)
        nb = (nb + 63) // 64 * 64
        o = self.off
        self.off += nb
        assert self.off <= self.size, ("arena overflow", self.off, self.size)
        v = self.t[:, o // 4:(o + nb) // 4]
        if dt == BF16:
            v = v.bitcast(BF16)
        v = v[:, 0:n]
        if len(shape) == 2:
            v = v.rearrange("p (a b) -> p a b", a=shape[0])
        elif len(shape) == 3:
            v = v.rearrange("p (a b c) -> p a b c", a=shape[0], b=shape[1])
        elif len(shape) == 4:
            v = v.rearrange("p (a b c d) -> p a b c d", a=shape[0], b=shape[1], c=shape[2])
        return v


def build_program(T, L, debug=False, stop_after=None):
    nc = bass.Bass("TRN2", target_bir_lowering=False)
    TK = TC + T
    NT = 512

    def din(name, shape, dt=F32):
        return nc.dram_tensor(name, list(shape), dt, kind="ExternalInput").ap()

    def dscr(name, shape, dt):
        return nc.dram_tensor(name, list(shape), dt, kind="ExternalOutput" if debug else "Internal").ap()

    x_in = din("x", [T, D])
    ctx_in = din("ctx", [TC, D])
    cvec_in = din("cvec", [128, 2, KD])
    w_mod = din("w_mod", [L, D, 6 * D])
    b_mod_t = din("b_mod_t", [128, L, 96])
    n1g_in = din("n1g", [128, L, KD])
    n2g_in = din("n2g", [128, L, KD])
    fng_in = din("fng", [128, KD])
    w_in = din("w_in", [L, D, INW])
    qg_in = din("qg", [128, L])
    kg_in = din("kg", [128, L])
    convw_in = din("convw", [128, L, 3, 4])
    lng_in = din("lng", [128, L, 4])
    lnb_in = din("lnb", [128, L, 4])
    wsT_in = din("wsT", [128, L, 4, 128])
    gmb_in = din("gmb", [128, L, 4, 128])
    w_out = din("w_out", [L, MIXW, D])
    w_up = din("w_up", [L, D, 2 * DFF])
    fcw_in = din("fcw", [128, L, 3, NFF])
    fcb_in = din("fcb", [128, L, NFF])
    w_down = din("w_down", [L, DFF, D])
    ident_in = din("ident", [128, 128])
    rrot_in = din("rrot", [128, 128])
    ropec_in = din("ropec", [128, T])
    ropes_in = din("ropes", [128, T])
    dftc_in = din("dftc", [128, 256])
    dftn_in = din("dftn", [2, T, T])
    dftnc_in = din("dftnc", [2, TC, TC])
    out = nc.dram_tensor("out", [T, D], F32, kind="ExternalOutput").ap()

    XT = dscr("XT", [KD, 128, T], F32)
    XC = dscr("XC", [KD, 128, TC], F32)
    XM = dscr("XM", [KD, 128, T], F32)
    XCM = dscr("XCM", [KD, 128, TC], F32)
    QS = dscr("QS", [8, 128, T], BF16)
    QSc = dscr("QSc", [8, 128, TC], BF16)
    KS = dscr("KS", [2, 128, TK], BF16)
    VS = dscr("VS", [2, TK, 128], BF16)
    FS = dscr("FS", [4, 128, T], BF16)
    FSc = dscr("FSc", [4, 128, TC], BF16)
    MIX = dscr("MIX", [20, 128, T], BF16)
    MIXc = dscr("MIXc", [20, 128, TC], BF16)

    Wi = nc.dram_tensor("Wi", [L, 9, 128, KD * 512], BF16, kind="Internal").ap()
    Wo = nc.dram_tensor("Wo", [L, 4, 128, 20 * 512], BF16, kind="Internal").ap()
    Wg = nc.dram_tensor("Wg", [L, 22, 128, KD * 256], BF16, kind="Internal").ap()
    Wu = nc.dram_tensor("Wu", [L, 22, 128, KD * 256], BF16, kind="Internal").ap()
    Wd = nc.dram_tensor("Wd", [L, 8, 128, NFF * 256], BF16, kind="Internal").ap()

    P = Prog(nc)
    A = Arena(P, 200 * 1024)

    def cast_layer(l):
        def c(dst, src, k):
            P.dma('pool', dst.rearrange("p (k n) -> p k n", k=k), src.rearrange("(k p) n -> p k n", p=128),
                  key='cw')
        for cb in range(9):
            c(Wi[l, cb], w_in[l, :, cb * 512:(cb + 1) * 512], KD)
        for cb in range(4):
            c(Wo[l, cb], w_out[l, :, cb * 512:(cb + 1) * 512], 20)
        for jb in range(22):
            c(Wg[l, jb], w_up[l, :, jb * 256:(jb + 1) * 256], KD)
            c(Wu[l, jb], w_up[l, :, DFF + jb * 256:DFF + (jb + 1) * 256], KD)
        for cb in range(8):
            c(Wd[l, cb], w_down[l, :, cb * 256:(cb + 1) * 256], NFF)

    def OP(eng, meth, *args, r=(), w=(), **kw):
        return P.add(eng, lambda e: getattr(e, meth)(*args, **kw), reads=r, writes=w)

    PA = [P.psum(f"pa{i}", [128, 512], F32) for i in range(3)]
    PST = P.psum("pst", [128, 512], F32)
    PST2 = P.psum("pst2", [128, 512], F32)
    PROT = P.psum("prot", [128, 512], F32)
    PSM = P.psum("psm", [128, 512], F32)
    PTR = P.psum("ptr", [128, 1024], BF16)
    PTRF = PTR[:, :].bitcast(F32)
    HB = [(PSM, 'psm'), (PTRF, 'ptr')]

    ident_f = A.alloc([128], F32)
    ident_b = A.alloc([128], BF16)
    ones_f = A.alloc([128], F32)
    ones_b = A.alloc([128], BF16)
    rrot_b = A.alloc([128], BF16)
    dftc_b = A.alloc([256], BF16)
    cst = A.alloc([4], F32)
    MOD = A.alloc([L, 96, 2], F32)
    VEC = A.alloc([2, L, 6, KD], F32)
    n1g = A.alloc([L, KD], F32)
    n2g = A.alloc([L, KD], F32)
    fng = A.alloc([KD], F32)
    qg = A.alloc([L], F32)
    kg = A.alloc([L], F32)
    convw = A.alloc([L, 3, 4], F32)
    lng = A.alloc([L, 4], F32)
    lnb = A.alloc([L, 4], F32)
    fcw = A.alloc([L, 3, NFF], F32)
    fcb = A.alloc([L, NFF], F32)
    wsT = A.alloc([4, 128], BF16)
    gmb = A.alloc([4, 128], F32)
    base_mark = A.off

    def ld(dst, src, name, key='cst0'):
        return P.dma('sp', dst, src, writes=[name], key=key)

    def ldc(dst, src, name, key='cst1'):
        return P.dma('pool', dst, src, writes=[name], key=key)

    cast_layer(0)
    ld(ident_f, ident_in, 'ident_f')
    ldc(ident_b, ident_in, 'ident_b')
    ldc(rrot_b, rrot_in, 'rrot_b')
    ldc(dftc_b, dftc_in, 'dftc_b')
    ld(n1g, n1g_in, 'n1g')
    ld(n2g, n2g_in, 'n2g')
    ld(fng, fng_in, 'fng')
    ld(qg, qg_in, 'qg', key='qgl')
    ld(kg, kg_in, 'kg')
    ld(convw, convw_in, 'convw')
    ld(lng, lng_in, 'lng')
    ld(lnb, lnb_in, 'lnb')
    ld(fcw, fcw_in, 'fcw')
    ld(fcb, fcb_in, 'fcb')
    OP('dve', 'memset', ones_f, 1.0, w=['ones_f'])
    OP('dve', 'memset', ones_b, 1.0, w=['ones_b'])
    OP('dve', 'memset', cst, 0.0, w=['cst'])
    OP('dve', 'memset', cst[:, 0:1], EPS, r=['cst'], w=['cst'])
    OP('dve', 'tensor_scalar', qg, qg, 128.0 ** -0.5, None, ALU.mult, r=['qg'], w=['qg'])
    eps_ap = cst[:, 0:1]
    P.barrier()

    m0 = A.off
    scv = A.alloc([2, KD], F32)
    bmt = A.alloc([L, 96], F32)
    wm = [A.alloc([KD, 512], F32) for _ in range(2)]
    ld(scv, cvec_in, 'scv', key='scv')
    ld(bmt, b_mod_t, 'bmt', key='bmt')
    OP('act', 'activation', out=scv, in_=scv, func=AF.Silu, r=['scv'], w=['scv'])
    it = 0
    for l in range(L):
        for cb in range(24):
            s = it % 2
            P.dma('sp', wm[s], w_mod[l, :, cb * 512:(cb + 1) * 512].rearrange("(k p) n -> p k n", p=128),
                  writes=[('wm', s)], key=f"wm{s}")
            for mi in range(4):
                for kc in range(KD):
                    OP('pe', 'matmul', PSM[:, 2 * mi:2 * mi + 2], wm[s][:, kc, mi * 128:(mi + 1) * 128],
                       scv[:, :, kc], start=(kc == 0), stop=(kc == KD - 1),
                       r=[('wm', s), 'scv'], w=['psm'])
            OP('dve', 'tensor_tensor', MOD[:, l, cb * 4:cb * 4 + 4, :],
               PSM[:, 0:8].rearrange("p (a b) -> p a b", a=4),
               bmt[:, l, cb * 4:cb * 4 + 4].unsqueeze(2).broadcast_to([128, 4, 2]), ALU.add,
               r=['psm', 'bmt'], w=['MOD'])
            it += 1
    for st in range(2):
        for l in range(L):
            for (dst, src, gain) in ((0, 16, n1g), (3, 64, n2g)):
                OP('dve', 'tensor_scalar', VEC[:, st, l, dst, :], MOD[:, l, src:src + 16, st], 1.0, None, ALU.add,
                   r=['MOD'], w=['VEC'])
                OP('dve', 'tensor_tensor', VEC[:, st, l, dst, :], VEC[:, st, l, dst, :], gain[:, l, :], ALU.mult,
                   r=['VEC', 'n1g', 'n2g'], w=['VEC'])
            for (dst, src) in ((1, 0), (2, 32), (4, 48), (5, 80)):
                OP('dve', 'tensor_copy', out=VEC[:, st, l, dst, :], in_=MOD[:, l, src:src + 16, st],
                   r=['MOD'], w=['VEC'])
    P.barrier()
    A.off = m0

    def to_feature_major(src, dst, ntok):
        m = A.off
        xin = [A.alloc([D], F32) for _ in range(2)]
        stg = [A.alloc([KD, 128], F32) for _ in range(2)]
        for tb in range(ntok // 128):
            s = tb % 2
            P.dma('sp', xin[s], src[tb * 128:(tb + 1) * 128, :], writes=[('xin', s)], key=f"xin{s}")
            for q4 in range(4):
                bank = PA[q4 % 3]
                bn = ('pa', q4 % 3)
                for j in range(4):
                    kc = q4 * 4 + j
                    OP('pe', 'transpose', bank[:, j * 128:(j + 1) * 128], xin[s][:, kc * 128:(kc + 1) * 128], ident_f,
                       r=[('xin', s), 'ident_f'], w=[bn])
                src4 = bank[:, :].rearrange("p (a b) -> p a b", a=4)
                if q4 % 2 == 0:
                    OP('act', 'activation', out=stg[s][:, q4 * 4:q4 * 4 + 4, :], in_=src4, func=AF.Copy,
                       r=[bn], w=[('stg', s)])
                else:
                    OP('dve', 'tensor_copy', out=stg[s][:, q4 * 4:q4 * 4 + 4, :], in_=src4, r=[bn], w=[('stg', s)])
            P.dma('sp', dst[:, :, tb * 128:(tb + 1) * 128].rearrange("k p t -> p k t"), stg[s],
                  reads=[('stg', s)], key=f"s_stg{s}")
        P.barrier()
        A.off = m

    to_feature_major(x_in, XT, T)
    to_feature_major(ctx_in, XC, TC)

    def load_xt(xt, XD, ntot, s0, N, halo):
        if halo:
            lo = max(s0 - 1, 0)
            hi = min(s0 + N + 1, ntot)
            c0 = lo - (s0 - 1)
            if s0 == 0:
                OP('dve', 'memset', xt[:, :, 0:1], 0.0, w=['xt'])
            if s0 + N == ntot:
                OP('dve', 'memset', xt[:, :, N + 1:N + 2], 0.0, w=['xt'])
            if hi - lo == N + 2:
                P.dma('sp', xt[:, :, 0:N + 1], XD[:, :, lo:hi - 1].rearrange("k p t -> p k t"),
                      writes=['xt'], key='xt')
                P.dma('sp', xt[:, :, N:N + 2], XD[:, :, hi - 2:hi].rearrange("k p t -> p k t"),
                      reads=['xt'], writes=['xt'], key='xt')
            else:
                P.dma('sp', xt[:, :, c0:c0 + (hi - lo)], XD[:, :, lo:hi].rearrange("k p t -> p k t"),
                      writes=['xt'], key='xt')
        else:
            P.dma('sp', xt[:, :, 0:N], XD[:, :, s0:s0 + N].rearrange("k p t -> p k t"), writes=['xt'], key='xt')

    def norm_mod(xt, h, W, gvec, svec, sq, tmp, rstd, first, last):
        W0 = min(W, 512)
        for kc in range(KD):
            s = kc % 2
            OP('act', 'activation', out=sq[s][:, 0:W], in_=xt[:, kc, 0:W], func=AF.Square,
               r=['xt'], w=[('sq', s)])
            OP('pe', 'matmul', PST[:, 0:W0], ones_f, sq[s][:, 0:W0], start=(kc == 0), stop=(kc == KD - 1),
               r=[('sq', s), 'ones_f'], w=['pst'])
            if W > 512:
                OP('pe', 'matmul', PROT[:, 0:W - 512], ones_f, sq[s][:, 512:W], start=(kc == 0),
                   stop=(kc == KD - 1), r=[('sq', s), 'ones_f'], w=['prot'])
        OP('act', 'activation', out=rstd[:, 0:W0], in_=PST[:, 0:W0], func=AF.Sqrt, bias=eps_ap, scale=1.0 / D,
           r=['pst', 'cst'], w=['rstd'])
        if W > 512:
            OP('act', 'activation', out=rstd[:, 512:W], in_=PROT[:, 0:W - 512], func=AF.Sqrt, bias=eps_ap,
               scale=1.0 / D, r=['prot', 'cst'], w=['rstd'])
        OP('dve', 'reciprocal', rstd[:, 0:W], rstd[:, 0:W], r=['rstd'], w=['rstd'])
        for kc in range(KD):
            s = kc % 2
            OP('dve', 'tensor_tensor', tmp[s][:, 0:W], xt[:, kc, 0:W], rstd[:, 0:W], ALU.mult,
               r=['xt', 'rstd'], w=[('tmp', s)])
            OP('act', 'activation', out=h[:, kc, 0:W], in_=tmp[s][:, 0:W], func=AF.Identity,
               bias=svec[:, kc:kc + 1], scale=gvec[:, kc:kc + 1], r=[('tmp', s), 'VEC'], w=[('h', kc)])
        if first:
            OP('dve', 'memset', h[:, :, 0:1], 0.0, r=[('h', k) for k in range(KD)], w=[('h', k) for k in range(KD)])
        if last:
            OP('dve', 'memset', h[:, :, W - 1:W], 0.0, r=[('h', k) for k in range(KD)],
               w=[('h', k) for k in range(KD)])

    def store(dst, src, res):
        key = "s_" + (res if isinstance(res, str) else f"{res[0]}{res[1]}")
        return P.dma('sp', dst, src, reads=[res], key=key)

    def phase1(l, stream, kv_only):
        is_ctx = (stream == 1)
        ntot = TC if is_ctx else T
        N = min(NT, ntot)
        XD = XC if is_ctx else XT
        QD = QSc if is_ctx else QS
        FD = FSc if is_ctx else FS
        MD = MIXc if is_ctx else MIX
        koff = 0 if is_ctx else TC
        W = N + 2
        nch = N // 128
        m = A.off
        xt = A.alloc([KD, W], F32)
        h = A.alloc([KD, W], BF16)
        sq = [A.alloc([W], F32) for _ in range(2)]
        tmp = [A.alloc([W], F32) for _ in range(2)]
        rstd = A.alloc([W], F32)
        wb = [A.alloc([KD, 512], BF16) for _ in range(2)]
        qf = A.alloc([N], F32)
        sqb = A.alloc([N], F32)
        rq = A.alloc([N], F32)
        qnb = A.alloc([N], BF16)
        t1 = A.alloc([N], F32)
        t2 = A.alloc([N], F32)
        qo = [A.alloc([N], BF16) for _ in range(2)]
        vb = A.alloc([N], BF16)
        vt = [A.alloc([4, 128], BF16) for _ in range(2)]
        fb = [A.alloc([N], BF16) for _ in range(2)]
        cbk = A.alloc([4, N], F32)
        ccb = A.alloc([4, W], F32)
        prod = A.alloc([W], F32)
        cv = A.alloc([N], F32)
        mixb = [A.alloc([N], BF16) for _ in range(2)]
        ub = A.alloc([4, N], F32)
        gvb = A.alloc([4, N], F32)
        mn = A.alloc([N], F32)
        msq = A.alloc([N], F32)
        lrs = A.alloc([N], F32)
        vh = A.alloc([N], BF16)
        vT = A.alloc([4, 128], BF16)
        rc = A.alloc([N], F32)
        rs = A.alloc([N], F32)
        gvec = VEC[:, stream, l, 0, :]
        svec = VEC[:, stream, l, 1, :]
        if not kv_only:
            ldc(wsT, wsT_in[:, l], 'wsT', key='wsT')
            ld(gmb, gmb_in[:, l], 'gmb', key='gmb')
        blocks = list(range(8, 12)) if kv_only else list(range(36))
        cbs = sorted(set(b // 4 for b in blocks))
        wit = 0
        for ti in range(ntot // N):
            s0 = ti * N
            load_xt(xt, XD, ntot, s0, N, True)
            if not is_ctx:
                P.dma('sp', rc, ropec_in[:, s0:s0 + N], writes=['rc'], key='rc')
                P.dma('sp', rs, ropes_in[:, s0:s0 + N], writes=['rs'], key='rs')
            norm_mod(xt, h, W, gvec, svec, sq, tmp, rstd, s0 == 0, s0 + N == ntot)
            mmi = 0
            for cb in cbs:
                ws = wit % 2
                wit += 1
                P.dma('pool', wb[ws], Wi[l, cb].rearrange("p (k n) -> p k n", k=KD),
                      writes=[('wb', ws)], key=f"wb{ws}")
                for mi in range(4):
                    mb = cb * 4 + mi
                    if mb not in blocks:
                        continue
                    bi = mmi % 3
                    mmi += 1
                    bank = PA[bi]
                    bn = ('pa', bi)
                    is_halo = 20 <= mb < 28
                    hbk, hbn = HB[mb % 2]
                    for kc in range(KD):
                        OP('pe', 'matmul', bank[:, 0:N], wb[ws][:, kc, mi * 128:(mi + 1) * 128], h[:, kc, 1:1 + N],
                           start=(kc == 0), stop=(kc == KD - 1), r=[('wb', ws), ('h', kc)], w=[bn])
                        if is_halo:
                            OP('pe', 'matmul', hbk[:, 0:2], wb[ws][:, kc, mi * 128:(mi + 1) * 128],
                               h[:, kc, 0:W:W - 1], start=(kc == 0), stop=(kc == KD - 1),
                               r=[('wb', ws), ('h', kc)], w=[hbn])
                    if mb < 10:
                        isq = mb < 8
                        OP('act', 'activation', out=qf, in_=bank[:, 0:N], func=AF.Copy, r=[bn], w=['qf'])
                        OP('act', 'activation', out=sqb, in_=bank[:, 0:N], func=AF.Square, r=[bn], w=['sqb'])
                        OP('pe', 'matmul', PST2[:, 0:N], ones_f, sqb, start=True, stop=True,
                           r=['sqb', 'ones_f'], w=['pst2'])
                        OP('act', 'activation', out=rq, in_=PST2[:, 0:N], func=AF.Sqrt, bias=eps_ap, scale=1.0 / 128,
                           r=['pst2', 'cst'], w=['rq'])
                        OP('dve', 'reciprocal', rq, rq, r=['rq'], w=['rq'])
                        gq = (qg if isq else kg)[:, l:l + 1]
                        qs_ = qo[mb % 2]
                        qn_ = ('qo', mb % 2)
                        if is_ctx:
                            OP('dve', 'scalar_tensor_tensor', qs_, qf, gq, rq, ALU.mult, ALU.mult,
                               r=['qf', 'rq', 'qg', 'kg'], w=[qn_])
                        else:
                            OP('dve', 'scalar_tensor_tensor', qnb, qf, gq, rq, ALU.mult, ALU.mult,
                               r=['qf', 'rq', 'qg', 'kg'], w=['qnb'])
                            OP('pe', 'matmul', PROT[:, 0:N], rrot_b, qnb, start=True, stop=True,
                               r=['qnb', 'rrot_b'], w=['prot'])
                            OP('dve', 'tensor_tensor', t1, qnb, rc, ALU.mult, r=['qnb', 'rc'], w=['t1'])
                            OP('dve', 'tensor_tensor', t2, PROT[:, 0:N], rs, ALU.mult, r=['prot', 'rs'], w=['t2'])
                            OP('dve', 'tensor_tensor', qs_, t1, t2, ALU.add, r=['t1', 't2'], w=[qn_])
                        if isq:
                            store(QD[mb, :, s0:s0 + N], qs_, qn_)
                        else:
                            store(KS[mb - 8, :, koff + s0:koff + s0 + N], qs_, qn_)
                    elif mb < 12:
                        hv = mb - 10
                        OP('act', 'activation', out=vb, in_=bank[:, 0:N], func=AF.Copy, r=[bn], w=['vb'])
                        for j in range(nch):
                            OP('pe', 'transpose', PTR[:, j * 128:(j + 1) * 128], vb[:, j * 128:(j + 1) * 128], ident_b,
                               r=['vb', 'ident_b'], w=['ptr'])
                        OP('dve', 'tensor_copy', out=vt[hv][:, 0:nch, :],
                           in_=PTR[:, 0:nch * 128].rearrange("p (a b) -> p a b", a=nch), r=['ptr'], w=[('vt', hv)])
                        store(VS[hv, koff + s0:koff + s0 + N, :].rearrange("(j p) d -> p j d", p=128),
                              vt[hv][:, 0:nch, :], ('vt', hv))
                    elif mb < 16:
                        g = mb - 12
                        OP('act', 'activation', out=fb[g % 2], in_=bank[:, 0:N], func=AF.Copy, r=[bn], w=[('fb', g % 2)])
                        store(FD[g, :, s0:s0 + N], fb[g % 2], ('fb', g % 2))
                    elif mb < 20:
                        g = mb - 16
                        OP('act', 'activation', out=cbk[:, g, :], in_=bank[:, 0:N], func=AF.Copy, r=[bn], w=[('cbk', g)])
                    elif mb < 24:
                        g = mb - 20
                        OP('act', 'activation', out=ccb[:, g, 1:1 + N], in_=bank[:, 0:N], func=AF.Copy,
                           r=[bn], w=[('ccb', g)])
                        OP('act', 'activation', out=ccb[:, g, 0:W:W - 1], in_=hbk[:, 0:2], func=AF.Copy,
                           r=[hbn, ('ccb', g)], w=[('ccb', g)])
                    elif mb < 28:
                        g = mb - 24
                        OP('dve', 'tensor_tensor', prod[:, 1:1 + N], ccb[:, g, 1:1 + N], bank[:, 0:N], ALU.mult,
                           r=[bn, ('ccb', g)], w=['prod'])
                        OP('dve', 'tensor_tensor', prod[:, 0:W:W - 1], ccb[:, g, 0:W:W - 1], hbk[:, 0:2],
                           ALU.mult, r=[hbn, ('ccb', g), 'prod'], w=['prod'])
                        OP('dve', 'tensor_scalar', cv, prod[:, 0:N], convw[:, l, 0, g:g + 1], None, ALU.mult,
                           r=['prod', 'convw'], w=['cv'])
                        OP('dve', 'scalar_tensor_tensor', cv, prod[:, 1:1 + N], convw[:, l, 1, g:g + 1], cv,
                           ALU.mult, ALU.add, r=['prod', 'cv', 'convw'], w=['cv'])
                        OP('dve', 'scalar_tensor_tensor', cv, prod[:, 2:2 + N], convw[:, l, 2, g:g + 1], cv,
                           ALU.mult, ALU.add, r=['prod', 'cv', 'convw'], w=['cv'])
                        OP('dve', 'tensor_tensor', mixb[g % 2], cv, cbk[:, g, :], ALU.mult,
                           r=['cv', ('cbk', g)], w=[('mixb', g % 2)])
                        store(MD[12 + g, :, s0:s0 + N], mixb[g % 2], ('mixb', g % 2))
                    elif mb < 32:
                        g = mb - 28
                        OP('act', 'activation', out=ub[:, g, :], in_=bank[:, 0:N], func=AF.Gelu_apprx_tanh,
                           r=[bn], w=[('ub', g)])
                    else:
                        g = mb - 32
                        OP('act', 'activation', out=gvb[:, g, :], in_=bank[:, 0:N], func=AF.Gelu_apprx_tanh,
                           r=[bn], w=[('gvb', g)])
                        OP('act', 'activation', out=sqb, in_=gvb[:, g, :], func=AF.Square, r=[('gvb', g)], w=['sqb'])
                        OP('pe', 'matmul', PROT[:, 0:N], ones_f, gvb[:, g, :], start=(g == 0), stop=(g == 3),
                           r=[('gvb', g), 'ones_f'], w=['prot'])
                        OP('pe', 'matmul', PST2[:, 0:N], ones_f, sqb, start=(g == 0), stop=(g == 3),
                           r=['sqb', 'ones_f'], w=['pst2'])
                        if g == 3:
                            OP('dve', 'tensor_scalar', mn, PROT[:, 0:N], 1.0 / 512, None, ALU.mult, r=['prot'], w=['mn'])
                            OP('dve', 'tensor_tensor', msq, mn, mn, ALU.mult, r=['mn'], w=['msq'])
                            OP('dve', 'scalar_tensor_tensor', lrs, PST2[:, 0:N], 1.0 / 512, msq, ALU.mult,
                               ALU.subtract, r=['pst2', 'msq'], w=['lrs'])
                            OP('act', 'activation', out=lrs, in_=lrs, func=AF.Sqrt, bias=eps_ap, scale=1.0,
                               r=['lrs', 'cst'], w=['lrs'])
                            OP('dve', 'reciprocal', lrs, lrs, r=['lrs'], w=['lrs'])
                            for g2 in range(4):
                                OP('dve', 'tensor_tensor', t1, gvb[:, g2, :], mn, ALU.subtract,
                                   r=[('gvb', g2), 'mn'], w=['t1'])
                                OP('dve', 'tensor_tensor', t1, t1, lrs, ALU.mult, r=['t1', 'lrs'], w=['t1'])
                                OP('act', 'activation', out=vh, in_=t1, func=AF.Identity, bias=lnb[:, l, g2:g2 + 1],
                                   scale=lng[:, l, g2:g2 + 1], r=['t1', 'lng', 'lnb'], w=['vh'])
                                for j in range(nch):
                                    OP('pe', 'transpose', PTR[:, j * 128:(j + 1) * 128], vh[:, j * 128:(j + 1) * 128],
                                       ident_b, r=['vh', 'ident_b'], w=['ptr'])
                                OP('act', 'activation', out=vT[:, 0:nch, :],
                                   in_=PTR[:, 0:nch * 128].rearrange("p (a b) -> p a b", a=nch), func=AF.Copy,
                                   r=['ptr'], w=['vT'])
                                for j in range(nch):
                                    OP('pe', 'matmul', PROT[:, j * 128:(j + 1) * 128], vT[:, j, :], wsT[:, g2, :],
                                       start=True, stop=True, r=['vT', 'wsT'], w=['prot'])
                                OP('dve', 'tensor_tensor', t2[:, 0:N].rearrange("p (a b) -> p a b", a=nch),
                                   PROT[:, 0:N].rearrange("p (a b) -> p a b", a=nch),
                                   gmb[:, g2, :].unsqueeze(1).broadcast_to([128, nch, 128]), ALU.add,
                                   r=['prot', 'gmb'], w=['t2'])
                                OP('dve', 'tensor_tensor', mixb[g2 % 2], t2, ub[:, g2, :], ALU.mult,
                                   r=['t2', ('ub', g2)], w=[('mixb', g2 % 2)])
                                store(MD[16 + g2, :, s0:s0 + N], mixb[g2 % 2], ('mixb', g2 % 2))
        P.barrier()
        A.off = m

    def attention(stream):
        is_ctx = (stream == 1)
        nq_tot = TC if is_ctx else T
        nk = TC if is_ctx else TK
        NQ = min(512, nq_tot)
        QD = QSc if is_ctx else QS
        MD = MIXc if is_ctx else MIX
        nkc = nk // 128
        m = A.off
        kT = A.alloc([2, nk], BF16)
        vv = A.alloc([nkc, 2, 128], BF16)
        qT = [A.alloc([8, NQ], BF16) for _ in range(2)]
        pT = [A.alloc([NQ], BF16) for _ in range(3)]
        rd = A.alloc([NQ], F32)
        ob = [A.alloc([NQ], BF16) for _ in range(2)]
        P.dma('sp', kT, KS[:, :, 0:nk].rearrange("h p t -> p h t"), writes=['kT'], key='kT')
        for hv_ in range(2):
            for c0_ in range(0, nkc, 8):
                c1_ = min(c0_ + 8, nkc)
                P.dma('sp', vv[:, c0_:c1_, hv_, :],
                      VS[hv_, c0_ * 128:c1_ * 128, :].rearrange("(c p) d -> p c d", p=128),
                      writes=[('vv', hv_)], key=f'vv{hv_}')
        PS_S = [PA[0], PA[1]]
        PS_O = [PST, PST2]
        PS_D = [PROT, PSM]
        hi = 0
        for qt in range(nq_tot // NQ):
            q0 = qt * NQ
            qs = qt % 2
            P.dma('sp', qT[qs], QD[:, :, q0:q0 + NQ].rearrange("h p t -> p h t"), writes=[('qT', qs)], key=f"qT{qs}")
            for hh in range(8):
                kvh = hh // 4
                po = PS_O[hi % 2]
                pd = PS_D[hi % 2]
                pon = ('pso', hi % 2)
                pdn = ('psd', hi % 2)

                def S(kc):
                    OP('pe', 'matmul', PS_S[kc % 2][:, 0:NQ], kT[:, kvh, kc * 128:(kc + 1) * 128], qT[qs][:, hh, :],
                       start=True, stop=True, r=['kT', ('qT', qs)], w=[('pss', kc % 2)])
                    OP('act', 'activation', out=pT[kc % 3], in_=PS_S[kc % 2][:, 0:NQ], func=AF.Exp,
                       r=[('pss', kc % 2)], w=[('pT', kc % 3)])

                def PV(kc):
                    OP('pe', 'matmul', po[:, 0:NQ], vv[:, kc, kvh, :], pT[kc % 3], start=(kc == 0),
                       stop=(kc == nkc - 1), r=[('vv', kvh), ('pT', kc % 3)], w=[pon])
                    OP('pe', 'matmul', pd[:, 0:NQ], ones_b, pT[kc % 3], start=(kc == 0), stop=(kc == nkc - 1),
                       r=['ones_b', ('pT', kc % 3)], w=[pdn])

                S(0)
                for kc in range(nkc):
                    if kc + 1 < nkc:
                        S(kc + 1)
                    PV(kc)
                OP('dve', 'reciprocal', rd, pd[:, 0:NQ], r=[pdn], w=['rd'])
                OP('dve', 'tensor_tensor', ob[hi % 2], po[:, 0:NQ], rd, ALU.mult, r=[pon, 'rd'], w=[('ob', hi % 2)])
                store(MD[hh, :, q0:q0 + NQ], ob[hi % 2], ('ob', hi % 2))
                hi += 1
        P.barrier()
        A.off = m

    def fourier(stream):
        is_ctx = (stream == 1)
        n = TC if is_ctx else T
        FD = FSc if is_ctx else FS
        MD = MIXc if is_ctx else MIX
        DN = dftnc_in if is_ctx else dftn_in
        ntc = n // 128
        NK = 256
        m = A.off
        AB = A.alloc([ntc, 4, 256], BF16)
        zT = [A.alloc([n], BF16) for _ in range(2)]
        cn = [A.alloc([ntc, NK], BF16) for _ in range(2)]
        sn = [A.alloc([ntc, NK], BF16) for _ in range(2)]
        yb = [A.alloc([NK], BF16) for _ in range(2)]
        ei = 0
        for g in range(4):
            P.dma('sp', zT[g % 2], FD[g], writes=[('zT', g % 2)], key=f"zT{g % 2}")
            for tcp in range(ntc // 2):
                bi = ei % 3
                for u in range(2):
                    tc_ = tcp * 2 + u
                    OP('pe', 'matmul', PA[bi][:, u * 256:(u + 1) * 256], zT[g % 2][:, tc_ * 128:(tc_ + 1) * 128],
                       dftc_b, start=True, stop=True, r=[('zT', g % 2), 'dftc_b'], w=[('pa', bi)])
                src = PA[bi][:, :].rearrange("p (a b) -> p a b", a=2)
                dst = AB[:, tcp * 2:tcp * 2 + 2, g, :]
                if ei % 2 == 0:
                    OP('act', 'activation', out=dst, in_=src, func=AF.Copy, r=[('pa', bi)], w=['AB'])
                else:
                    OP('dve', 'tensor_copy', out=dst, in_=src, r=[('pa', bi)], w=['AB'])
                ei += 1
        yi = 0
        for kt in range(n // NK):
            s = kt % 2
            for c0_ in range(0, ntc, 8):
                c1_ = min(c0_ + 8, ntc)
                P.dma('pool', cn[s][:, c0_:c1_, :],
                      DN[0, c0_ * 128:c1_ * 128, kt * NK:(kt + 1) * NK].rearrange("(c p) k -> p c k", p=128),
                      writes=[('cn', s, c0_ // 8)], key=f"cn{s}_{c0_ // 8}")
                P.dma('pool', sn[s][:, c0_:c1_, :],
                      DN[1, c0_ * 128:c1_ * 128, kt * NK:(kt + 1) * NK].rearrange("(c p) k -> p c k", p=128),
                      writes=[('sn', s, c0_ // 8)], key=f"sn{s}_{c0_ // 8}")
            for g in range(4):
                bi = yi % 3
                for tc_ in range(ntc):
                    OP('pe', 'matmul', PA[bi][:, 0:NK], AB[:, tc_, g, 0:128], cn[s][:, tc_, :], start=(tc_ == 0),
                       stop=False, r=['AB', ('cn', s, tc_ // 8)], w=[('pa', bi)])
                    OP('pe', 'matmul', PA[bi][:, 0:NK], AB[:, tc_, g, 128:256], sn[s][:, tc_, :], start=False,
                       stop=(tc_ == ntc - 1), r=['AB', ('sn', s, tc_ // 8)], w=[('pa', bi)])
                if yi % 2 == 0:
                    OP('act', 'activation', out=yb[yi % 2], in_=PA[bi][:, 0:NK], func=AF.Copy,
                       r=[('pa', bi)], w=[('yb', yi % 2)])
                else:
                    OP('dve', 'tensor_copy', out=yb[yi % 2], in_=PA[bi][:, 0:NK], r=[('pa', bi)], w=[('yb', yi % 2)])
                store(MD[8 + g, :, kt * NK:(kt + 1) * NK], yb[yi % 2], ('yb', yi % 2))
                yi += 1
        P.barrier()
        A.off = m

    def phase3(l, stream):
        is_ctx = (stream == 1)
        ntot = TC if is_ctx else T
        N = min(NT, ntot)
        XD = XC if is_ctx else XT
        MD = MIXc if is_ctx else MIX
        m = A.off
        xt = A.alloc([KD, N], F32)
        mt = A.alloc([20, N], BF16)
        wo = [A.alloc([20, 512], BF16) for _ in range(2)]
        ga = VEC[:, stream, l, 2, :]
        wit = 0
        mmi = 0
        for ti in range(ntot // N):
            s0 = ti * N
            load_xt(xt, XD, ntot, s0, N, False)
            P.dma('sp', mt, MD[:, :, s0:s0 + N].rearrange("k p t -> p k t"), writes=['mt'], key='mt')
            for cb in range(4):
                ws = wit % 2
                wit += 1
                P.dma('pool', wo[ws], Wo[l, cb].rearrange("p (k n) -> p k n", k=20),
                      writes=[('wo', ws)], key=f"wo{ws}")
                for mi in range(4):
                    mb = cb * 4 + mi
                    bi = mmi % 3
                    mmi += 1
                    for k in range(20):
                        OP('pe', 'matmul', PA[bi][:, 0:N], wo[ws][:, k, mi * 128:(mi + 1) * 128], mt[:, k, :],
                           start=(k == 0), stop=(k == 19), r=[('wo', ws), 'mt'], w=[('pa', bi)])
                    OP('dve', 'scalar_tensor_tensor', xt[:, mb, :], PA[bi][:, 0:N], ga[:, mb:mb + 1], xt[:, mb, :],
                       ALU.mult, ALU.add, r=[('pa', bi), 'xt', 'VEC'], w=['xt'])
            P.dma('sp', (XCM if is_ctx else XM)[:, :, s0:s0 + N].rearrange("k p t -> p k t"), xt, reads=['xt'], key='xts')
        P.barrier()
        A.off = m

    def phase4(l, stream):
        is_ctx = (stream == 1)
        ntot = TC if is_ctx else T
        N = min(NT, ntot)
        XD = XC if is_ctx else XT
        W = N + 2
        m = A.off
        xt = A.alloc([KD, W], F32)
        h = A.alloc([KD, W], BF16)
        sq = [A.alloc([W], F32) for _ in range(2)]
        tmp = [A.alloc([W], F32) for _ in range(2)]
        rstd = A.alloc([W], F32)
        act = A.alloc([NFF, N], BF16)
        wg = [A.alloc([KD, 256], BF16) for _ in range(2)]
        wu = [A.alloc([KD, 256], BF16) for _ in range(2)]
        wd = [A.alloc([NFF, 256], BF16) for _ in range(2)]
        gb = sq
        cv = [t_[:, 0:N] for t_ in tmp]
        sg = cv
        gvec = VEC[:, stream, l, 3, :]
        svec = VEC[:, stream, l, 4, :]
        ga = VEC[:, stream, l, 5, :]
        wit = 0
        wdi = 0
        ji = 0
        for ti in range(ntot // N):
            s0 = ti * N
            load_xt(xt, XCM if is_ctx else XM, ntot, s0, N, True)
            norm_mod(xt, h, W, gvec, svec, sq, tmp, rstd, s0 == 0, s0 + N == ntot)
            for jb in range(NFF // 2):
                ws = wit % 2
                wit += 1
                P.dma('pool', wg[ws], Wg[l, jb].rearrange("p (k n) -> p k n", k=KD),
                      writes=[('wg', ws)], key=f"wg{ws}")
                P.dma('pool', wu[ws], Wu[l, jb].rearrange("p (k n) -> p k n", k=KD),
                      writes=[('wu', ws)], key=f"wu{ws}")
                for j2 in range(2):
                    j = jb * 2 + j2
                    s = ji % 2
                    ji += 1
                    pg = PA[0] if s == 0 else PA[1]
                    pgn = ('pa', 0 if s == 0 else 1)
                    pu = PST if s == 0 else PST2
                    pun = 'pst' if s == 0 else 'pst2'
                    hbk, hbn = HB[s]
                    for kc in range(KD):
                        OP('pe', 'matmul', pg[:, 0:N], wg[ws][:, kc, j2 * 128:(j2 + 1) * 128], h[:, kc, 1:1 + N],
                           start=(kc == 0), stop=(kc == KD - 1), r=[('wg', ws), ('h', kc)], w=[pgn])
                        OP('pe', 'matmul', hbk[:, 0:2], wg[ws][:, kc, j2 * 128:(j2 + 1) * 128],
                           h[:, kc, 0:W:W - 1], start=(kc == 0), stop=(kc == KD - 1),
                           r=[('wg', ws), ('h', kc)], w=[hbn])
                    for kc in range(KD):
                        OP('pe', 'matmul', pu[:, 0:N], wu[ws][:, kc, j2 * 128:(j2 + 1) * 128], h[:, kc, 1:1 + N],
                           start=(kc == 0), stop=(kc == KD - 1), r=[('wu', ws), ('h', kc)], w=[pun])
                    OP('act', 'activation', out=gb[s][:, 1:1 + N], in_=pg[:, 0:N], func=AF.Copy, r=[pgn], w=[('sq', s)])
                    OP('act', 'activation', out=gb[s][:, 0:W:W - 1], in_=hbk[:, 0:2], func=AF.Copy,
                       r=[hbn, ('sq', s)], w=[('sq', s)])
                    OP('dve', 'tensor_scalar', cv[s], gb[s][:, 0:N], fcw[:, l, 0, j:j + 1], None, ALU.mult,
                       r=[('sq', s), 'fcw'], w=[('tmp', s)])
                    OP('dve', 'scalar_tensor_tensor', cv[s], gb[s][:, 1:1 + N], fcw[:, l, 1, j:j + 1], cv[s],
                       ALU.mult, ALU.add, r=[('sq', s), ('tmp', s), 'fcw'], w=[('tmp', s)])
                    OP('dve', 'scalar_tensor_tensor', cv[s], gb[s][:, 2:2 + N], fcw[:, l, 2, j:j + 1], cv[s],
                       ALU.mult, ALU.add, r=[('sq', s), ('tmp', s), 'fcw'], w=[('tmp', s)])
                    OP('act', 'activation', out=sg[s], in_=cv[s], func=AF.Silu, bias=fcb[:, l, j:j + 1], scale=1.0,
                       r=[('tmp', s), 'fcb'], w=[('tmp', s)])
                    OP('dve', 'tensor_tensor', act[:, j, :], sg[s], pu[:, 0:N], ALU.mult,
                       r=[('tmp', s), pun], w=[('act', j)])
            for cb in range(8):
                ws = wdi % 2
                wdi += 1
                P.dma('pool', wd[ws], Wd[l, cb].rearrange("p (k n) -> p k n", k=NFF),
                      writes=[('wd', ws)], key=f"wd{ws}")
                for mi in range(2):
                    mb = cb * 2 + mi
                    bank = PROT if mb % 2 == 0 else PA[2]
                    bn = 'prot' if mb % 2 == 0 else ('pa', 2)
                    for j in range(NFF):
                        OP('pe', 'matmul', bank[:, 0:N], wd[ws][:, j, mi * 128:(mi + 1) * 128], act[:, j, :],
                           start=(j == 0), stop=(j == NFF - 1), r=[('wd', ws), ('act', j)], w=[bn])
                    OP('dve', 'scalar_tensor_tensor', xt[:, mb, 1:1 + N], bank[:, 0:N], ga[:, mb:mb + 1],
                       xt[:, mb, 1:1 + N], ALU.mult, ALU.add, r=[bn, 'xt', 'VEC'], w=['xt'])
            P.dma('sp', XD[:, :, s0:s0 + N].rearrange("k p t -> p k t"), xt[:, :, 1:1 + N], reads=['xt'], key='xts')
        P.barrier()
        A.off = m

    def final_norm():
        N = NT
        m = A.off
        xt = A.alloc([KD, N], F32)
        sq = [A.alloc([N], F32) for _ in range(2)]
        rstd = A.alloc([N], F32)
        y = A.alloc([KD, N], F32)
        ot = [A.alloc([D], F32) for _ in range(2)]
        fins = []
        oi = 0
        for ti in range(T // N):
            s0 = ti * N
            load_xt(xt, XT, T, s0, N, False)
            for kc in range(KD):
                s = kc % 2
                OP('act', 'activation', out=sq[s], in_=xt[:, kc, :], func=AF.Square, r=['xt'], w=[('sq', s)])
                OP('pe', 'matmul', PST[:, 0:N], ones_f, sq[s], start=(kc == 0), stop=(kc == KD - 1),
                   r=[('sq', s), 'ones_f'], w=['pst'])
            OP('act', 'activation', out=rstd, in_=PST[:, 0:N], func=AF.Sqrt, bias=eps_ap, scale=1.0 / D,
               r=['pst', 'cst'], w=['rstd'])
            OP('dve', 'reciprocal', rstd, rstd, r=['rstd'], w=['rstd'])
            for kc in range(KD):
                OP('dve', 'scalar_tensor_tensor', y[:, kc, :], xt[:, kc, :], fng[:, kc:kc + 1], rstd, ALU.mult,
                   ALU.mult, r=['xt', 'rstd', 'fng'], w=[('y', kc)])
            for tb in range(N // 128):
                o = oi % 2
                oi += 1
                for q4 in range(4):
                    bi = q4 % 3
                    for j in range(4):
                        kc = q4 * 4 + j
                        OP('pe', 'transpose', PA[bi][:, j * 128:(j + 1) * 128], y[:, kc, tb * 128:(tb + 1) * 128],
                           ident_f, r=[('y', kc), 'ident_f'], w=[('pa', bi)])
                    if q4 % 2 == 0:
                        OP('act', 'activation', out=ot[o][:, q4 * 512:(q4 + 1) * 512], in_=PA[bi][:, :], func=AF.Copy,
                           r=[('pa', bi)], w=[('ot', o)])
                    else:
                        OP('dve', 'tensor_copy', out=ot[o][:, q4 * 512:(q4 + 1) * 512], in_=PA[bi][:, :],
                           r=[('pa', bi)], w=[('ot', o)])
                fins.append(P.dma('sp', out[s0 + tb * 128:s0 + (tb + 1) * 128, :], ot[o], reads=[('ot', o)],
                                  key=f"out{o}"))
        A.off = m
        return fins

    steps = []
    for l in range(L):
        lastl = (l == L - 1)
        steps.append(('p1c', lambda l=l, lastl=lastl: phase1(l, 1, lastl)))
        steps.append(('p1x', lambda l=l: phase1(l, 0, False)))
        if not lastl:
            steps.append(('atc', lambda: attention(1)))
        if not lastl:
            steps.append(('cast', lambda l=l: cast_layer(l + 1)))
        steps.append(('atx', lambda: attention(0)))
        if not lastl:
            steps.append(('foc', lambda: fourier(1)))
        steps.append(('fox', lambda: fourier(0)))
        if not lastl:
            steps.append(('p3c', lambda l=l: phase3(l, 1)))
        steps.append(('p3x', lambda l=l: phase3(l, 0)))
        if not lastl:
            steps.append(('p4c', lambda l=l: phase4(l, 1)))
        steps.append(('p4x', lambda l=l: phase4(l, 0)))
    nsteps = len(steps) if stop_after is None else stop_after
    for name, fn in steps[:nsteps]:
        fn()
    fins = final_norm()
    P.emit(final_waits=fins)
    return nc, P


def _fm(v, n):
    v = np.asarray(v, np.float32)
    lead = v.shape[:-1]
    v = v.reshape(*lead, n, 128)
    nd = v.ndim
    return np.ascontiguousarray(np.moveaxis(v, -1, 0))


def _constants(T):
    ident = np.eye(128, dtype=np.float32)
    rrot = np.zeros((128, 128), np.float32)
    for base in (0, 64):
        for i in range(32):
            rrot[base + 32 + i, base + i] = -1.0
            rrot[base + i, base + 32 + i] = 1.0
    t = np.arange(T)
    row = (t // 64).astype(np.float64)
    col = (t % 64).astype(np.float64)
    freqs = 10000.0 ** (-np.arange(0, 64, 2, dtype=np.float32).astype(np.float64) / 64)
    ang_r = (row[:, None].astype(np.float32) * freqs[None, :].astype(np.float32)).astype(np.float32)
    ang_c = (col[:, None].astype(np.float32) * freqs[None, :].astype(np.float32)).astype(np.float32)
    cr, sr, cc, sc = np.cos(ang_r), np.sin(ang_r), np.cos(ang_c), np.sin(ang_c)
    ropec = np.concatenate([cr, cr, cc, cc], axis=1).T.astype(np.float32)
    ropes = np.concatenate([sr, sr, sc, sc], axis=1).T.astype(np.float32)
    j = np.arange(128)
    angc = 2 * np.pi * ((j[:, None] * j[None, :]) % 128) / 128
    dftc = (np.concatenate([np.cos(angc), np.sin(angc)], axis=1) / np.sqrt(128.0)).astype(np.float32)

    def dn(n):
        k = np.arange(n, dtype=np.int64)
        ang = 2 * np.pi * ((k[:, None] * k[None, :]) % n).astype(np.float64) / n
        o = np.empty((2, n, n), np.float32)
        o[0] = np.cos(ang) / np.sqrt(n)
        o[1] = -np.sin(ang) / np.sqrt(n)
        return o
    return dict(ident=ident, rrot=rrot, ropec=np.ascontiguousarray(ropec), ropes=np.ascontiguousarray(ropes),
                dftc=dftc, dftn=dn(T), dftnc=dn(TC))


def make_in_maps(inp, T, L, ncores):
    f = lambda k: np.asarray(inp[k], np.float32)
    shared = dict(
        w_mod=np.ascontiguousarray(f('w_mod')[:L]),
        b_mod_t=np.ascontiguousarray(f('b_mod')[:L].reshape(L, 96, 128).transpose(2, 0, 1)),
        n1g=np.ascontiguousarray(f('norm1_g')[:L].reshape(L, KD, 128).transpose(2, 0, 1)),
        n2g=np.ascontiguousarray(f('norm2_g')[:L].reshape(L, KD, 128).transpose(2, 0, 1)),
        fng=np.ascontiguousarray(f('final_norm_g').reshape(KD, 128).T),
        w_in=np.ascontiguousarray(f('w_in')[:L]),
        qg=np.ascontiguousarray(f('q_norm_g')[:L].T),
        kg=np.ascontiguousarray(f('k_norm_g')[:L].T),
        convw=np.ascontiguousarray(f('conv_w')[:L].reshape(L, 3, 4, 128).transpose(3, 0, 1, 2)),
        lng=np.ascontiguousarray(f('gm_ln_g')[:L].reshape(L, 4, 128).transpose(2, 0, 1)),
        lnb=np.ascontiguousarray(f('gm_ln_b')[:L].reshape(L, 4, 128).transpose(2, 0, 1)),
        wsT=np.ascontiguousarray(f('gm_ws')[:L].transpose(3, 0, 1, 2)),
        gmb=np.ascontiguousarray(np.broadcast_to(f('gm_b')[:L][None], (128, L, 4, 128))),
        w_out=np.ascontiguousarray(f('w_out')[:L]),
        w_up=np.ascontiguousarray(f('w_up')[:L]),
        w_down=np.ascontiguousarray(f('w_down')[:L]),
        fcw=np.ascontiguousarray(f('ffn_conv_w')[:L].reshape(L, 3, NFF, 128).transpose(3, 0, 1, 2)),
        fcb=np.ascontiguousarray(f('ffn_conv_b')[:L].reshape(L, NFF, 128).transpose(2, 0, 1)),
    )
    shared.update(_constants(T))
    x = f('x')
    ctx = f('ctx')
    c = f('c')
    cc = f('c_ctx')
    maps = []
    for b in range(ncores):
        mp = dict(shared)
        mp['x'] = np.ascontiguousarray(x[b, :T])
        mp['ctx'] = np.ascontiguousarray(ctx[b])
        cv = np.stack([c[b].reshape(KD, 128).T, cc.reshape(KD, 128).T], axis=1)
        mp['cvec'] = np.ascontiguousarray(cv)
        maps.append(mp)
    return maps


def kernel(**inputs):
    T = 4096
    L = 4
    nc, _ = build_program(T, L)
    maps = make_in_maps(inputs, T, L, NCORES)
    res = run_bass_kernel_spmd(nc, maps, core_ids=list(range(NCORES)))
    return np.stack([np.asarray(res.results[b]["out"], np.float32) for b in range(NCORES)], axis=0)
```

```python
from contextlib import ExitStack
import math
import numpy as np
import concourse.bass as bass
import concourse.mybir as mybir
from concourse.bass_utils import run_bass_kernel_spmd

F32 = mybir.dt.float32
BF16 = mybir.dt.bfloat16
AF = mybir.ActivationFunctionType
ALU = mybir.AluOpType

EPOCH = 30000
DMA_EPOCH = 1800
ENGS = ('sp', 'act', 'pe', 'dve', 'pool')

D = 2048
KD = 16
INW = 4608
MIXW = 2560
DFF = 5632
NFF = 44
EPS = 1e-6
TC = 256
NCORES = 4


class Op:
    __slots__ = ('eng', 'fn', 'deps', 'need_inc', 'pos', 'key', 'dn', 'is_dma')

    def __init__(self, eng, fn, is_dma=False, key=None):
        self.eng = eng
        self.fn = fn
        self.deps = ()
        self.need_inc = False
        self.pos = -1
        self.key = key
        self.dn = -1
        self.is_dma = is_dma


class Prog:
    def __init__(self, nc):
        self.nc = nc
        self.es = ExitStack()
        self.streams = {e: [] for e in ENGS}
        self.res = {}
        self.dma_count = {}
        self.dma_since = {}
        self.last_compute = {}
        self.n_ops = 0

    def sbuf(self, name, shape, dt):
        return self.es.enter_context(self.nc.sbuf_tensor(name, list(shape), dt))

    def psum(self, name, shape, dt):
        return self.es.enter_context(self.nc.psum_tensor(name, list(shape), dt))

    def _track(self, o, reads, writes):
        deps = {}
        res = self.res
        for r in reads:
            st = res.get(r)
            if st is not None and st[0] is not None:
                deps[id(st[0])] = st[0]
        for w in writes:
            st = res.get(w)
            if st is not None:
                if st[0] is not None:
                    deps[id(st[0])] = st[0]
                for d in st[1].values():
                    deps[id(d)] = d
                for d in st[2]:
                    deps[id(d)] = d
        for r in reads:
            st = res.get(r)
            if st is None:
                st = res[r] = [None, {}, []]
            if o.is_dma:
                st[2].append(o)
            else:
                st[1][o.eng] = o
        for w in writes:
            res[w] = [o, {}, []]
        dl = []
        for d in deps.values():
            if d is o:
                continue
            if (not d.is_dma) and d.eng == 'pe' and o.eng == 'pe' and not o.is_dma:
                continue
            d.need_inc = True
            dl.append(d)
        o.deps = dl

    def add(self, eng, fn, reads=(), writes=()):
        o = Op(eng, fn)
        self._track(o, reads, writes)
        self.streams[eng].append(o)
        self.last_compute[eng] = o
        self.n_ops += 1
        return o

    def dma(self, eng, out, in_, reads=(), writes=(), key=None, fn=None):
        assert key is not None
        if fn is None:
            fn = lambda e: e.dma_start(out=out, in_=in_)
        o = Op(eng, fn, is_dma=True, key=key)
        n = self.dma_count.get(key, 0)
        o.dn = n
        self.dma_count[key] = n + 1
        self._track(o, reads, writes)
        self.streams[eng].append(o)
        self.dma_since[key] = o
        self.n_ops += 1
        return o

    def barrier(self):
        deps = list(self.last_compute.values()) + list(self.dma_since.values())
        for d in deps:
            d.need_inc = True
        for e in ENGS:
            o = Op(e, None)
            o.deps = [d for d in deps if d.is_dma or d.eng != e or e != 'pe']
            self.streams[e].append(o)
        self.dma_since = {}
        self.res = {}

    def emit(self, final_waits=()):
        nc = self.nc
        es = self.es
        npos = {}
        for e in ENGS:
            p = 0
            for o in self.streams[e]:
                if (not o.is_dma) and o.need_inc:
                    o.pos = p
                    p += 1
            npos[e] = p
        eng_sems = {}
        nsem = 0
        for e in ENGS:
            k = max(1, (npos[e] + EPOCH - 1) // EPOCH)
            eng_sems[e] = [es.enter_context(nc.semaphore(f"se_{e}_{i}")) for i in range(k)]
            nsem += k
        dma_sems = {}
        for key, cnt in self.dma_count.items():
            k = max(1, (cnt + DMA_EPOCH - 1) // DMA_EPOCH)
            dma_sems[key] = [es.enter_context(nc.semaphore(f"sd_{len(dma_sems)}_{i}")) for i in range(k)]
            nsem += k
        self.nsem = nsem
        block = es.enter_context(nc.Block())
        streams = self.streams
        final_waits = list(final_waits)

        def make_body(e):
            def body(eng):
                known = {x: -1 for x in ENGS}
                known_d = {}

                def wait_for(d):
                    if d.is_dma:
                        ep = d.dn // DMA_EPOCH
                        kk = (d.key, ep)
                        v = (d.dn % DMA_EPOCH + 1) * 16
                        if known_d.get(kk, 0) >= v:
                            return
                        known_d[kk] = v
                        eng.wait_ge(dma_sems[d.key][ep], v)
                    else:
                        if known[d.eng] >= d.pos:
                            return
                        known[d.eng] = d.pos
                        ep = d.pos // EPOCH
                        eng.wait_ge(eng_sems[d.eng][ep], d.pos % EPOCH + 1)

                for o in streams[e]:
                    for d in o.deps:
                        wait_for(d)
                    if o.fn is None:
                        continue
                    ins = o.fn(eng)
                    if o.is_dma:
                        ep = o.dn // DMA_EPOCH
                        ins.then_inc(dma_sems[o.key][ep], 16)
                    elif o.need_inc:
                        ep = o.pos // EPOCH
                        ins.then_inc(eng_sems[e][ep], 1)
                if e == 'sp':
                    for d in final_waits:
                        wait_for(d)
            return body

        block.sync(make_body('sp'))
        block.scalar(make_body('act'))
        block.tensor(make_body('pe'))
        block.vector(make_body('dve'))
        block.gpsimd(make_body('pool'))
        es.close()


class Arena:
    def __init__(self, P, nbytes):
        self.t = P.sbuf("arena", [128, nbytes // 4], F32)
        self.off = 0
        self.size = nbytes

    def alloc(self, shape, dt):
        n = 1
        for s in shape:
            n *= s
        nb = n * (4 if dt == F32 else 2)
        nb = (nb + 63) // 64 * 64
        o = self.off
        self.off += nb
        assert self.off <= self.size, ("arena overflow", self.off, self.size)
        v = self.t[:, o // 4:(o + nb) // 4]
        if dt == BF16:
            v = v.bitcast(BF16)
        v = v[:, 0:n]
        if len(shape) == 2:
            v = v.rearrange("p (a b) -> p a b", a=shape[0])
        elif len(shape) == 3:
            v = v.rearrange("p (a b c) -> p a b c", a=shape[0], b=shape[1])
        elif len(shape) == 4:
            v = v.rearrange("p (a b c d) -> p a b c d", a=shape[0], b=shape[1], c=shape[2])
        return v


def build_program(T, L, debug=False, stop_after=None):
    nc = bass.Bass("TRN2", target_bir_lowering=False)
    TK = TC + T
    NT = 512

    def din(name, shape, dt=F32):
        return nc.dram_tensor(name, list(shape), dt, kind="ExternalInput").ap()

    def dscr(name, shape, dt):
        return nc.dram_tensor(name, list(shape), dt, kind="ExternalOutput" if debug else "Internal").ap()

    x_in = din("x", [T, D])
    ctx_in = din("ctx", [TC, D])
    cvec_in = din("cvec", [128, 2, KD])
    w_mod = din("w_mod", [L, D, 6 * D])
    b_mod_t = din("b_mod_t", [128, L, 96])
    n1g_in = din("n1g", [128, L, KD])
    n2g_in = din("n2g", [128, L, KD])
    fng_in = din("fng", [128, KD])
    w_in = din("w_in", [L, D, INW])
    qg_in = din("qg", [128, L])
    kg_in = din("kg", [128, L])
    convw_in = din("convw", [128, L, 3, 4])
    lng_in = din("lng", [128, L, 4])
    lnb_in = din("lnb", [128, L, 4])
    wsT_in = din("wsT", [128, L, 4, 128])
    gmb_in = din("gmb", [128, L, 4, 128])
    w_out = din("w_out", [L, MIXW, D])
    w_up = din("w_up", [L, D, 2 * DFF])
    fcw_in = din("fcw", [128, L, 3, NFF])
    fcb_in = din("fcb", [128, L, NFF])
    w_down = din("w_down", [L, DFF, D])
    ident_in = din("ident", [128, 128])
    rrot_in = din("rrot", [128, 128])
    ropec_in = din("ropec", [128, T])
    ropes_in = din("ropes", [128, T])
    dftc_in = din("dftc", [128, 256])
    dftn_in = din("dftn", [2, T, T])
    dftnc_in = din("dftnc", [2, TC, TC])
    out = nc.dram_tensor("out", [T, D], F32, kind="ExternalOutput").ap()

    XT = dscr("XT", [KD, 128, T], F32)
    XC = dscr("XC", [KD, 128, TC], F32)
    XM = dscr("XM", [KD, 128, T], F32)
    XCM = dscr("XCM", [KD, 128, TC], F32)
    QS = dscr("QS", [8, 128, T], BF16)
    QSc = dscr("QSc", [8, 128, TC], BF16)
    KS = dscr("KS", [2, 128, TK], BF16)
    VS = dscr("VS", [2, TK, 128], BF16)
    FS = dscr("FS", [4, 128, T], BF16)
    FSc = dscr("FSc", [4, 128, TC], BF16)
    MIX = dscr("MIX", [20, 128, T], BF16)
    MIXc = dscr("MIXc", [20, 128, TC], BF16)

    Wi = nc.dram_tensor("Wi", [L, 9, 128, KD * 512], BF16, kind="Internal").ap()
    Wo = nc.dram_tensor("Wo", [L, 4, 128, 20 * 512], BF16, kind="Internal").ap()
    Wg = nc.dram_tensor("Wg", [L, 22, 128, KD * 256], BF16, kind="Internal").ap()
    Wu = nc.dram_tensor("Wu", [L, 22, 128, KD * 256], BF16, kind="Internal").ap()
    Wd = nc.dram_tensor("Wd", [L, 8, 128, NFF * 256], BF16, kind="Internal").ap()

    NKF = 256
    DN16 = nc.dram_tensor("DN16", [2, T // NKF, 128, (T // 128) * NKF], BF16, kind="Internal").ap()

    P = Prog(nc)
    A = Arena(P, 200 * 1024)

    def cast_dft():
        ntc_ = T // 128
        for mtx in range(2):
            for kt in range(T // NKF):
                for c0_ in range(0, ntc_, 8):
                    c1_ = min(c0_ + 8, ntc_)
                    P.dma('pool', DN16[mtx, kt].rearrange("p (c k) -> p c k", c=ntc_)[:, c0_:c1_, :],
                          dftn_in[mtx, c0_ * 128:c1_ * 128, kt * NKF:(kt + 1) * NKF].rearrange("(c p) k -> p c k", p=128),
                          key='cw')

    def cast_layer(l):
        def c(dst, src, k):
            P.dma('pool', dst.rearrange("p (k n) -> p k n", k=k), src.rearrange("(k p) n -> p k n", p=128),
                  key='cw')
        for cb in range(9):
            c(Wi[l, cb], w_in[l, :, cb * 512:(cb + 1) * 512], KD)
        for cb in range(4):
            c(Wo[l, cb], w_out[l, :, cb * 512:(cb + 1) * 512], 20)
        for jb in range(22):
            c(Wg[l, jb], w_up[l, :, jb * 256:(jb + 1) * 256], KD)
            c(Wu[l, jb], w_up[l, :, DFF + jb * 256:DFF + (jb + 1) * 256], KD)
        for cb in range(8):
            c(Wd[l, cb], w_down[l, :, cb * 256:(cb + 1) * 256], NFF)

    def OP(eng, meth, *args, r=(), w=(), **kw):
        return P.add(eng, lambda e: getattr(e, meth)(*args, **kw), reads=r, writes=w)

    PP = [P.psum(f"pp{i}", [128, 1024], F32) for i in range(4)]
    BK = [PP[i // 2][:, (i % 2) * 512:(i % 2 + 1) * 512] for i in range(8)]
    PA = [BK[0], BK[1], BK[2]]
    PST = BK[3]
    PST2 = BK[4]
    PROT = BK[5]
    PSM = BK[6]
    PTRF = BK[7]
    PTR = PTRF.bitcast(BF16)
    HB = [(PSM, 'psm'), (PTRF, 'ptr')]

    ident_f = A.alloc([128], F32)
    ident_b = A.alloc([128], BF16)
    ones_f = A.alloc([128], F32)
    ones_b = A.alloc([128], BF16)
    rrot_b = A.alloc([128], BF16)
    dftc_b = A.alloc([256], BF16)
    cst = A.alloc([4], F32)
    MOD = A.alloc([L, 96, 2], F32)
    VEC = A.alloc([2, L, 6, KD], F32)
    n1g = A.alloc([L, KD], F32)
    n2g = A.alloc([L, KD], F32)
    fng = A.alloc([KD], F32)
    qg = A.alloc([L], F32)
    kg = A.alloc([L], F32)
    convw = A.alloc([L, 3, 4], F32)
    lng = A.alloc([L, 4], F32)
    lnb = A.alloc([L, 4], F32)
    fcw = A.alloc([L, 3, NFF], F32)
    fcb = A.alloc([L, NFF], F32)
    wsT = A.alloc([4, 128], BF16)
    gmb = A.alloc([4, 128], F32)
    base_mark = A.off

    def ld(dst, src, name, key='cst0'):
        return P.dma('sp', dst, src, writes=[name], key=key)

    def ldc(dst, src, name, key='cst1'):
        return P.dma('pool', dst, src, writes=[name], key=key)

    cast_layer(0)
    cast_dft()
    ld(ident_f, ident_in, 'ident_f')
    ldc(ident_b, ident_in, 'ident_b')
    ldc(rrot_b, rrot_in, 'rrot_b')
    ldc(dftc_b, dftc_in, 'dftc_b')
    ld(n1g, n1g_in, 'n1g')
    ld(n2g, n2g_in, 'n2g')
    ld(fng, fng_in, 'fng')
    ld(qg, qg_in, 'qg', key='qgl')
    ld(kg, kg_in, 'kg')
    ld(convw, convw_in, 'convw')
    ld(lng, lng_in, 'lng')
    ld(lnb, lnb_in, 'lnb')
    ld(fcw, fcw_in, 'fcw')
    ld(fcb, fcb_in, 'fcb')
    OP('dve', 'memset', ones_f, 1.0, w=['ones_f'])
    OP('dve', 'memset', ones_b, 1.0, w=['ones_b'])
    OP('dve', 'memset', cst, 0.0, w=['cst'])
    OP('dve', 'memset', cst[:, 0:1], EPS, r=['cst'], w=['cst'])
    OP('dve', 'tensor_scalar', qg, qg, 128.0 ** -0.5, None, ALU.mult, r=['qg'], w=['qg'])
    eps_ap = cst[:, 0:1]
    P.barrier()

    m0 = A.off
    scv = A.alloc([2, KD], F32)
    bmt = A.alloc([L, 96], F32)
    wm = [A.alloc([KD, 512], F32) for _ in range(2)]
    ld(scv, cvec_in, 'scv', key='scv')
    ld(bmt, b_mod_t, 'bmt', key='bmt')
    OP('act', 'activation', out=scv, in_=scv, func=AF.Silu, r=['scv'], w=['scv'])
    it = 0
    for l in range(L):
        for cb in range(24):
            s = it % 2
            P.dma('sp', wm[s], w_mod[l, :, cb * 512:(cb + 1) * 512].rearrange("(k p) n -> p k n", p=128),
                  writes=[('wm', s)], key=f"wm{s}")
            for mi in range(4):
                for kc in range(KD):
                    OP('pe', 'matmul', PSM[:, 2 * mi:2 * mi + 2], wm[s][:, kc, mi * 128:(mi + 1) * 128],
                       scv[:, :, kc], start=(kc == 0), stop=(kc == KD - 1),
                       r=[('wm', s), 'scv'], w=['psm'])
            OP('dve', 'tensor_tensor', MOD[:, l, cb * 4:cb * 4 + 4, :],
               PSM[:, 0:8].rearrange("p (a b) -> p a b", a=4),
               bmt[:, l, cb * 4:cb * 4 + 4].unsqueeze(2).broadcast_to([128, 4, 2]), ALU.add,
               r=['psm', 'bmt'], w=['MOD'])
            it += 1
    for st in range(2):
        for l in range(L):
            for (dst, src, gain) in ((0, 16, n1g), (3, 64, n2g)):
                OP('dve', 'tensor_scalar', VEC[:, st, l, dst, :], MOD[:, l, src:src + 16, st], 1.0, None, ALU.add,
                   r=['MOD'], w=['VEC'])
                OP('dve', 'tensor_tensor', VEC[:, st, l, dst, :], VEC[:, st, l, dst, :], gain[:, l, :], ALU.mult,
                   r=['VEC', 'n1g', 'n2g'], w=['VEC'])
            for (dst, src) in ((1, 0), (2, 32), (4, 48), (5, 80)):
                OP('dve', 'tensor_copy', out=VEC[:, st, l, dst, :], in_=MOD[:, l, src:src + 16, st],
                   r=['MOD'], w=['VEC'])
    P.barrier()
    A.off = m0

    def to_feature_major(src, dst, ntok):
        m = A.off
        xin = [A.alloc([D], F32) for _ in range(2)]
        stg = [A.alloc([KD, 128], F32) for _ in range(2)]
        for tb in range(ntok // 128):
            s = tb % 2
            P.dma('sp', xin[s], src[tb * 128:(tb + 1) * 128, :], writes=[('xin', s)], key=f"xin{s}")
            for q4 in range(4):
                bank = PA[q4 % 3]
                bn = ('pa', q4 % 3)
                for j in range(4):
                    kc = q4 * 4 + j
                    OP('pe', 'transpose', bank[:, j * 128:(j + 1) * 128], xin[s][:, kc * 128:(kc + 1) * 128], ident_f,
                       r=[('xin', s), 'ident_f'], w=[bn])
                src4 = bank[:, :].rearrange("p (a b) -> p a b", a=4)
                if q4 % 2 == 0:
                    OP('act', 'activation', out=stg[s][:, q4 * 4:q4 * 4 + 4, :], in_=src4, func=AF.Copy,
                       r=[bn], w=[('stg', s)])
                else:
                    OP('dve', 'tensor_copy', out=stg[s][:, q4 * 4:q4 * 4 + 4, :], in_=src4, r=[bn], w=[('stg', s)])
            P.dma('sp', dst[:, :, tb * 128:(tb + 1) * 128].rearrange("k p t -> p k t"), stg[s],
                  reads=[('stg', s)], key=f"s_stg{s}")
        P.barrier()
        A.off = m

    to_feature_major(x_in, XT, T)
    to_feature_major(ctx_in, XC, TC)

    def load_xt(xt, XD, ntot, s0, N, halo):
        if halo:
            lo = max(s0 - 1, 0)
            hi = min(s0 + N + 1, ntot)
            c0 = lo - (s0 - 1)
            if s0 == 0:
                OP('dve', 'memset', xt[:, :, 0:1], 0.0, w=['xt'])
            if s0 + N == ntot:
                OP('dve', 'memset', xt[:, :, N + 1:N + 2], 0.0, w=['xt'])
            if hi - lo == N + 2:
                P.dma('sp', xt[:, :, 0:N + 1], XD[:, :, lo:hi - 1].rearrange("k p t -> p k t"),
                      writes=['xt'], key='xt')
                P.dma('sp', xt[:, :, N:N + 2], XD[:, :, hi - 2:hi].rearrange("k p t -> p k t"),
                      reads=['xt'], writes=['xt'], key='xt')
            else:
                P.dma('sp', xt[:, :, c0:c0 + (hi - lo)], XD[:, :, lo:hi].rearrange("k p t -> p k t"),
                      writes=['xt'], key='xt')
        else:
            P.dma('sp', xt[:, :, 0:N], XD[:, :, s0:s0 + N].rearrange("k p t -> p k t"), writes=['xt'], key='xt')

    def norm_mod(xt, h, W, gvec, svec, sq, tmp, rstd, first, last):
        W0 = min(W, 512)
        for kc in range(KD):
            s = kc % 2
            OP('act', 'activation', out=sq[s][:, 0:W], in_=xt[:, kc, 0:W], func=AF.Square,
               r=['xt'], w=[('sq', s)])
            OP('pe', 'matmul', PST[:, 0:W0], ones_f, sq[s][:, 0:W0], start=(kc == 0), stop=(kc == KD - 1),
               r=[('sq', s), 'ones_f'], w=['pst'])
            if W > 512:
                OP('pe', 'matmul', PROT[:, 0:W - 512], ones_f, sq[s][:, 512:W], start=(kc == 0),
                   stop=(kc == KD - 1), r=[('sq', s), 'ones_f'], w=['prot'])
        OP('act', 'activation', out=rstd[:, 0:W0], in_=PST[:, 0:W0], func=AF.Sqrt, bias=eps_ap, scale=1.0 / D,
           r=['pst', 'cst'], w=['rstd'])
        if W > 512:
            OP('act', 'activation', out=rstd[:, 512:W], in_=PROT[:, 0:W - 512], func=AF.Sqrt, bias=eps_ap,
               scale=1.0 / D, r=['prot', 'cst'], w=['rstd'])
        OP('dve', 'reciprocal', rstd[:, 0:W], rstd[:, 0:W], r=['rstd'], w=['rstd'])
        for kc in range(KD):
            s = kc % 2
            OP('dve', 'tensor_tensor', tmp[s][:, 0:W], xt[:, kc, 0:W], rstd[:, 0:W], ALU.mult,
               r=['xt', 'rstd'], w=[('tmp', s)])
            OP('act', 'activation', out=h[:, kc, 0:W], in_=tmp[s][:, 0:W], func=AF.Identity,
               bias=svec[:, kc:kc + 1], scale=gvec[:, kc:kc + 1], r=[('tmp', s), 'VEC'], w=[('h', kc)])
        if first:
            OP('dve', 'memset', h[:, :, 0:1], 0.0, r=[('h', k) for k in range(KD)], w=[('h', k) for k in range(KD)])
        if last:
            OP('dve', 'memset', h[:, :, W - 1:W], 0.0, r=[('h', k) for k in range(KD)],
               w=[('h', k) for k in range(KD)])

    def store(dst, src, res):
        key = "s_" + (res if isinstance(res, str) else f"{res[0]}{res[1]}")
        return P.dma('sp', dst, src, reads=[res], key=key)

    def phase1(l, stream, kv_only):
        is_ctx = (stream == 1)
        ntot = TC if is_ctx else T
        N = min(NT, ntot)
        XD = XC if is_ctx else XT
        QD = QSc if is_ctx else QS
        FD = FSc if is_ctx else FS
        MD = MIXc if is_ctx else MIX
        koff = 0 if is_ctx else TC
        W = N + 2
        nch = N // 128
        m = A.off
        xt = A.alloc([KD, W], F32)
        h = A.alloc([KD, W], BF16)
        sq = [A.alloc([W], F32) for _ in range(2)]
        tmp = [A.alloc([W], F32) for _ in range(2)]
        rstd = A.alloc([W], F32)
        wb = [A.alloc([KD, 512], BF16) for _ in range(2)]
        qf = A.alloc([N], F32)
        sqb = A.alloc([N], F32)
        rq = A.alloc([N], F32)
        qnb = A.alloc([N], BF16)
        t1 = A.alloc([N], F32)
        t2 = A.alloc([N], F32)
        qo = [A.alloc([N], BF16) for _ in range(2)]
        vb = A.alloc([N], BF16)
        vt = [A.alloc([4, 128], BF16) for _ in range(2)]
        fb = [A.alloc([N], BF16) for _ in range(2)]
        cbk = A.alloc([4, N], F32)
        ccb = A.alloc([4, W], F32)
        prod = A.alloc([W], F32)
        cv = A.alloc([N], F32)
        mixb = [A.alloc([N], BF16) for _ in range(2)]
        ub = A.alloc([4, N], F32)
        gvb = A.alloc([4, N], F32)
        mn = A.alloc([N], F32)
        msq = A.alloc([N], F32)
        lrs = A.alloc([N], F32)
        vh = A.alloc([N], BF16)
        vT = A.alloc([4, 128], BF16)
        rc = A.alloc([N], F32)
        rs = A.alloc([N], F32)
        gvec = VEC[:, stream, l, 0, :]
        svec = VEC[:, stream, l, 1, :]
        if not kv_only:
            ldc(wsT, wsT_in[:, l], 'wsT', key='wsT')
            ld(gmb, gmb_in[:, l], 'gmb', key='gmb')
        blocks = list(range(8, 12)) if kv_only else list(range(36))
        cbs = sorted(set(b // 4 for b in blocks))
        wit = 0
        for ti in range(ntot // N):
            s0 = ti * N
            load_xt(xt, XD, ntot, s0, N, True)
            if not is_ctx:
                P.dma('sp', rc, ropec_in[:, s0:s0 + N], writes=['rc'], key='rc')
                P.dma('sp', rs, ropes_in[:, s0:s0 + N], writes=['rs'], key='rs')
            norm_mod(xt, h, W, gvec, svec, sq, tmp, rstd, s0 == 0, s0 + N == ntot)
            mmi = 0
            for cb in cbs:
                ws = wit % 2
                wit += 1
                P.dma('pool', wb[ws], Wi[l, cb].rearrange("p (k n) -> p k n", k=KD),
                      writes=[('wb', ws)], key=f"wb{ws}")
                for mi in range(4):
                    mb = cb * 4 + mi
                    if mb not in blocks:
                        continue
                    bi = mmi % 3
                    mmi += 1
                    bank = PA[bi]
                    bn = ('pa', bi)
                    is_halo = 20 <= mb < 28
                    hbk, hbn = HB[mb % 2]
                    for kc in range(KD):
                        OP('pe', 'matmul', bank[:, 0:N], wb[ws][:, kc, mi * 128:(mi + 1) * 128], h[:, kc, 1:1 + N],
                           start=(kc == 0), stop=(kc == KD - 1), r=[('wb', ws), ('h', kc)], w=[bn])
                        if is_halo:
                            OP('pe', 'matmul', hbk[:, 0:2], wb[ws][:, kc, mi * 128:(mi + 1) * 128],
                               h[:, kc, 0:W:W - 1], start=(kc == 0), stop=(kc == KD - 1),
                               r=[('wb', ws), ('h', kc)], w=[hbn])
                    if mb < 10:
                        isq = mb < 8
                        OP('act', 'activation', out=qf, in_=bank[:, 0:N], func=AF.Copy, r=[bn], w=['qf'])
                        OP('act', 'activation', out=sqb, in_=bank[:, 0:N], func=AF.Square, r=[bn], w=['sqb'])
                        OP('pe', 'matmul', PST2[:, 0:N], ones_f, sqb, start=True, stop=True,
                           r=['sqb', 'ones_f'], w=['pst2'])
                        OP('act', 'activation', out=rq, in_=PST2[:, 0:N], func=AF.Sqrt, bias=eps_ap, scale=1.0 / 128,
                           r=['pst2', 'cst'], w=['rq'])
                        OP('dve', 'reciprocal', rq, rq, r=['rq'], w=['rq'])
                        gq = (qg if isq else kg)[:, l:l + 1]
                        qs_ = qo[mb % 2]
                        qn_ = ('qo', mb % 2)
                        if is_ctx:
                            OP('dve', 'scalar_tensor_tensor', qs_, qf, gq, rq, ALU.mult, ALU.mult,
                               r=['qf', 'rq', 'qg', 'kg'], w=[qn_])
                        else:
                            OP('dve', 'scalar_tensor_tensor', qnb, qf, gq, rq, ALU.mult, ALU.mult,
                               r=['qf', 'rq', 'qg', 'kg'], w=['qnb'])
                            OP('pe', 'matmul', PROT[:, 0:N], rrot_b, qnb, start=True, stop=True,
                               r=['qnb', 'rrot_b'], w=['prot'])
                            OP('dve', 'tensor_tensor', t1, qnb, rc, ALU.mult, r=['qnb', 'rc'], w=['t1'])
                            OP('dve', 'tensor_tensor', t2, PROT[:, 0:N], rs, ALU.mult, r=['prot', 'rs'], w=['t2'])
                            OP('dve', 'tensor_tensor', qs_, t1, t2, ALU.add, r=['t1', 't2'], w=[qn_])
                        if isq:
                            store(QD[mb, :, s0:s0 + N], qs_, qn_)
                        else:
                            store(KS[mb - 8, :, koff + s0:koff + s0 + N], qs_, qn_)
                    elif mb < 12:
                        hv = mb - 10
                        OP('act', 'activation', out=vb, in_=bank[:, 0:N], func=AF.Copy, r=[bn], w=['vb'])
                        for j in range(nch):
                            OP('pe', 'transpose', PTR[:, j * 128:(j + 1) * 128], vb[:, j * 128:(j + 1) * 128], ident_b,
                               r=['vb', 'ident_b'], w=['ptr'])
                        OP('dve', 'tensor_copy', out=vt[hv][:, 0:nch, :],
                           in_=PTR[:, 0:nch * 128].rearrange("p (a b) -> p a b", a=nch), r=['ptr'], w=[('vt', hv)])
                        store(VS[hv, koff + s0:koff + s0 + N, :].rearrange("(j p) d -> p j d", p=128),
                              vt[hv][:, 0:nch, :], ('vt', hv))
                    elif mb < 16:
                        g = mb - 12
                        OP('act', 'activation', out=fb[g % 2], in_=bank[:, 0:N], func=AF.Copy, r=[bn], w=[('fb', g % 2)])
                        store(FD[g, :, s0:s0 + N], fb[g % 2], ('fb', g % 2))
                    elif mb < 20:
                        g = mb - 16
                        OP('act', 'activation', out=cbk[:, g, :], in_=bank[:, 0:N], func=AF.Copy, r=[bn], w=[('cbk', g)])
                    elif mb < 24:
                        g = mb - 20
                        OP('act', 'activation', out=ccb[:, g, 1:1 + N], in_=bank[:, 0:N], func=AF.Copy,
                           r=[bn], w=[('ccb', g)])
                        OP('act', 'activation', out=ccb[:, g, 0:W:W - 1], in_=hbk[:, 0:2], func=AF.Copy,
                           r=[hbn, ('ccb', g)], w=[('ccb', g)])
                    elif mb < 28:
                        g = mb - 24
                        OP('dve', 'tensor_tensor', prod[:, 1:1 + N], ccb[:, g, 1:1 + N], bank[:, 0:N], ALU.mult,
                           r=[bn, ('ccb', g)], w=['prod'])
                        OP('dve', 'tensor_tensor', prod[:, 0:W:W - 1], ccb[:, g, 0:W:W - 1], hbk[:, 0:2],
                           ALU.mult, r=[hbn, ('ccb', g), 'prod'], w=['prod'])
                        OP('dve', 'tensor_scalar', cv, prod[:, 0:N], convw[:, l, 0, g:g + 1], None, ALU.mult,
                           r=['prod', 'convw'], w=['cv'])
                        OP('dve', 'scalar_tensor_tensor', cv, prod[:, 1:1 + N], convw[:, l, 1, g:g + 1], cv,
                           ALU.mult, ALU.add, r=['prod', 'cv', 'convw'], w=['cv'])
                        OP('dve', 'scalar_tensor_tensor', cv, prod[:, 2:2 + N], convw[:, l, 2, g:g + 1], cv,
                           ALU.mult, ALU.add, r=['prod', 'cv', 'convw'], w=['cv'])
                        OP('dve', 'tensor_tensor', mixb[g % 2], cv, cbk[:, g, :], ALU.mult,
                           r=['cv', ('cbk', g)], w=[('mixb', g % 2)])
                        store(MD[12 + g, :, s0:s0 + N], mixb[g % 2], ('mixb', g % 2))
                    elif mb < 32:
                        g = mb - 28
                        OP('act', 'activation', out=ub[:, g, :], in_=bank[:, 0:N], func=AF.Gelu_apprx_tanh,
                           r=[bn], w=[('ub', g)])
                    else:
                        g = mb - 32
                        OP('act', 'activation', out=gvb[:, g, :], in_=bank[:, 0:N], func=AF.Gelu_apprx_tanh,
                           r=[bn], w=[('gvb', g)])
                        OP('act', 'activation', out=sqb, in_=gvb[:, g, :], func=AF.Square, r=[('gvb', g)], w=['sqb'])
                        OP('pe', 'matmul', PROT[:, 0:N], ones_f, gvb[:, g, :], start=(g == 0), stop=(g == 3),
                           r=[('gvb', g), 'ones_f'], w=['prot'])
                        OP('pe', 'matmul', PST2[:, 0:N], ones_f, sqb, start=(g == 0), stop=(g == 3),
                           r=['sqb', 'ones_f'], w=['pst2'])
                        if g == 3:
                            OP('dve', 'tensor_scalar', mn, PROT[:, 0:N], 1.0 / 512, None, ALU.mult, r=['prot'], w=['mn'])
                            OP('dve', 'tensor_tensor', msq, mn, mn, ALU.mult, r=['mn'], w=['msq'])
                            OP('dve', 'scalar_tensor_tensor', lrs, PST2[:, 0:N], 1.0 / 512, msq, ALU.mult,
                               ALU.subtract, r=['pst2', 'msq'], w=['lrs'])
                            OP('act', 'activation', out=lrs, in_=lrs, func=AF.Sqrt, bias=eps_ap, scale=1.0,
                               r=['lrs', 'cst'], w=['lrs'])
                            OP('dve', 'reciprocal', lrs, lrs, r=['lrs'], w=['lrs'])
                            for g2 in range(4):
                                OP('dve', 'tensor_tensor', t1, gvb[:, g2, :], mn, ALU.subtract,
                                   r=[('gvb', g2), 'mn'], w=['t1'])
                                OP('dve', 'tensor_tensor', t1, t1, lrs, ALU.mult, r=['t1', 'lrs'], w=['t1'])
                                OP('act', 'activation', out=vh, in_=t1, func=AF.Identity, bias=lnb[:, l, g2:g2 + 1],
                                   scale=lng[:, l, g2:g2 + 1], r=['t1', 'lng', 'lnb'], w=['vh'])
                                for j in range(nch):
                                    OP('pe', 'transpose', PTR[:, j * 128:(j + 1) * 128], vh[:, j * 128:(j + 1) * 128],
                                       ident_b, r=['vh', 'ident_b'], w=['ptr'])
                                OP('act', 'activation', out=vT[:, 0:nch, :],
                                   in_=PTR[:, 0:nch * 128].rearrange("p (a b) -> p a b", a=nch), func=AF.Copy,
                                   r=['ptr'], w=['vT'])
                                for j in range(nch):
                                    OP('pe', 'matmul', PROT[:, j * 128:(j + 1) * 128], vT[:, j, :], wsT[:, g2, :],
                                       start=True, stop=True, r=['vT', 'wsT'], w=['prot'])
                                OP('dve', 'tensor_tensor', t2[:, 0:N].rearrange("p (a b) -> p a b", a=nch),
                                   PROT[:, 0:N].rearrange("p (a b) -> p a b", a=nch),
                                   gmb[:, g2, :].unsqueeze(1).broadcast_to([128, nch, 128]), ALU.add,
                                   r=['prot', 'gmb'], w=['t2'])
                                OP('dve', 'tensor_tensor', mixb[g2 % 2], t2, ub[:, g2, :], ALU.mult,
                                   r=['t2', ('ub', g2)], w=[('mixb', g2 % 2)])
                                store(MD[16 + g2, :, s0:s0 + N], mixb[g2 % 2], ('mixb', g2 % 2))
        P.barrier()
        A.off = m

    def attention(stream):
        is_ctx = (stream == 1)
        nq_tot = TC if is_ctx else T
        nk = TC if is_ctx else TK
        NQ = min(512, nq_tot)
        QD = QSc if is_ctx else QS
        MD = MIXc if is_ctx else MIX
        nkc = nk // 128
        npair = nkc // 2
        m = A.off
        kT = A.alloc([2, nk], BF16)
        vv = A.alloc([nkc, 2, 128], BF16)
        qT = [A.alloc([8, NQ], BF16) for _ in range(2)]
        pT = [A.alloc([2, NQ], BF16) for _ in range(3)]
        rd = A.alloc([NQ], F32)
        ob = [A.alloc([NQ], BF16) for _ in range(2)]
        P.dma('sp', kT, KS[:, :, 0:nk].rearrange("h p t -> p h t"), writes=['kT'], key='kT')
        for hv_ in range(2):
            for c0_ in range(0, nkc, 8):
                c1_ = min(c0_ + 8, nkc)
                P.dma('sp', vv[:, c0_:c1_, hv_, :],
                      VS[hv_, c0_ * 128:c1_ * 128, :].rearrange("(c p) d -> p c d", p=128),
                      writes=[('vv', hv_)], key=f'vv{hv_}')
        PS_S = [PP[0], PP[1]]
        PS_O = [BK[4], BK[5]]
        PS_D = [BK[6], BK[7]]
        hi = 0
        for qt in range(nq_tot // NQ):
            q0 = qt * NQ
            qs = qt % 2
            P.dma('sp', qT[qs], QD[:, :, q0:q0 + NQ].rearrange("h p t -> p h t"), writes=[('qT', qs)], key=f"qT{qs}")
            for hh in range(8):
                kvh = hh // 4
                po = PS_O[hi % 2]
                pd = PS_D[hi % 2]
                pon = ('pso', hi % 2)
                pdn = ('psd', hi % 2)

                def S(p):
                    ps = PS_S[p % 2]
                    for u in range(2):
                        kc = 2 * p + u
                        OP('pe', 'matmul', ps[:, u * 512:u * 512 + NQ], kT[:, kvh, kc * 128:(kc + 1) * 128],
                           qT[qs][:, hh, :], start=True, stop=True, r=['kT', ('qT', qs)], w=[('pss', p % 2)])
                    OP('act', 'activation', out=pT[p % 3],
                       in_=ps[:, :].rearrange("p (a b) -> p a b", a=2)[:, :, 0:NQ], func=AF.Exp,
                       r=[('pss', p % 2)], w=[('pT', p % 3)])

                def PV(p):
                    for u in range(2):
                        kc = 2 * p + u
                        OP('pe', 'matmul', po[:, 0:NQ], vv[:, kc, kvh, :], pT[p % 3][:, u, :], start=(kc == 0),
                           stop=(kc == nkc - 1), r=[('vv', kvh), ('pT', p % 3)], w=[pon])
                        OP('pe', 'matmul', pd[:, 0:NQ], ones_b, pT[p % 3][:, u, :], start=(kc == 0),
                           stop=(kc == nkc - 1), r=['ones_b', ('pT', p % 3)], w=[pdn])

                S(0)
                for p in range(npair):
                    if p + 1 < npair:
                        S(p + 1)
                    PV(p)
                OP('dve', 'reciprocal', rd, pd[:, 0:NQ], r=[pdn], w=['rd'])
                OP('dve', 'tensor_tensor', ob[hi % 2], po[:, 0:NQ], rd, ALU.mult, r=[pon, 'rd'], w=[('ob', hi % 2)])
                store(MD[hh, :, q0:q0 + NQ], ob[hi % 2], ('ob', hi % 2))
                hi += 1
        P.barrier()
        A.off = m

    def fourier(stream):
        is_ctx = (stream == 1)
        n = TC if is_ctx else T
        FD = FSc if is_ctx else FS
        MD = MIXc if is_ctx else MIX
        DN = dftnc_in if is_ctx else dftn_in
        ntc = n // 128
        NK = 256
        m = A.off
        AB = A.alloc([ntc, 4, 256], BF16)
        zT = [A.alloc([n], BF16) for _ in range(2)]
        cn = [A.alloc([ntc, NK], BF16) for _ in range(2)]
        sn = [A.alloc([ntc, NK], BF16) for _ in range(2)]
        yb = [A.alloc([NK], BF16) for _ in range(2)]
        ei = 0
        for g in range(4):
            P.dma('sp', zT[g % 2], FD[g], writes=[('zT', g % 2)], key=f"zT{g % 2}")
            for tcp in range(ntc // 2):
                bi = ei % 3
                for u in range(2):
                    tc_ = tcp * 2 + u
                    OP('pe', 'matmul', PA[bi][:, u * 256:(u + 1) * 256], zT[g % 2][:, tc_ * 128:(tc_ + 1) * 128],
                       dftc_b, start=True, stop=True, r=[('zT', g % 2), 'dftc_b'], w=[('pa', bi)])
                src = PA[bi][:, :].rearrange("p (a b) -> p a b", a=2)
                dst = AB[:, tcp * 2:tcp * 2 + 2, g, :]
                if ei % 2 == 0:
                    OP('act', 'activation', out=dst, in_=src, func=AF.Copy, r=[('pa', bi)], w=['AB'])
                else:
                    OP('dve', 'tensor_copy', out=dst, in_=src, r=[('pa', bi)], w=['AB'])
                ei += 1
        yi = 0
        for kt in range(n // NK):
            s = kt % 2
            if is_ctx:
                P.dma('pool', cn[s], DN[0, :, kt * NK:(kt + 1) * NK].rearrange("(c p) k -> p c k", p=128),
                      writes=[('cn', s, 0)], key=f"cn{s}_0")
                P.dma('pool', sn[s], DN[1, :, kt * NK:(kt + 1) * NK].rearrange("(c p) k -> p c k", p=128),
                      writes=[('sn', s, 0)], key=f"sn{s}_0")
            else:
                P.dma('pool', cn[s], DN16[0, kt].rearrange("p (c k) -> p c k", c=ntc), writes=[('cn', s, 0)],
                      key=f"cn{s}_0")
                P.dma('pool', sn[s], DN16[1, kt].rearrange("p (c k) -> p c k", c=ntc), writes=[('sn', s, 0)],
                      key=f"sn{s}_0")
            for g in range(4):
                bi = yi % 3
                for tc_ in range(ntc):
                    OP('pe', 'matmul', PA[bi][:, 0:NK], AB[:, tc_, g, 0:128], cn[s][:, tc_, :], start=(tc_ == 0),
                       stop=False, r=['AB', ('cn', s, 0)], w=[('pa', bi)])
                    OP('pe', 'matmul', PA[bi][:, 0:NK], AB[:, tc_, g, 128:256], sn[s][:, tc_, :], start=False,
                       stop=(tc_ == ntc - 1), r=['AB', ('sn', s, 0)], w=[('pa', bi)])
                if yi % 2 == 0:
                    OP('act', 'activation', out=yb[yi % 2], in_=PA[bi][:, 0:NK], func=AF.Copy,
                       r=[('pa', bi)], w=[('yb', yi % 2)])
                else:
                    OP('dve', 'tensor_copy', out=yb[yi % 2], in_=PA[bi][:, 0:NK], r=[('pa', bi)], w=[('yb', yi % 2)])
                store(MD[8 + g, :, kt * NK:(kt + 1) * NK], yb[yi % 2], ('yb', yi % 2))
                yi += 1
        P.barrier()
        A.off = m

    def phase3(l, stream):
        is_ctx = (stream == 1)
        ntot = TC if is_ctx else T
        N = min(NT, ntot)
        XD = XC if is_ctx else XT
        MD = MIXc if is_ctx else MIX
        m = A.off
        xt = A.alloc([KD, N], F32)
        mt = A.alloc([20, N], BF16)
        wo = [A.alloc([20, 512], BF16) for _ in range(2)]
        ga = VEC[:, stream, l, 2, :]
        wit = 0
        mmi = 0
        for ti in range(ntot // N):
            s0 = ti * N
            load_xt(xt, XD, ntot, s0, N, False)
            P.dma('sp', mt, MD[:, :, s0:s0 + N].rearrange("k p t -> p k t"), writes=['mt'], key='mt')
            for cb in range(4):
                ws = wit % 2
                wit += 1
                P.dma('pool', wo[ws], Wo[l, cb].rearrange("p (k n) -> p k n", k=20),
                      writes=[('wo', ws)], key=f"wo{ws}")
                for mi in range(4):
                    mb = cb * 4 + mi
                    bi = mmi % 3
                    mmi += 1
                    for k in range(20):
                        OP('pe', 'matmul', PA[bi][:, 0:N], wo[ws][:, k, mi * 128:(mi + 1) * 128], mt[:, k, :],
                           start=(k == 0), stop=(k == 19), r=[('wo', ws), 'mt'], w=[('pa', bi)])
                    OP('dve', 'scalar_tensor_tensor', xt[:, mb, :], PA[bi][:, 0:N], ga[:, mb:mb + 1], xt[:, mb, :],
                       ALU.mult, ALU.add, r=[('pa', bi), 'xt', 'VEC'], w=['xt'])
            P.dma('sp', (XCM if is_ctx else XM)[:, :, s0:s0 + N].rearrange("k p t -> p k t"), xt, reads=['xt'], key='xts')
        P.barrier()
        A.off = m

    def phase4(l, stream):
        is_ctx = (stream == 1)
        ntot = TC if is_ctx else T
        N = min(NT, ntot)
        XD = XC if is_ctx else XT
        W = N + 2
        m = A.off
        xt = A.alloc([KD, W], F32)
        h = A.alloc([KD, W], BF16)
        sq = [A.alloc([W], F32) for _ in range(2)]
        tmp = [A.alloc([W], F32) for _ in range(2)]
        rstd = A.alloc([W], F32)
        act = A.alloc([NFF, N], BF16)
        wg = [A.alloc([KD, 256], BF16) for _ in range(2)]
        wu = [A.alloc([KD, 256], BF16) for _ in range(2)]
        wd = [A.alloc([NFF, 256], BF16) for _ in range(2)]
        gb = sq
        cv = [t_[:, 0:N] for t_ in tmp]
        sg = cv
        gvec = VEC[:, stream, l, 3, :]
        svec = VEC[:, stream, l, 4, :]
        ga = VEC[:, stream, l, 5, :]
        wit = 0
        wdi = 0
        ji = 0
        for ti in range(ntot // N):
            s0 = ti * N
            load_xt(xt, XCM if is_ctx else XM, ntot, s0, N, True)
            norm_mod(xt, h, W, gvec, svec, sq, tmp, rstd, s0 == 0, s0 + N == ntot)
            for jb in range(NFF // 2):
                ws = wit % 2
                wit += 1
                P.dma('pool', wg[ws], Wg[l, jb].rearrange("p (k n) -> p k n", k=KD),
                      writes=[('wg', ws)], key=f"wg{ws}")
                P.dma('pool', wu[ws], Wu[l, jb].rearrange("p (k n) -> p k n", k=KD),
                      writes=[('wu', ws)], key=f"wu{ws}")
                for j2 in range(2):
                    j = jb * 2 + j2
                    s = ji % 2
                    ji += 1
                    pg = PA[0] if s == 0 else PA[1]
                    pgn = ('pa', 0 if s == 0 else 1)
                    pu = PST if s == 0 else PST2
                    pun = 'pst' if s == 0 else 'pst2'
                    hbk, hbn = HB[s]
                    for kc in range(KD):
                        OP('pe', 'matmul', pg[:, 0:N], wg[ws][:, kc, j2 * 128:(j2 + 1) * 128], h[:, kc, 1:1 + N],
                           start=(kc == 0), stop=(kc == KD - 1), r=[('wg', ws), ('h', kc)], w=[pgn])
                        OP('pe', 'matmul', hbk[:, 0:2], wg[ws][:, kc, j2 * 128:(j2 + 1) * 128],
                           h[:, kc, 0:W:W - 1], start=(kc == 0), stop=(kc == KD - 1),
                           r=[('wg', ws), ('h', kc)], w=[hbn])
                    for kc in range(KD):
                        OP('pe', 'matmul', pu[:, 0:N], wu[ws][:, kc, j2 * 128:(j2 + 1) * 128], h[:, kc, 1:1 + N],
                           start=(kc == 0), stop=(kc == KD - 1), r=[('wu', ws), ('h', kc)], w=[pun])
                    OP('act', 'activation', out=gb[s][:, 1:1 + N], in_=pg[:, 0:N], func=AF.Copy, r=[pgn], w=[('sq', s)])
                    OP('act', 'activation', out=gb[s][:, 0:W:W - 1], in_=hbk[:, 0:2], func=AF.Copy,
                       r=[hbn, ('sq', s)], w=[('sq', s)])
                    OP('dve', 'tensor_scalar', cv[s], gb[s][:, 0:N], fcw[:, l, 0, j:j + 1], None, ALU.mult,
                       r=[('sq', s), 'fcw'], w=[('tmp', s)])
                    OP('dve', 'scalar_tensor_tensor', cv[s], gb[s][:, 1:1 + N], fcw[:, l, 1, j:j + 1], cv[s],
                       ALU.mult, ALU.add, r=[('sq', s), ('tmp', s), 'fcw'], w=[('tmp', s)])
                    OP('dve', 'scalar_tensor_tensor', cv[s], gb[s][:, 2:2 + N], fcw[:, l, 2, j:j + 1], cv[s],
                       ALU.mult, ALU.add, r=[('sq', s), ('tmp', s), 'fcw'], w=[('tmp', s)])
                    OP('act', 'activation', out=sg[s], in_=cv[s], func=AF.Silu, bias=fcb[:, l, j:j + 1], scale=1.0,
                       r=[('tmp', s), 'fcb'], w=[('tmp', s)])
                    OP('dve', 'tensor_tensor', act[:, j, :], sg[s], pu[:, 0:N], ALU.mult,
                       r=[('tmp', s), pun], w=[('act', j)])
            for cb in range(8):
                ws = wdi % 2
                wdi += 1
                P.dma('pool', wd[ws], Wd[l, cb].rearrange("p (k n) -> p k n", k=NFF),
                      writes=[('wd', ws)], key=f"wd{ws}")
                for mi in range(2):
                    mb = cb * 2 + mi
                    bank = PROT if mb % 2 == 0 else PA[2]
                    bn = 'prot' if mb % 2 == 0 else ('pa', 2)
                    for j in range(NFF):
                        OP('pe', 'matmul', bank[:, 0:N], wd[ws][:, j, mi * 128:(mi + 1) * 128], act[:, j, :],
                           start=(j == 0), stop=(j == NFF - 1), r=[('wd', ws), ('act', j)], w=[bn])
                    OP('dve', 'scalar_tensor_tensor', xt[:, mb, 1:1 + N], bank[:, 0:N], ga[:, mb:mb + 1],
                       xt[:, mb, 1:1 + N], ALU.mult, ALU.add, r=[bn, 'xt', 'VEC'], w=['xt'])
            P.dma('sp', XD[:, :, s0:s0 + N].rearrange("k p t -> p k t"), xt[:, :, 1:1 + N], reads=['xt'], key='xts')
        P.barrier()
        A.off = m

    def final_norm():
        N = NT
        m = A.off
        xt = A.alloc([KD, N], F32)
        sq = [A.alloc([N], F32) for _ in range(2)]
        rstd = A.alloc([N], F32)
        y = A.alloc([KD, N], F32)
        ot = [A.alloc([D], F32) for _ in range(2)]
        fins = []
        oi = 0
        for ti in range(T // N):
            s0 = ti * N
            load_xt(xt, XT, T, s0, N, False)
            for kc in range(KD):
                s = kc % 2
                OP('act', 'activation', out=sq[s], in_=xt[:, kc, :], func=AF.Square, r=['xt'], w=[('sq', s)])
                OP('pe', 'matmul', PST[:, 0:N], ones_f, sq[s], start=(kc == 0), stop=(kc == KD - 1),
                   r=[('sq', s), 'ones_f'], w=['pst'])
            OP('act', 'activation', out=rstd, in_=PST[:, 0:N], func=AF.Sqrt, bias=eps_ap, scale=1.0 / D,
               r=['pst', 'cst'], w=['rstd'])
            OP('dve', 'reciprocal', rstd, rstd, r=['rstd'], w=['rstd'])
            for kc in range(KD):
                OP('dve', 'scalar_tensor_tensor', y[:, kc, :], xt[:, kc, :], fng[:, kc:kc + 1], rstd, ALU.mult,
                   ALU.mult, r=['xt', 'rstd', 'fng'], w=[('y', kc)])
            for tb in range(N // 128):
                o = oi % 2
                oi += 1
                for q4 in range(4):
                    bi = q4 % 3
                    for j in range(4):
                        kc = q4 * 4 + j
                        OP('pe', 'transpose', PA[bi][:, j * 128:(j + 1) * 128], y[:, kc, tb * 128:(tb + 1) * 128],
                           ident_f, r=[('y', kc), 'ident_f'], w=[('pa', bi)])
                    if q4 % 2 == 0:
                        OP('act', 'activation', out=ot[o][:, q4 * 512:(q4 + 1) * 512], in_=PA[bi][:, :], func=AF.Copy,
                           r=[('pa', bi)], w=[('ot', o)])
                    else:
                        OP('dve', 'tensor_copy', out=ot[o][:, q4 * 512:(q4 + 1) * 512], in_=PA[bi][:, :],
                           r=[('pa', bi)], w=[('ot', o)])
                fins.append(P.dma('sp', out[s0 + tb * 128:s0 + (tb + 1) * 128, :], ot[o], reads=[('ot', o)],
                                  key=f"out{o}"))
        A.off = m
        return fins

    steps = []
    for l in range(L):
        lastl = (l == L - 1)
        steps.append(('p1c', lambda l=l, lastl=lastl: phase1(l, 1, lastl)))
        steps.append(('p1x', lambda l=l: phase1(l, 0, False)))
        if not lastl:
            steps.append(('atc', lambda: attention(1)))
        if not lastl:
            steps.append(('cast', lambda l=l: cast_layer(l + 1)))
        steps.append(('atx', lambda: attention(0)))
        if not lastl:
            steps.append(('foc', lambda: fourier(1)))
        steps.append(('fox', lambda: fourier(0)))
        if not lastl:
            steps.append(('p3c', lambda l=l: phase3(l, 1)))
        steps.append(('p3x', lambda l=l: phase3(l, 0)))
        if not lastl:
            steps.append(('p4c', lambda l=l: phase4(l, 1)))
        steps.append(('p4x', lambda l=l: phase4(l, 0)))
    nsteps = len(steps) if stop_after is None else stop_after
    P.marks = [('prologue', 0)]
    for name, fn in steps[:nsteps]:
        P.marks.append((name, sum(1 for o in P.streams['pe'] if o.fn is not None)))
        fn()
    P.marks.append(('final', sum(1 for o in P.streams['pe'] if o.fn is not None)))
    fins = final_norm()
    P.emit(final_waits=fins)
    return nc, P


def _fm(v, n):
    v = np.asarray(v, np.float32)
    lead = v.shape[:-1]
    v = v.reshape(*lead, n, 128)
    nd = v.ndim
    return np.ascontiguousarray(np.moveaxis(v, -1, 0))


def _constants(T):
    ident = np.eye(128, dtype=np.float32)
    rrot = np.zeros((128, 128), np.float32)
    for base in (0, 64):
        for i in range(32):
            rrot[base + 32 + i, base + i] = -1.0
            rrot[base + i, base + 32 + i] = 1.0
    t = np.arange(T)
    row = (t // 64).astype(np.float64)
    col = (t % 64).astype(np.float64)
    freqs = 10000.0 ** (-np.arange(0, 64, 2, dtype=np.float32).astype(np.float64) / 64)
    ang_r = (row[:, None].astype(np.float32) * freqs[None, :].astype(np.float32)).astype(np.float32)
    ang_c = (col[:, None].astype(np.float32) * freqs[None, :].astype(np.float32)).astype(np.float32)
    cr, sr, cc, sc = np.cos(ang_r), np.sin(ang_r), np.cos(ang_c), np.sin(ang_c)
    ropec = np.concatenate([cr, cr, cc, cc], axis=1).T.astype(np.float32)
    ropes = np.concatenate([sr, sr, sc, sc], axis=1).T.astype(np.float32)
    j = np.arange(128)
    angc = 2 * np.pi * ((j[:, None] * j[None, :]) % 128) / 128
    dftc = (np.concatenate([np.cos(angc), np.sin(angc)], axis=1) / np.sqrt(128.0)).astype(np.float32)

    def dn(n):
        k = np.arange(n, dtype=np.int64)
        ang = 2 * np.pi * ((k[:, None] * k[None, :]) % n).astype(np.float64) / n
        o = np.empty((2, n, n), np.float32)
        o[0] = np.cos(ang) / np.sqrt(n)
        o[1] = -np.sin(ang) / np.sqrt(n)
        return o
    return dict(ident=ident, rrot=rrot, ropec=np.ascontiguousarray(ropec), ropes=np.ascontiguousarray(ropes),
                dftc=dftc, dftn=dn(T), dftnc=dn(TC))


def make_in_maps(inp, T, L, ncores):
    f = lambda k: np.asarray(inp[k], np.float32)
    shared = dict(
        w_mod=np.ascontiguousarray(f('w_mod')[:L]),
        b_mod_t=np.ascontiguousarray(f('b_mod')[:L].reshape(L, 96, 128).transpose(2, 0, 1)),
        n1g=np.ascontiguousarray(f('norm1_g')[:L].reshape(L, KD, 128).transpose(2, 0, 1)),
        n2g=np.ascontiguousarray(f('norm2_g')[:L].reshape(L, KD, 128).transpose(2, 0, 1)),
        fng=np.ascontiguousarray(f('final_norm_g').reshape(KD, 128).T),
        w_in=np.ascontiguousarray(f('w_in')[:L]),
        qg=np.ascontiguousarray(f('q_norm_g')[:L].T),
        kg=np.ascontiguousarray(f('k_norm_g')[:L].T),
        convw=np.ascontiguousarray(f('conv_w')[:L].reshape(L, 3, 4, 128).transpose(3, 0, 1, 2)),
        lng=np.ascontiguousarray(f('gm_ln_g')[:L].reshape(L, 4, 128).transpose(2, 0, 1)),
        lnb=np.ascontiguousarray(f('gm_ln_b')[:L].reshape(L, 4, 128).transpose(2, 0, 1)),
        wsT=np.ascontiguousarray(f('gm_ws')[:L].transpose(3, 0, 1, 2)),
        gmb=np.ascontiguousarray(np.broadcast_to(f('gm_b')[:L][None], (128, L, 4, 128))),
        w_out=np.ascontiguousarray(f('w_out')[:L]),
        w_up=np.ascontiguousarray(f('w_up')[:L]),
        w_down=np.ascontiguousarray(f('w_down')[:L]),
        fcw=np.ascontiguousarray(f('ffn_conv_w')[:L].reshape(L, 3, NFF, 128).transpose(3, 0, 1, 2)),
        fcb=np.ascontiguousarray(f('ffn_conv_b')[:L].reshape(L, NFF, 128).transpose(2, 0, 1)),
    )
    shared.update(_constants(T))
    x = f('x')
    ctx = f('ctx')
    c = f('c')
    cc = f('c_ctx')
    maps = []
    for b in range(ncores):
        mp = dict(shared)
        mp['x'] = np.ascontiguousarray(x[b, :T])
        mp['ctx'] = np.ascontiguousarray(ctx[b])
        cv = np.stack([c[b].reshape(KD, 128).T, cc.reshape(KD, 128).T], axis=1)
        mp['cvec'] = np.ascontiguousarray(cv)
        maps.append(mp)
    return maps


def kernel(**inputs):
    T = 4096
    L = 4
    nc, _ = build_program(T, L)
    maps = make_in_maps(inputs, T, L, NCORES)
    res = run_bass_kernel_spmd(nc, maps, core_ids=list(range(NCORES)))
    return np.stack([np.asarray(res.results[b]["out"], np.float32) for b in range(NCORES)], axis=0)
```

```python
from contextlib import ExitStack
import math
import numpy as np
import concourse.bass as bass
import concourse.mybir as mybir
from concourse.bass_utils import run_bass_kernel_spmd

F32 = mybir.dt.float32
BF16 = mybir.dt.bfloat16
AF = mybir.ActivationFunctionType
ALU = mybir.AluOpType

EPOCH = 30000
DMA_EPOCH = 1800
ENGS = ('sp', 'act', 'pe', 'dve', 'pool')

D = 2048
KD = 16
INW = 4608
MIXW = 2560
DFF = 5632
NFF = 44
EPS = 1e-6
TC = 256
NCORES = 4


class Op:
    __slots__ = ('eng', 'fn', 'deps', 'need_inc', 'pos', 'key', 'dn', 'is_dma')

    def __init__(self, eng, fn, is_dma=False, key=None):
        self.eng = eng
        self.fn = fn
        self.deps = ()
        self.need_inc = False
        self.pos = -1
        self.key = key
        self.dn = -1
        self.is_dma = is_dma


class Prog:
    def __init__(self, nc):
        self.nc = nc
        self.es = ExitStack()
        self.streams = {e: [] for e in ENGS}
        self.res = {}
        self.dma_count = {}
        self.dma_since = {}
        self.last_compute = {}
        self.n_ops = 0

    def sbuf(self, name, shape, dt):
        return self.es.enter_context(self.nc.sbuf_tensor(name, list(shape), dt))

    def psum(self, name, shape, dt):
        return self.es.enter_context(self.nc.psum_tensor(name, list(shape), dt))

    def _track(self, o, reads, writes):
        deps = {}
        res = self.res
        for r in reads:
            st = res.get(r)
            if st is not None and st[0] is not None:
                deps[id(st[0])] = st[0]
        for w in writes:
            st = res.get(w)
            if st is not None:
                if st[0] is not None:
                    deps[id(st[0])] = st[0]
                for d in st[1].values():
                    deps[id(d)] = d
                for d in st[2]:
                    deps[id(d)] = d
        for r in reads:
            st = res.get(r)
            if st is None:
                st = res[r] = [None, {}, []]
            if o.is_dma:
                st[2].append(o)
            else:
                st[1][o.eng] = o
        for w in writes:
            res[w] = [o, {}, []]
        dl = []
        for d in deps.values():
            if d is o:
                continue
            if (not d.is_dma) and d.eng == 'pe' and o.eng == 'pe' and not o.is_dma:
                continue
            d.need_inc = True
            dl.append(d)
        o.deps = dl

    def add(self, eng, fn, reads=(), writes=()):
        o = Op(eng, fn)
        self._track(o, reads, writes)
        self.streams[eng].append(o)
        self.last_compute[eng] = o
        self.n_ops += 1
        return o

    def dma(self, eng, out, in_, reads=(), writes=(), key=None, fn=None):
        assert key is not None
        if fn is None:
            fn = lambda e: e.dma_start(out=out, in_=in_)
        o = Op(eng, fn, is_dma=True, key=key)
        n = self.dma_count.get(key, 0)
        o.dn = n
        self.dma_count[key] = n + 1
        self._track(o, reads, writes)
        self.streams[eng].append(o)
        self.dma_since[key] = o
        self.n_ops += 1
        return o

    def barrier(self):
        deps = list(self.last_compute.values()) + list(self.dma_since.values())
        for d in deps:
            d.need_inc = True
        for e in ENGS:
            o = Op(e, None)
            o.deps = [d for d in deps if d.is_dma or d.eng != e or e != 'pe']
            self.streams[e].append(o)
        self.dma_since = {}
        self.res = {}

    def emit(self, final_waits=()):
        nc = self.nc
        es = self.es
        npos = {}
        for e in ENGS:
            p = 0
            for o in self.streams[e]:
                if (not o.is_dma) and o.need_inc:
                    o.pos = p
                    p += 1
            npos[e] = p
        eng_sems = {}
        nsem = 0
        for e in ENGS:
            k = max(1, (npos[e] + EPOCH - 1) // EPOCH)
            eng_sems[e] = [es.enter_context(nc.semaphore(f"se_{e}_{i}")) for i in range(k)]
            nsem += k
        dma_sems = {}
        for key, cnt in self.dma_count.items():
            k = max(1, (cnt + DMA_EPOCH - 1) // DMA_EPOCH)
            dma_sems[key] = [es.enter_context(nc.semaphore(f"sd_{len(dma_sems)}_{i}")) for i in range(k)]
            nsem += k
        self.nsem = nsem
        block = es.enter_context(nc.Block())
        streams = self.streams
        final_waits = list(final_waits)

        def make_body(e):
            def body(eng):
                known = {x: -1 for x in ENGS}
                known_d = {}

                def wait_for(d):
                    if d.is_dma:
                        ep = d.dn // DMA_EPOCH
                        kk = (d.key, ep)
                        v = (d.dn % DMA_EPOCH + 1) * 16
                        if known_d.get(kk, 0) >= v:
                            return
                        known_d[kk] = v
                        eng.wait_ge(dma_sems[d.key][ep], v)
                    else:
                        if known[d.eng] >= d.pos:
                            return
                        known[d.eng] = d.pos
                        ep = d.pos // EPOCH
                        eng.wait_ge(eng_sems[d.eng][ep], d.pos % EPOCH + 1)

                for o in streams[e]:
                    for d in o.deps:
                        wait_for(d)
                    if o.fn is None:
                        continue
                    ins = o.fn(eng)
                    if o.is_dma:
                        ep = o.dn // DMA_EPOCH
                        ins.then_inc(dma_sems[o.key][ep], 16)
                    elif o.need_inc:
                        ep = o.pos // EPOCH
                        ins.then_inc(eng_sems[e][ep], 1)
                if e == 'sp':
                    for d in final_waits:
                        wait_for(d)
            return body

        block.sync(make_body('sp'))
        block.scalar(make_body('act'))
        block.tensor(make_body('pe'))
        block.vector(make_body('dve'))
        block.gpsimd(make_body('pool'))
        es.close()


class Arena:
    def __init__(self, P, nbytes):
        self.t = P.sbuf("arena", [128, nbytes // 4], F32)
        self.off = 0
        self.size = nbytes

    def alloc(self, shape, dt):
        n = 1
        for s in shape:
            n *= s
        nb = n * (4 if dt == F32 else 2)
        nb = (nb + 63) // 64 * 64
        o = self.off
        self.off += nb
        assert self.off <= self.size, ("arena overflow", self.off, self.size)
        v = self.t[:, o // 4:(o + nb) // 4]
        if dt == BF16:
            v = v.bitcast(BF16)
        v = v[:, 0:n]
        if len(shape) == 2:
            v = v.rearrange("p (a b) -> p a b", a=shape[0])
        elif len(shape) == 3:
            v = v.rearrange("p (a b c) -> p a b c", a=shape[0], b=shape[1])
        elif len(shape) == 4:
            v = v.rearrange("p (a b c d) -> p a b c d", a=shape[0], b=shape[1], c=shape[2])
        return v


def build_program(T, L, debug=False, stop_after=None):
    nc = bass.Bass("TRN2", target_bir_lowering=False)
    TK = TC + T
    NT = 512

    def din(name, shape, dt=F32):
        return nc.dram_tensor(name, list(shape), dt, kind="ExternalInput").ap()

    def dscr(name, shape, dt):
        return nc.dram_tensor(name, list(shape), dt, kind="ExternalOutput" if debug else "Internal").ap()

    x_in = din("x", [T, D])
    ctx_in = din("ctx", [TC, D])
    cvec_in = din("cvec", [128, 2, KD])
    w_mod = din("w_mod", [L, D, 6 * D])
    b_mod_t = din("b_mod_t", [128, L, 96])
    n1g_in = din("n1g", [128, L, KD])
    n2g_in = din("n2g", [128, L, KD])
    fng_in = din("fng", [128, KD])
    w_in = din("w_in", [L, D, INW])
    qg_in = din("qg", [128, L])
    kg_in = din("kg", [128, L])
    convw_in = din("convw", [128, L, 3, 4])
    lng_in = din("lng", [128, L, 4])
    lnb_in = din("lnb", [128, L, 4])
    wsT_in = din("wsT", [128, L, 4, 128])
    gmb_in = din("gmb", [128, L, 4, 128])
    w_out = din("w_out", [L, MIXW, D])
    w_up = din("w_up", [L, D, 2 * DFF])
    fcw_in = din("fcw", [128, L, 3, NFF])
    fcb_in = din("fcb", [128, L, NFF])
    w_down = din("w_down", [L, DFF, D])
    ident_in = din("ident", [128, 128])
    rrot_in = din("rrot", [128, 128])
    ropec_in = din("ropec", [128, T])
    ropes_in = din("ropes", [128, T])
    dftc_in = din("dftc", [128, 256])
    dftn_in = din("dftn", [2, T, T])
    dftnc_in = din("dftnc", [2, TC, TC])
    out = nc.dram_tensor("out", [T, D], F32, kind="ExternalOutput").ap()

    XT = dscr("XT", [KD, 128, T], F32)
    XC = dscr("XC", [KD, 128, TC], F32)
    XM = dscr("XM", [KD, 128, T], F32)
    XCM = dscr("XCM", [KD, 128, TC], F32)
    QS = dscr("QS", [8, 128, T], BF16)
    QSc = dscr("QSc", [8, 128, TC], BF16)
    KS = dscr("KS", [2, 128, TK], BF16)
    VS = dscr("VS", [2, TK, 128], BF16)
    FS = dscr("FS", [4, 128, T], BF16)
    FSc = dscr("FSc", [4, 128, TC], BF16)
    MIX = dscr("MIX", [20, 128, T], BF16)
    MIXc = dscr("MIXc", [20, 128, TC], BF16)

    Wi = nc.dram_tensor("Wi", [L, 9, 128, KD * 512], BF16, kind="Internal").ap()
    Wo = nc.dram_tensor("Wo", [L, 4, 128, 20 * 512], BF16, kind="Internal").ap()
    Wg = nc.dram_tensor("Wg", [L, 22, 128, KD * 256], BF16, kind="Internal").ap()
    Wu = nc.dram_tensor("Wu", [L, 22, 128, KD * 256], BF16, kind="Internal").ap()
    Wd = nc.dram_tensor("Wd", [L, 8, 128, NFF * 256], BF16, kind="Internal").ap()

    NKF = 512
    DN16 = nc.dram_tensor("DN16", [2, T // NKF, 128, (T // 128) * NKF], BF16, kind="Internal").ap()

    P = Prog(nc)
    A = Arena(P, 200 * 1024)

    def cast_dft():
        ntc_ = T // 128
        for mtx in range(2):
            for kt in range(T // NKF):
                for c0_ in range(0, ntc_, 8):
                    c1_ = min(c0_ + 8, ntc_)
                    P.dma('pool', DN16[mtx, kt].rearrange("p (c k) -> p c k", c=ntc_)[:, c0_:c1_, :],
                          dftn_in[mtx, c0_ * 128:c1_ * 128, kt * NKF:(kt + 1) * NKF].rearrange("(c p) k -> p c k", p=128),
                          key='cw')

    def cast_layer(l):
        def c(dst, src, k):
            P.dma('pool', dst.rearrange("p (k n) -> p k n", k=k), src.rearrange("(k p) n -> p k n", p=128),
                  key='cw')
        for cb in range(9):
            c(Wi[l, cb], w_in[l, :, cb * 512:(cb + 1) * 512], KD)
        for cb in range(4):
            c(Wo[l, cb], w_out[l, :, cb * 512:(cb + 1) * 512], 20)
        for jb in range(22):
            c(Wg[l, jb], w_up[l, :, jb * 256:(jb + 1) * 256], KD)
            c(Wu[l, jb], w_up[l, :, DFF + jb * 256:DFF + (jb + 1) * 256], KD)
        for cb in range(8):
            c(Wd[l, cb], w_down[l, :, cb * 256:(cb + 1) * 256], NFF)

    def OP(eng, meth, *args, r=(), w=(), **kw):
        return P.add(eng, lambda e: getattr(e, meth)(*args, **kw), reads=r, writes=w)

    PP = [P.psum(f"pp{i}", [128, 1024], F32) for i in range(4)]
    BK = [PP[i // 2][:, (i % 2) * 512:(i % 2 + 1) * 512] for i in range(8)]
    PA = [BK[0], BK[1], BK[2]]
    PST = BK[3]
    PST2 = BK[4]
    PROT = BK[5]
    PSM = BK[6]
    PTRF = BK[7]
    PTR = PTRF.bitcast(BF16)
    HB = [(PSM, 'psm'), (PTRF, 'ptr')]

    ident_f = A.alloc([128], F32)
    ident_b = A.alloc([128], BF16)
    ones_f = A.alloc([128], F32)
    ones_b = A.alloc([128], BF16)
    rrot_b = A.alloc([128], BF16)
    dftc_b = A.alloc([256], BF16)
    cst = A.alloc([4], F32)
    MOD = A.alloc([L, 96, 2], F32)
    VEC = A.alloc([2, L, 6, KD], F32)
    n1g = A.alloc([L, KD], F32)
    n2g = A.alloc([L, KD], F32)
    fng = A.alloc([KD], F32)
    qg = A.alloc([L], F32)
    kg = A.alloc([L], F32)
    convw = A.alloc([L, 3, 4], F32)
    lng = A.alloc([L, 4], F32)
    lnb = A.alloc([L, 4], F32)
    fcw = A.alloc([L, 3, NFF], F32)
    fcb = A.alloc([L, NFF], F32)
    wsT = A.alloc([4, 128], BF16)
    gmb = A.alloc([4, 128], F32)
    base_mark = A.off

    def ld(dst, src, name, key='cst0'):
        return P.dma('sp', dst, src, writes=[name], key=key)

    def ldc(dst, src, name, key='cst1'):
        return P.dma('pool', dst, src, writes=[name], key=key)

    cast_layer(0)
    cast_dft()
    ld(ident_f, ident_in, 'ident_f')
    ldc(ident_b, ident_in, 'ident_b')
    ldc(rrot_b, rrot_in, 'rrot_b')
    ldc(dftc_b, dftc_in, 'dftc_b')
    ld(n1g, n1g_in, 'n1g')
    ld(n2g, n2g_in, 'n2g')
    ld(fng, fng_in, 'fng')
    ld(qg, qg_in, 'qg', key='qgl')
    ld(kg, kg_in, 'kg')
    ld(convw, convw_in, 'convw')
    ld(lng, lng_in, 'lng')
    ld(lnb, lnb_in, 'lnb')
    ld(fcw, fcw_in, 'fcw')
    ld(fcb, fcb_in, 'fcb')
    OP('dve', 'memset', ones_f, 1.0, w=['ones_f'])
    OP('dve', 'memset', ones_b, 1.0, w=['ones_b'])
    OP('dve', 'memset', cst, 0.0, w=['cst'])
    OP('dve', 'memset', cst[:, 0:1], EPS, r=['cst'], w=['cst'])
    OP('dve', 'tensor_scalar', qg, qg, 128.0 ** -0.5, None, ALU.mult, r=['qg'], w=['qg'])
    eps_ap = cst[:, 0:1]
    P.barrier()

    m0 = A.off
    scv = A.alloc([2, KD], F32)
    bmt = A.alloc([L, 96], F32)
    wm = [A.alloc([KD, 512], F32) for _ in range(2)]
    ld(scv, cvec_in, 'scv', key='scv')
    ld(bmt, b_mod_t, 'bmt', key='bmt')
    OP('act', 'activation', out=scv, in_=scv, func=AF.Silu, r=['scv'], w=['scv'])
    it = 0
    for l in range(L):
        for cb in range(24):
            s = it % 2
            P.dma('sp', wm[s], w_mod[l, :, cb * 512:(cb + 1) * 512].rearrange("(k p) n -> p k n", p=128),
                  writes=[('wm', s)], key=f"wm{s}")
            for mi in range(4):
                for kc in range(KD):
                    OP('pe', 'matmul', PSM[:, 2 * mi:2 * mi + 2], wm[s][:, kc, mi * 128:(mi + 1) * 128],
                       scv[:, :, kc], start=(kc == 0), stop=(kc == KD - 1),
                       r=[('wm', s), 'scv'], w=['psm'])
            OP('dve', 'tensor_tensor', MOD[:, l, cb * 4:cb * 4 + 4, :],
               PSM[:, 0:8].rearrange("p (a b) -> p a b", a=4),
               bmt[:, l, cb * 4:cb * 4 + 4].unsqueeze(2).broadcast_to([128, 4, 2]), ALU.add,
               r=['psm', 'bmt'], w=['MOD'])
            it += 1
    for st in range(2):
        for l in range(L):
            for (dst, src, gain) in ((0, 16, n1g), (3, 64, n2g)):
                OP('dve', 'tensor_scalar', VEC[:, st, l, dst, :], MOD[:, l, src:src + 16, st], 1.0, None, ALU.add,
                   r=['MOD'], w=['VEC'])
                OP('dve', 'tensor_tensor', VEC[:, st, l, dst, :], VEC[:, st, l, dst, :], gain[:, l, :], ALU.mult,
                   r=['VEC', 'n1g', 'n2g'], w=['VEC'])
            for (dst, src) in ((1, 0), (2, 32), (4, 48), (5, 80)):
                OP('dve', 'tensor_copy', out=VEC[:, st, l, dst, :], in_=MOD[:, l, src:src + 16, st],
                   r=['MOD'], w=['VEC'])
    P.barrier()
    A.off = m0

    def to_feature_major(src, dst, ntok):
        m = A.off
        xin = [A.alloc([D], F32) for _ in range(2)]
        stg = [A.alloc([KD, 128], F32) for _ in range(2)]
        for tb in range(ntok // 128):
            s = tb % 2
            P.dma('sp', xin[s], src[tb * 128:(tb + 1) * 128, :], writes=[('xin', s)], key=f"xin{s}")
            for q4 in range(4):
                bank = PA[q4 % 3]
                bn = ('pa', q4 % 3)
                for j in range(4):
                    kc = q4 * 4 + j
                    OP('pe', 'transpose', bank[:, j * 128:(j + 1) * 128], xin[s][:, kc * 128:(kc + 1) * 128], ident_f,
                       r=[('xin', s), 'ident_f'], w=[bn])
                src4 = bank[:, :].rearrange("p (a b) -> p a b", a=4)
                if q4 % 2 == 0:
                    OP('act', 'activation', out=stg[s][:, q4 * 4:q4 * 4 + 4, :], in_=src4, func=AF.Copy,
                       r=[bn], w=[('stg', s)])
                else:
                    OP('dve', 'tensor_copy', out=stg[s][:, q4 * 4:q4 * 4 + 4, :], in_=src4, r=[bn], w=[('stg', s)])
            P.dma('sp', dst[:, :, tb * 128:(tb + 1) * 128].rearrange("k p t -> p k t"), stg[s],
                  reads=[('stg', s)], key=f"s_stg{s}")
        P.barrier()
        A.off = m

    to_feature_major(x_in, XT, T)
    to_feature_major(ctx_in, XC, TC)

    def load_xt(xt, XD, ntot, s0, N, halo):
        if halo:
            lo = max(s0 - 1, 0)
            hi = min(s0 + N + 1, ntot)
            c0 = lo - (s0 - 1)
            if s0 == 0:
                OP('dve', 'memset', xt[:, :, 0:1], 0.0, w=['xt'])
            if s0 + N == ntot:
                OP('dve', 'memset', xt[:, :, N + 1:N + 2], 0.0, w=['xt'])
            if hi - lo == N + 2:
                P.dma('sp', xt[:, :, 0:N + 1], XD[:, :, lo:hi - 1].rearrange("k p t -> p k t"),
                      writes=['xt'], key='xt')
                P.dma('sp', xt[:, :, N:N + 2], XD[:, :, hi - 2:hi].rearrange("k p t -> p k t"),
                      reads=['xt'], writes=['xt'], key='xt')
            else:
                P.dma('sp', xt[:, :, c0:c0 + (hi - lo)], XD[:, :, lo:hi].rearrange("k p t -> p k t"),
                      writes=['xt'], key='xt')
        else:
            P.dma('sp', xt[:, :, 0:N], XD[:, :, s0:s0 + N].rearrange("k p t -> p k t"), writes=['xt'], key='xt')

    def norm_mod(xt, h, W, gvec, svec, sq, tmp, rstd, first, last):
        W0 = min(W, 512)
        for kc in range(KD):
            s = kc % 2
            OP('act', 'activation', out=sq[s][:, 0:W], in_=xt[:, kc, 0:W], func=AF.Square,
               r=['xt'], w=[('sq', s)])
            OP('pe', 'matmul', PST[:, 0:W0], ones_f, sq[s][:, 0:W0], start=(kc == 0), stop=(kc == KD - 1),
               r=[('sq', s), 'ones_f'], w=['pst'])
            if W > 512:
                OP('pe', 'matmul', PROT[:, 0:W - 512], ones_f, sq[s][:, 512:W], start=(kc == 0),
                   stop=(kc == KD - 1), r=[('sq', s), 'ones_f'], w=['prot'])
        OP('act', 'activation', out=rstd[:, 0:W0], in_=PST[:, 0:W0], func=AF.Sqrt, bias=eps_ap, scale=1.0 / D,
           r=['pst', 'cst'], w=['rstd'])
        if W > 512:
            OP('act', 'activation', out=rstd[:, 512:W], in_=PROT[:, 0:W - 512], func=AF.Sqrt, bias=eps_ap,
               scale=1.0 / D, r=['prot', 'cst'], w=['rstd'])
        OP('dve', 'reciprocal', rstd[:, 0:W], rstd[:, 0:W], r=['rstd'], w=['rstd'])
        for kc in range(KD):
            s = kc % 2
            OP('dve', 'tensor_tensor', tmp[s][:, 0:W], xt[:, kc, 0:W], rstd[:, 0:W], ALU.mult,
               r=['xt', 'rstd'], w=[('tmp', s)])
            OP('act', 'activation', out=h[:, kc, 0:W], in_=tmp[s][:, 0:W], func=AF.Identity,
               bias=svec[:, kc:kc + 1], scale=gvec[:, kc:kc + 1], r=[('tmp', s), 'VEC'], w=[('h', kc)])
        if first:
            OP('dve', 'memset', h[:, :, 0:1], 0.0, r=[('h', k) for k in range(KD)], w=[('h', k) for k in range(KD)])
        if last:
            OP('dve', 'memset', h[:, :, W - 1:W], 0.0, r=[('h', k) for k in range(KD)],
               w=[('h', k) for k in range(KD)])

    def store(dst, src, res):
        key = "s_" + (res if isinstance(res, str) else f"{res[0]}{res[1]}")
        return P.dma('sp', dst, src, reads=[res], key=key)

    def phase1(l, stream, kv_only):
        is_ctx = (stream == 1)
        ntot = TC if is_ctx else T
        N = min(NT, ntot)
        XD = XC if is_ctx else XT
        QD = QSc if is_ctx else QS
        FD = FSc if is_ctx else FS
        MD = MIXc if is_ctx else MIX
        koff = 0 if is_ctx else TC
        W = N + 2
        nch = N // 128
        m = A.off
        xt = A.alloc([KD, W], F32)
        h = A.alloc([KD, W], BF16)
        sq = [A.alloc([W], F32) for _ in range(2)]
        tmp = [A.alloc([W], F32) for _ in range(2)]
        rstd = A.alloc([W], F32)
        wb = [A.alloc([KD, 512], BF16) for _ in range(2)]
        qf = A.alloc([N], F32)
        sqb = A.alloc([N], F32)
        rq = A.alloc([N], F32)
        qnb = A.alloc([N], BF16)
        t1 = A.alloc([N], F32)
        t2 = A.alloc([N], F32)
        qo = [A.alloc([N], BF16) for _ in range(2)]
        vb = A.alloc([N], BF16)
        vt = [A.alloc([4, 128], BF16) for _ in range(2)]
        fb = [A.alloc([N], BF16) for _ in range(2)]
        cbk = A.alloc([4, N], F32)
        ccb = A.alloc([4, W], F32)
        prod = A.alloc([W], F32)
        cv = A.alloc([N], F32)
        mixb = [A.alloc([N], BF16) for _ in range(2)]
        ub = A.alloc([4, N], F32)
        gvb = A.alloc([4, N], F32)
        mn = A.alloc([N], F32)
        msq = A.alloc([N], F32)
        lrs = A.alloc([N], F32)
        vh = A.alloc([N], BF16)
        vT = A.alloc([4, 128], BF16)
        rc = A.alloc([N], F32)
        rs = A.alloc([N], F32)
        gvec = VEC[:, stream, l, 0, :]
        svec = VEC[:, stream, l, 1, :]
        if not kv_only:
            ldc(wsT, wsT_in[:, l], 'wsT', key='wsT')
            ld(gmb, gmb_in[:, l], 'gmb', key='gmb')
        blocks = list(range(8, 12)) if kv_only else list(range(36))
        cbs = sorted(set(b // 4 for b in blocks))
        wit = 0
        for ti in range(ntot // N):
            s0 = ti * N
            load_xt(xt, XD, ntot, s0, N, True)
            if not is_ctx:
                P.dma('sp', rc, ropec_in[:, s0:s0 + N], writes=['rc'], key='rc')
                P.dma('sp', rs, ropes_in[:, s0:s0 + N], writes=['rs'], key='rs')
            norm_mod(xt, h, W, gvec, svec, sq, tmp, rstd, s0 == 0, s0 + N == ntot)
            mmi = 0
            for cb in cbs:
                ws = wit % 2
                wit += 1
                P.dma('pool', wb[ws], Wi[l, cb].rearrange("p (k n) -> p k n", k=KD),
                      writes=[('wb', ws)], key=f"wb{ws}")
                for mi in range(4):
                    mb = cb * 4 + mi
                    if mb not in blocks:
                        continue
                    bi = mmi % 3
                    mmi += 1
                    bank = PA[bi]
                    bn = ('pa', bi)
                    is_halo = 20 <= mb < 28
                    hbk, hbn = HB[mb % 2]
                    for kc in range(KD):
                        OP('pe', 'matmul', bank[:, 0:N], wb[ws][:, kc, mi * 128:(mi + 1) * 128], h[:, kc, 1:1 + N],
                           start=(kc == 0), stop=(kc == KD - 1), r=[('wb', ws), ('h', kc)], w=[bn])
                        if is_halo:
                            OP('pe', 'matmul', hbk[:, 0:2], wb[ws][:, kc, mi * 128:(mi + 1) * 128],
                               h[:, kc, 0:W:W - 1], start=(kc == 0), stop=(kc == KD - 1),
                               r=[('wb', ws), ('h', kc)], w=[hbn])
                    if mb < 10:
                        isq = mb < 8
                        OP('act', 'activation', out=qf, in_=bank[:, 0:N], func=AF.Copy, r=[bn], w=['qf'])
                        OP('act', 'activation', out=sqb, in_=bank[:, 0:N], func=AF.Square, r=[bn], w=['sqb'])
                        OP('pe', 'matmul', PST2[:, 0:N], ones_f, sqb, start=True, stop=True,
                           r=['sqb', 'ones_f'], w=['pst2'])
                        OP('act', 'activation', out=rq, in_=PST2[:, 0:N], func=AF.Sqrt, bias=eps_ap, scale=1.0 / 128,
                           r=['pst2', 'cst'], w=['rq'])
                        OP('dve', 'reciprocal', rq, rq, r=['rq'], w=['rq'])
                        gq = (qg if isq else kg)[:, l:l + 1]
                        qs_ = qo[mb % 2]
                        qn_ = ('qo', mb % 2)
                        if is_ctx:
                            OP('dve', 'scalar_tensor_tensor', qs_, qf, gq, rq, ALU.mult, ALU.mult,
                               r=['qf', 'rq', 'qg', 'kg'], w=[qn_])
                        else:
                            OP('dve', 'scalar_tensor_tensor', qnb, qf, gq, rq, ALU.mult, ALU.mult,
                               r=['qf', 'rq', 'qg', 'kg'], w=['qnb'])
                            OP('pe', 'matmul', PROT[:, 0:N], rrot_b, qnb, start=True, stop=True,
                               r=['qnb', 'rrot_b'], w=['prot'])
                            OP('dve', 'tensor_tensor', t1, qnb, rc, ALU.mult, r=['qnb', 'rc'], w=['t1'])
                            OP('dve', 'tensor_tensor', t2, PROT[:, 0:N], rs, ALU.mult, r=['prot', 'rs'], w=['t2'])
                            OP('dve', 'tensor_tensor', qs_, t1, t2, ALU.add, r=['t1', 't2'], w=[qn_])
                        if isq:
                            store(QD[mb, :, s0:s0 + N], qs_, qn_)
                        else:
                            store(KS[mb - 8, :, koff + s0:koff + s0 + N], qs_, qn_)
                    elif mb < 12:
                        hv = mb - 10
                        OP('act', 'activation', out=vb, in_=bank[:, 0:N], func=AF.Copy, r=[bn], w=['vb'])
                        for j in range(nch):
                            OP('pe', 'transpose', PTR[:, j * 128:(j + 1) * 128], vb[:, j * 128:(j + 1) * 128], ident_b,
                               r=['vb', 'ident_b'], w=['ptr'])
                        OP('dve', 'tensor_copy', out=vt[hv][:, 0:nch, :],
                           in_=PTR[:, 0:nch * 128].rearrange("p (a b) -> p a b", a=nch), r=['ptr'], w=[('vt', hv)])
                        store(VS[hv, koff + s0:koff + s0 + N, :].rearrange("(j p) d -> p j d", p=128),
                              vt[hv][:, 0:nch, :], ('vt', hv))
                    elif mb < 16:
                        g = mb - 12
                        OP('act', 'activation', out=fb[g % 2], in_=bank[:, 0:N], func=AF.Copy, r=[bn], w=[('fb', g % 2)])
                        store(FD[g, :, s0:s0 + N], fb[g % 2], ('fb', g % 2))
                    elif mb < 20:
                        g = mb - 16
                        OP('act', 'activation', out=cbk[:, g, :], in_=bank[:, 0:N], func=AF.Copy, r=[bn], w=[('cbk', g)])
                    elif mb < 24:
                        g = mb - 20
                        OP('act', 'activation', out=ccb[:, g, 1:1 + N], in_=bank[:, 0:N], func=AF.Copy,
                           r=[bn], w=[('ccb', g)])
                        OP('act', 'activation', out=ccb[:, g, 0:W:W - 1], in_=hbk[:, 0:2], func=AF.Copy,
                           r=[hbn, ('ccb', g)], w=[('ccb', g)])
                    elif mb < 28:
                        g = mb - 24
                        OP('dve', 'tensor_tensor', prod[:, 1:1 + N], ccb[:, g, 1:1 + N], bank[:, 0:N], ALU.mult,
                           r=[bn, ('ccb', g)], w=['prod'])
                        OP('dve', 'tensor_tensor', prod[:, 0:W:W - 1], ccb[:, g, 0:W:W - 1], hbk[:, 0:2],
                           ALU.mult, r=[hbn, ('ccb', g), 'prod'], w=['prod'])
                        OP('dve', 'tensor_scalar', cv, prod[:, 0:N], convw[:, l, 0, g:g + 1], None, ALU.mult,
                           r=['prod', 'convw'], w=['cv'])
                        OP('dve', 'scalar_tensor_tensor', cv, prod[:, 1:1 + N], convw[:, l, 1, g:g + 1], cv,
                           ALU.mult, ALU.add, r=['prod', 'cv', 'convw'], w=['cv'])
                        OP('dve', 'scalar_tensor_tensor', cv, prod[:, 2:2 + N], convw[:, l, 2, g:g + 1], cv,
                           ALU.mult, ALU.add, r=['prod', 'cv', 'convw'], w=['cv'])
                        OP('dve', 'tensor_tensor', mixb[g % 2], cv, cbk[:, g, :], ALU.mult,
                           r=['cv', ('cbk', g)], w=[('mixb', g % 2)])
                        store(MD[12 + g, :, s0:s0 + N], mixb[g % 2], ('mixb', g % 2))
                    elif mb < 32:
                        g = mb - 28
                        OP('act', 'activation', out=ub[:, g, :], in_=bank[:, 0:N], func=AF.Gelu_apprx_tanh,
                           r=[bn], w=[('ub', g)])
                    else:
                        g = mb - 32
                        OP('act', 'activation', out=gvb[:, g, :], in_=bank[:, 0:N], func=AF.Gelu_apprx_tanh,
                           r=[bn], w=[('gvb', g)])
                        OP('act', 'activation', out=sqb, in_=gvb[:, g, :], func=AF.Square, r=[('gvb', g)], w=['sqb'])
                        OP('pe', 'matmul', PROT[:, 0:N], ones_f, gvb[:, g, :], start=(g == 0), stop=(g == 3),
                           r=[('gvb', g), 'ones_f'], w=['prot'])
                        OP('pe', 'matmul', PST2[:, 0:N], ones_f, sqb, start=(g == 0), stop=(g == 3),
                           r=['sqb', 'ones_f'], w=['pst2'])
                        if g == 3:
                            OP('dve', 'tensor_scalar', mn, PROT[:, 0:N], 1.0 / 512, None, ALU.mult, r=['prot'], w=['mn'])
                            OP('dve', 'tensor_tensor', msq, mn, mn, ALU.mult, r=['mn'], w=['msq'])
                            OP('dve', 'scalar_tensor_tensor', lrs, PST2[:, 0:N], 1.0 / 512, msq, ALU.mult,
                               ALU.subtract, r=['pst2', 'msq'], w=['lrs'])
                            OP('act', 'activation', out=lrs, in_=lrs, func=AF.Sqrt, bias=eps_ap, scale=1.0,
                               r=['lrs', 'cst'], w=['lrs'])
                            OP('dve', 'reciprocal', lrs, lrs, r=['lrs'], w=['lrs'])
                            for g2 in range(4):
                                OP('dve', 'tensor_tensor', t1, gvb[:, g2, :], mn, ALU.subtract,
                                   r=[('gvb', g2), 'mn'], w=['t1'])
                                OP('dve', 'tensor_tensor', t1, t1, lrs, ALU.mult, r=['t1', 'lrs'], w=['t1'])
                                OP('act', 'activation', out=vh, in_=t1, func=AF.Identity, bias=lnb[:, l, g2:g2 + 1],
                                   scale=lng[:, l, g2:g2 + 1], r=['t1', 'lng', 'lnb'], w=['vh'])
                                for j in range(nch):
                                    OP('pe', 'transpose', PTR[:, j * 128:(j + 1) * 128], vh[:, j * 128:(j + 1) * 128],
                                       ident_b, r=['vh', 'ident_b'], w=['ptr'])
                                OP('act', 'activation', out=vT[:, 0:nch, :],
                                   in_=PTR[:, 0:nch * 128].rearrange("p (a b) -> p a b", a=nch), func=AF.Copy,
                                   r=['ptr'], w=['vT'])
                                for j in range(nch):
                                    OP('pe', 'matmul', PROT[:, j * 128:(j + 1) * 128], vT[:, j, :], wsT[:, g2, :],
                                       start=True, stop=True, r=['vT', 'wsT'], w=['prot'])
                                OP('dve', 'tensor_tensor', t2[:, 0:N].rearrange("p (a b) -> p a b", a=nch),
                                   PROT[:, 0:N].rearrange("p (a b) -> p a b", a=nch),
                                   gmb[:, g2, :].unsqueeze(1).broadcast_to([128, nch, 128]), ALU.add,
                                   r=['prot', 'gmb'], w=['t2'])
                                OP('dve', 'tensor_tensor', mixb[g2 % 2], t2, ub[:, g2, :], ALU.mult,
                                   r=['t2', ('ub', g2)], w=[('mixb', g2 % 2)])
                                store(MD[16 + g2, :, s0:s0 + N], mixb[g2 % 2], ('mixb', g2 % 2))
        P.barrier()
        A.off = m

    def attention(stream):
        is_ctx = (stream == 1)
        nq_tot = TC if is_ctx else T
        nk = TC if is_ctx else TK
        NQ = min(512, nq_tot)
        QD = QSc if is_ctx else QS
        MD = MIXc if is_ctx else MIX
        nkc = nk // 128
        npair = nkc // 2
        m = A.off
        kT = A.alloc([2, nk], BF16)
        vv = A.alloc([nkc, 2, 128], BF16)
        qT = [A.alloc([8, NQ], BF16) for _ in range(2)]
        pT = [A.alloc([2, NQ], BF16) for _ in range(4)]
        rd = A.alloc([NQ], F32)
        ob = [A.alloc([NQ], BF16) for _ in range(2)]
        P.dma('sp', kT, KS[:, :, 0:nk].rearrange("h p t -> p h t"), writes=['kT'], key='kT')
        for hv_ in range(2):
            for c0_ in range(0, nkc, 8):
                c1_ = min(c0_ + 8, nkc)
                P.dma('sp', vv[:, c0_:c1_, hv_, :],
                      VS[hv_, c0_ * 128:c1_ * 128, :].rearrange("(c p) d -> p c d", p=128),
                      writes=[('vv', hv_)], key=f'vv{hv_}')
        PS_S = [PP[0], PP[1], PP[2]]
        PS_O = [BK[6], BK[6]]
        PS_D = [BK[7], BK[7]]
        hi = 0
        for qt in range(nq_tot // NQ):
            q0 = qt * NQ
            qs = qt % 2
            P.dma('sp', qT[qs], QD[:, :, q0:q0 + NQ].rearrange("h p t -> p h t"), writes=[('qT', qs)], key=f"qT{qs}")
            for hh in range(8):
                kvh = hh // 4
                po = PS_O[hi % 2]
                pd = PS_D[hi % 2]
                pon = 'pso'
                pdn = 'psd'

                def S(p):
                    ps = PS_S[p % 3]
                    for u in range(2):
                        kc = 2 * p + u
                        OP('pe', 'matmul', ps[:, u * 512:u * 512 + NQ], kT[:, kvh, kc * 128:(kc + 1) * 128],
                           qT[qs][:, hh, :], start=True, stop=True, r=['kT', ('qT', qs)], w=[('pss', p % 3)])
                    OP('act', 'activation', out=pT[p % 4],
                       in_=ps[:, :].rearrange("p (a b) -> p a b", a=2)[:, :, 0:NQ], func=AF.Exp,
                       r=[('pss', p % 3)], w=[('pT', p % 4)])

                def PV(p):
                    for u in range(2):
                        kc = 2 * p + u
                        OP('pe', 'matmul', po[:, 0:NQ], vv[:, kc, kvh, :], pT[p % 4][:, u, :], start=(kc == 0),
                           stop=(kc == nkc - 1), r=[('vv', kvh), ('pT', p % 4)], w=[pon])
                        OP('pe', 'matmul', pd[:, 0:NQ], ones_b, pT[p % 4][:, u, :], start=(kc == 0),
                           stop=(kc == nkc - 1), r=['ones_b', ('pT', p % 4)], w=[pdn])

                S(0)
                if npair > 1:
                    S(1)
                for p in range(npair):
                    if p + 2 < npair:
                        S(p + 2)
                    PV(p)
                OP('dve', 'reciprocal', rd, pd[:, 0:NQ], r=[pdn], w=['rd'])
                OP('dve', 'tensor_tensor', ob[hi % 2], po[:, 0:NQ], rd, ALU.mult, r=[pon, 'rd'], w=[('ob', hi % 2)])
                store(MD[hh, :, q0:q0 + NQ], ob[hi % 2], ('ob', hi % 2))
                hi += 1
        P.barrier()
        A.off = m

    def fourier(stream):
        is_ctx = (stream == 1)
        n = TC if is_ctx else T
        FD = FSc if is_ctx else FS
        MD = MIXc if is_ctx else MIX
        DN = dftnc_in if is_ctx else dftn_in
        ntc = n // 128
        NK = min(512, n)
        m = A.off
        AB = A.alloc([ntc, 4, 256], BF16)
        zT = [A.alloc([n], BF16) for _ in range(2)]
        cn = [A.alloc([ntc, NK], BF16)]
        sn = [A.alloc([ntc, NK], BF16)]
        yb = [A.alloc([NK], BF16) for _ in range(2)]
        CG = 8 if ntc >= 8 else ntc
        ei = 0
        for g in range(4):
            P.dma('sp', zT[g % 2], FD[g], writes=[('zT', g % 2)], key=f"zT{g % 2}")
            for tcp in range(ntc // 2):
                bi = ei % 3
                for u in range(2):
                    tc_ = tcp * 2 + u
                    OP('pe', 'matmul', PA[bi][:, u * 256:(u + 1) * 256], zT[g % 2][:, tc_ * 128:(tc_ + 1) * 128],
                       dftc_b, start=True, stop=True, r=[('zT', g % 2), 'dftc_b'], w=[('pa', bi)])
                src = PA[bi][:, :].rearrange("p (a b) -> p a b", a=2)
                dst = AB[:, tcp * 2:tcp * 2 + 2, g, :]
                if ei % 2 == 0:
                    OP('act', 'activation', out=dst, in_=src, func=AF.Copy, r=[('pa', bi)], w=['AB'])
                else:
                    OP('dve', 'tensor_copy', out=dst, in_=src, r=[('pa', bi)], w=['AB'])
                ei += 1
        yi = 0
        for kt in range(n // NK):
            s = 0
            for c0_ in range(0, ntc, CG):
                c1_ = min(c0_ + CG, ntc)
                cg = c0_ // CG
                if is_ctx:
                    P.dma('pool', cn[s][:, c0_:c1_, :],
                          DN[0, c0_ * 128:c1_ * 128, kt * NK:(kt + 1) * NK].rearrange("(c p) k -> p c k", p=128),
                          writes=[('cn', s, cg)], key=f"cn{s}_{cg}")
                    P.dma('pool', sn[s][:, c0_:c1_, :],
                          DN[1, c0_ * 128:c1_ * 128, kt * NK:(kt + 1) * NK].rearrange("(c p) k -> p c k", p=128),
                          writes=[('sn', s, cg)], key=f"sn{s}_{cg}")
                else:
                    P.dma('pool', cn[s][:, c0_:c1_, :],
                          DN16[0, kt].rearrange("p (c k) -> p c k", c=ntc)[:, c0_:c1_, :],
                          writes=[('cn', s, cg)], key=f"cn{s}_{cg}")
                    P.dma('pool', sn[s][:, c0_:c1_, :],
                          DN16[1, kt].rearrange("p (c k) -> p c k", c=ntc)[:, c0_:c1_, :],
                          writes=[('sn', s, cg)], key=f"sn{s}_{cg}")
            for g in range(4):
                bi = yi % 3
                for tc_ in range(ntc):
                    OP('pe', 'matmul', PA[bi][:, 0:NK], AB[:, tc_, g, 0:128], cn[s][:, tc_, :], start=(tc_ == 0),
                       stop=False, r=['AB', ('cn', s, tc_ // CG)], w=[('pa', bi)])
                    OP('pe', 'matmul', PA[bi][:, 0:NK], AB[:, tc_, g, 128:256], sn[s][:, tc_, :], start=False,
                       stop=(tc_ == ntc - 1), r=['AB', ('sn', s, tc_ // CG)], w=[('pa', bi)])
                if yi % 2 == 0:
                    OP('act', 'activation', out=yb[yi % 2], in_=PA[bi][:, 0:NK], func=AF.Copy,
                       r=[('pa', bi)], w=[('yb', yi % 2)])
                else:
                    OP('dve', 'tensor_copy', out=yb[yi % 2], in_=PA[bi][:, 0:NK], r=[('pa', bi)], w=[('yb', yi % 2)])
                store(MD[8 + g, :, kt * NK:(kt + 1) * NK], yb[yi % 2], ('yb', yi % 2))
                yi += 1
        P.barrier()
        A.off = m

    def phase3(l, stream):
        is_ctx = (stream == 1)
        ntot = TC if is_ctx else T
        N = min(NT, ntot)
        XD = XC if is_ctx else XT
        MD = MIXc if is_ctx else MIX
        m = A.off
        xt = A.alloc([KD, N], F32)
        mt = A.alloc([20, N], BF16)
        wo = [A.alloc([20, 512], BF16) for _ in range(2)]
        ga = VEC[:, stream, l, 2, :]
        wit = 0
        mmi = 0
        for ti in range(ntot // N):
            s0 = ti * N
            load_xt(xt, XD, ntot, s0, N, False)
            P.dma('sp', mt, MD[:, :, s0:s0 + N].rearrange("k p t -> p k t"), writes=['mt'], key='mt')
            for cb in range(4):
                ws = wit % 2
                wit += 1
                P.dma('pool', wo[ws], Wo[l, cb].rearrange("p (k n) -> p k n", k=20),
                      writes=[('wo', ws)], key=f"wo{ws}")
                for mi in range(4):
                    mb = cb * 4 + mi
                    bi = mmi % 3
                    mmi += 1
                    for k in range(20):
                        OP('pe', 'matmul', PA[bi][:, 0:N], wo[ws][:, k, mi * 128:(mi + 1) * 128], mt[:, k, :],
                           start=(k == 0), stop=(k == 19), r=[('wo', ws), 'mt'], w=[('pa', bi)])
                    OP('dve', 'scalar_tensor_tensor', xt[:, mb, :], PA[bi][:, 0:N], ga[:, mb:mb + 1], xt[:, mb, :],
                       ALU.mult, ALU.add, r=[('pa', bi), 'xt', 'VEC'], w=['xt'])
            P.dma('sp', (XCM if is_ctx else XM)[:, :, s0:s0 + N].rearrange("k p t -> p k t"), xt, reads=['xt'], key='xts')
        P.barrier()
        A.off = m

    def phase4(l, stream):
        is_ctx = (stream == 1)
        ntot = TC if is_ctx else T
        N = min(NT, ntot)
        XD = XC if is_ctx else XT
        W = N + 2
        m = A.off
        xt = A.alloc([KD, W], F32)
        h = A.alloc([KD, W], BF16)
        sq = [A.alloc([W], F32) for _ in range(2)]
        tmp = [A.alloc([W], F32) for _ in range(2)]
        rstd = A.alloc([W], F32)
        act = A.alloc([NFF, N], BF16)
        wg = [A.alloc([KD, 256], BF16) for _ in range(2)]
        wu = [A.alloc([KD, 256], BF16) for _ in range(2)]
        wd = [A.alloc([NFF, 256], BF16) for _ in range(2)]
        gb = sq
        cv = [t_[:, 0:N] for t_ in tmp]
        sg = cv
        gvec = VEC[:, stream, l, 3, :]
        svec = VEC[:, stream, l, 4, :]
        ga = VEC[:, stream, l, 5, :]
        wit = 0
        wdi = 0
        ji = 0
        for ti in range(ntot // N):
            s0 = ti * N
            load_xt(xt, XCM if is_ctx else XM, ntot, s0, N, True)
            norm_mod(xt, h, W, gvec, svec, sq, tmp, rstd, s0 == 0, s0 + N == ntot)
            for jb in range(NFF // 2):
                ws = wit % 2
                wit += 1
                P.dma('pool', wg[ws], Wg[l, jb].rearrange("p (k n) -> p k n", k=KD),
                      writes=[('wg', ws)], key=f"wg{ws}")
                P.dma('pool', wu[ws], Wu[l, jb].rearrange("p (k n) -> p k n", k=KD),
                      writes=[('wu', ws)], key=f"wu{ws}")
                for j2 in range(2):
                    j = jb * 2 + j2
                    s = ji % 2
                    ji += 1
                    pg = PA[0] if s == 0 else PA[1]
                    pgn = ('pa', 0 if s == 0 else 1)
                    pu = PST if s == 0 else PST2
                    pun = 'pst' if s == 0 else 'pst2'
                    hbk, hbn = HB[s]
                    for kc in range(KD):
                        OP('pe', 'matmul', pg[:, 0:N], wg[ws][:, kc, j2 * 128:(j2 + 1) * 128], h[:, kc, 1:1 + N],
                           start=(kc == 0), stop=(kc == KD - 1), r=[('wg', ws), ('h', kc)], w=[pgn])
                        OP('pe', 'matmul', hbk[:, 0:2], wg[ws][:, kc, j2 * 128:(j2 + 1) * 128],
                           h[:, kc, 0:W:W - 1], start=(kc == 0), stop=(kc == KD - 1),
                           r=[('wg', ws), ('h', kc)], w=[hbn])
                    for kc in range(KD):
                        OP('pe', 'matmul', pu[:, 0:N], wu[ws][:, kc, j2 * 128:(j2 + 1) * 128], h[:, kc, 1:1 + N],
                           start=(kc == 0), stop=(kc == KD - 1), r=[('wu', ws), ('h', kc)], w=[pun])
                    OP('act', 'activation', out=gb[s][:, 1:1 + N], in_=pg[:, 0:N], func=AF.Copy, r=[pgn], w=[('sq', s)])
                    OP('act', 'activation', out=gb[s][:, 0:W:W - 1], in_=hbk[:, 0:2], func=AF.Copy,
                       r=[hbn, ('sq', s)], w=[('sq', s)])
                    OP('dve', 'tensor_scalar', cv[s], gb[s][:, 0:N], fcw[:, l, 0, j:j + 1], None, ALU.mult,
                       r=[('sq', s), 'fcw'], w=[('tmp', s)])
                    OP('dve', 'scalar_tensor_tensor', cv[s], gb[s][:, 1:1 + N], fcw[:, l, 1, j:j + 1], cv[s],
                       ALU.mult, ALU.add, r=[('sq', s), ('tmp', s), 'fcw'], w=[('tmp', s)])
                    OP('dve', 'scalar_tensor_tensor', cv[s], gb[s][:, 2:2 + N], fcw[:, l, 2, j:j + 1], cv[s],
                       ALU.mult, ALU.add, r=[('sq', s), ('tmp', s), 'fcw'], w=[('tmp', s)])
                    OP('act', 'activation', out=sg[s], in_=cv[s], func=AF.Silu, bias=fcb[:, l, j:j + 1], scale=1.0,
                       r=[('tmp', s), 'fcb'], w=[('tmp', s)])
                    OP('dve', 'tensor_tensor', act[:, j, :], sg[s], pu[:, 0:N], ALU.mult,
                       r=[('tmp', s), pun], w=[('act', j)])
            for cb in range(8):
                ws = wdi % 2
                wdi += 1
                P.dma('pool', wd[ws], Wd[l, cb].rearrange("p (k n) -> p k n", k=NFF),
                      writes=[('wd', ws)], key=f"wd{ws}")
                for mi in range(2):
                    mb = cb * 2 + mi
                    bank = PROT if mb % 2 == 0 else PA[2]
                    bn = 'prot' if mb % 2 == 0 else ('pa', 2)
                    for j in range(NFF):
                        OP('pe', 'matmul', bank[:, 0:N], wd[ws][:, j, mi * 128:(mi + 1) * 128], act[:, j, :],
                           start=(j == 0), stop=(j == NFF - 1), r=[('wd', ws), ('act', j)], w=[bn])
                    OP('dve', 'scalar_tensor_tensor', xt[:, mb, 1:1 + N], bank[:, 0:N], ga[:, mb:mb + 1],
                       xt[:, mb, 1:1 + N], ALU.mult, ALU.add, r=[bn, 'xt', 'VEC'], w=['xt'])
            P.dma('sp', XD[:, :, s0:s0 + N].rearrange("k p t -> p k t"), xt[:, :, 1:1 + N], reads=['xt'], key='xts')
        P.barrier()
        A.off = m

    def final_norm():
        N = NT
        m = A.off
        xt = A.alloc([KD, N], F32)
        sq = [A.alloc([N], F32) for _ in range(2)]
        rstd = A.alloc([N], F32)
        y = A.alloc([KD, N], F32)
        ot = [A.alloc([D], F32) for _ in range(2)]
        fins = []
        oi = 0
        for ti in range(T // N):
            s0 = ti * N
            load_xt(xt, XT, T, s0, N, False)
            for kc in range(KD):
                s = kc % 2
                OP('act', 'activation', out=sq[s], in_=xt[:, kc, :], func=AF.Square, r=['xt'], w=[('sq', s)])
                OP('pe', 'matmul', PST[:, 0:N], ones_f, sq[s], start=(kc == 0), stop=(kc == KD - 1),
                   r=[('sq', s), 'ones_f'], w=['pst'])
            OP('act', 'activation', out=rstd, in_=PST[:, 0:N], func=AF.Sqrt, bias=eps_ap, scale=1.0 / D,
               r=['pst', 'cst'], w=['rstd'])
            OP('dve', 'reciprocal', rstd, rstd, r=['rstd'], w=['rstd'])
            for kc in range(KD):
                OP('dve', 'scalar_tensor_tensor', y[:, kc, :], xt[:, kc, :], fng[:, kc:kc + 1], rstd, ALU.mult,
                   ALU.mult, r=['xt', 'rstd', 'fng'], w=[('y', kc)])
            for tb in range(N // 128):
                o = oi % 2
                oi += 1
                for q4 in range(4):
                    bi = q4 % 3
                    for j in range(4):
                        kc = q4 * 4 + j
                        OP('pe', 'transpose', PA[bi][:, j * 128:(j + 1) * 128], y[:, kc, tb * 128:(tb + 1) * 128],
                           ident_f, r=[('y', kc), 'ident_f'], w=[('pa', bi)])
                    if q4 % 2 == 0:
                        OP('act', 'activation', out=ot[o][:, q4 * 512:(q4 + 1) * 512], in_=PA[bi][:, :], func=AF.Copy,
                           r=[('pa', bi)], w=[('ot', o)])
                    else:
                        OP('dve', 'tensor_copy', out=ot[o][:, q4 * 512:(q4 + 1) * 512], in_=PA[bi][:, :],
                           r=[('pa', bi)], w=[('ot', o)])
                fins.append(P.dma('sp', out[s0 + tb * 128:s0 + (tb + 1) * 128, :], ot[o], reads=[('ot', o)],
                                  key=f"out{o}"))
        A.off = m
        return fins

    steps = []
    for l in range(L):
        lastl = (l == L - 1)
        steps.append(('p1c', lambda l=l, lastl=lastl: phase1(l, 1, lastl)))
        steps.append(('p1x', lambda l=l: phase1(l, 0, False)))
        if not lastl:
            steps.append(('atc', lambda: attention(1)))
        if not lastl:
            steps.append(('cast', lambda l=l: cast_layer(l + 1)))
        steps.append(('atx', lambda: attention(0)))
        if not lastl:
            steps.append(('foc', lambda: fourier(1)))
        steps.append(('fox', lambda: fourier(0)))
        if not lastl:
            steps.append(('p3c', lambda l=l: phase3(l, 1)))
        steps.append(('p3x', lambda l=l: phase3(l, 0)))
        if not lastl:
            steps.append(('p4c', lambda l=l: phase4(l, 1)))
        steps.append(('p4x', lambda l=l: phase4(l, 0)))
    nsteps = len(steps) if stop_after is None else stop_after
    P.marks = [('prologue', 0)]
    for name, fn in steps[:nsteps]:
        P.marks.append((name, sum(1 for o in P.streams['pe'] if o.fn is not None)))
        fn()
    P.marks.append(('final', sum(1 for o in P.streams['pe'] if o.fn is not None)))
    fins = final_norm()
    P.emit(final_waits=fins)
    return nc, P


def _fm(v, n):
    v = np.asarray(v, np.float32)
    lead = v.shape[:-1]
    v = v.reshape(*lead, n, 128)
    nd = v.ndim
    return np.ascontiguousarray(np.moveaxis(v, -1, 0))


def _constants(T):
    ident = np.eye(128, dtype=np.float32)
    rrot = np.zeros((128, 128), np.float32)
    for base in (0, 64):
        for i in range(32):
            rrot[base + 32 + i, base + i] = -1.0
            rrot[base + i, base + 32 + i] = 1.0
    t = np.arange(T)
    row = (t // 64).astype(np.float64)
    col = (t % 64).astype(np.float64)
    freqs = 10000.0 ** (-np.arange(0, 64, 2, dtype=np.float32).astype(np.float64) / 64)
    ang_r = (row[:, None].astype(np.float32) * freqs[None, :].astype(np.float32)).astype(np.float32)
    ang_c = (col[:, None].astype(np.float32) * freqs[None, :].astype(np.float32)).astype(np.float32)
    cr, sr, cc, sc = np.cos(ang_r), np.sin(ang_r), np.cos(ang_c), np.sin(ang_c)
    ropec = np.concatenate([cr, cr, cc, cc], axis=1).T.astype(np.float32)
    ropes = np.concatenate([sr, sr, sc, sc], axis=1).T.astype(np.float32)
    j = np.arange(128)
    angc = 2 * np.pi * ((j[:, None] * j[None, :]) % 128) / 128
    dftc = (np.concatenate([np.cos(angc), np.sin(angc)], axis=1) / np.sqrt(128.0)).astype(np.float32)

    def dn(n):
        k = np.arange(n, dtype=np.int64)
        ang = 2 * np.pi * ((k[:, None] * k[None, :]) % n).astype(np.float64) / n
        o = np.empty((2, n, n), np.float32)
        o[0] = np.cos(ang) / np.sqrt(n)
        o[1] = -np.sin(ang) / np.sqrt(n)
        return o
    return dict(ident=ident, rrot=rrot, ropec=np.ascontiguousarray(ropec), ropes=np.ascontiguousarray(ropes),
                dftc=dftc, dftn=dn(T), dftnc=dn(TC))


def make_in_maps(inp, T, L, ncores):
    f = lambda k: np.asarray(inp[k], np.float32)
    shared = dict(
        w_mod=np.ascontiguousarray(f('w_mod')[:L]),
        b_mod_t=np.ascontiguousarray(f('b_mod')[:L].reshape(L, 96, 128).transpose(2, 0, 1)),
        n1g=np.ascontiguousarray(f('norm1_g')[:L].reshape(L, KD, 128).transpose(2, 0, 1)),
        n2g=np.ascontiguousarray(f('norm2_g')[:L].reshape(L, KD, 128).transpose(2, 0, 1)),
        fng=np.ascontiguousarray(f('final_norm_g').reshape(KD, 128).T),
        w_in=np.ascontiguousarray(f('w_in')[:L]),
        qg=np.ascontiguousarray(f('q_norm_g')[:L].T),
        kg=np.ascontiguousarray(f('k_norm_g')[:L].T),
        convw=np.ascontiguousarray(f('conv_w')[:L].reshape(L, 3, 4, 128).transpose(3, 0, 1, 2)),
        lng=np.ascontiguousarray(f('gm_ln_g')[:L].reshape(L, 4, 128).transpose(2, 0, 1)),
        lnb=np.ascontiguousarray(f('gm_ln_b')[:L].reshape(L, 4, 128).transpose(2, 0, 1)),
        wsT=np.ascontiguousarray(f('gm_ws')[:L].transpose(3, 0, 1, 2)),
        gmb=np.ascontiguousarray(np.broadcast_to(f('gm_b')[:L][None], (128, L, 4, 128))),
        w_out=np.ascontiguousarray(f('w_out')[:L]),
        w_up=np.ascontiguousarray(f('w_up')[:L]),
        w_down=np.ascontiguousarray(f('w_down')[:L]),
        fcw=np.ascontiguousarray(f('ffn_conv_w')[:L].reshape(L, 3, NFF, 128).transpose(3, 0, 1, 2)),
        fcb=np.ascontiguousarray(f('ffn_conv_b')[:L].reshape(L, NFF, 128).transpose(2, 0, 1)),
    )
    shared.update(_constants(T))
    x = f('x')
    ctx = f('ctx')
    c = f('c')
    cc = f('c_ctx')
    maps = []
    for b in range(ncores):
        mp = dict(shared)
        mp['x'] = np.ascontiguousarray(x[b, :T])
        mp['ctx'] = np.ascontiguousarray(ctx[b])
        cv = np.stack([c[b].reshape(KD, 128).T, cc.reshape(KD, 128).T], axis=1)
        mp['cvec'] = np.ascontiguousarray(cv)
        maps.append(mp)
    return maps


def kernel(**inputs):
    T = 4096
    L = 4
    nc, _ = build_program(T, L)
    maps = make_in_maps(inputs, T, L, NCORES)
    res = run_bass_kernel_spmd(nc, maps, core_ids=list(range(NCORES)))
    return np.stack([np.asarray(res.results[b]["out"], np.float32) for b in range(NCORES)], axis=0)
```

```python
from contextlib import ExitStack
import math
import numpy as np
import concourse.bass as bass
import concourse.mybir as mybir
from concourse.bass_utils import run_bass_kernel_spmd

F32 = mybir.dt.float32
BF16 = mybir.dt.bfloat16
AF = mybir.ActivationFunctionType
ALU = mybir.AluOpType

EPOCH = 30000
DMA_EPOCH = 1800
ENGS = ('sp', 'act', 'pe', 'dve', 'pool')

D = 2048
KD = 16
INW = 4608
MIXW = 2560
DFF = 5632
NFF = 44
EPS = 1e-6
TC = 256
NCORES = 4
ATT_LA = 1
FOUR_NK = 512


class Op:
    __slots__ = ('eng', 'fn', 'deps', 'need_inc', 'pos', 'key', 'dn', 'is_dma')

    def __init__(self, eng, fn, is_dma=False, key=None):
        self.eng = eng
        self.fn = fn
        self.deps = ()
        self.need_inc = False
        self.pos = -1
        self.key = key
        self.dn = -1
        self.is_dma = is_dma


class Prog:
    def __init__(self, nc):
        self.nc = nc
        self.es = ExitStack()
        self.streams = {e: [] for e in ENGS}
        self.res = {}
        self.dma_count = {}
        self.dma_since = {}
        self.last_compute = {}
        self.n_ops = 0

    def sbuf(self, name, shape, dt):
        return self.es.enter_context(self.nc.sbuf_tensor(name, list(shape), dt))

    def psum(self, name, shape, dt):
        return self.es.enter_context(self.nc.psum_tensor(name, list(shape), dt))

    def _track(self, o, reads, writes):
        deps = {}
        res = self.res
        for r in reads:
            st = res.get(r)
            if st is not None and st[0] is not None:
                deps[id(st[0])] = st[0]
        for w in writes:
            st = res.get(w)
            if st is not None:
                if st[0] is not None:
                    deps[id(st[0])] = st[0]
                for d in st[1].values():
                    deps[id(d)] = d
                for d in st[2]:
                    deps[id(d)] = d
        for r in reads:
            st = res.get(r)
            if st is None:
                st = res[r] = [None, {}, []]
            if o.is_dma:
                st[2].append(o)
            else:
                st[1][o.eng] = o
        for w in writes:
            res[w] = [o, {}, []]
        dl = []
        for d in deps.values():
            if d is o:
                continue
            if (not d.is_dma) and d.eng == 'pe' and o.eng == 'pe' and not o.is_dma:
                continue
            d.need_inc = True
            dl.append(d)
        o.deps = dl

    def add(self, eng, fn, reads=(), writes=()):
        o = Op(eng, fn)
        self._track(o, reads, writes)
        self.streams[eng].append(o)
        self.last_compute[eng] = o
        self.n_ops += 1
        return o

    def dma(self, eng, out, in_, reads=(), writes=(), key=None, fn=None):
        assert key is not None
        if fn is None:
            fn = lambda e: e.dma_start(out=out, in_=in_)
        o = Op(eng, fn, is_dma=True, key=key)
        n = self.dma_count.get(key, 0)
        o.dn = n
        self.dma_count[key] = n + 1
        self._track(o, reads, writes)
        self.streams[eng].append(o)
        self.dma_since[key] = o
        self.n_ops += 1
        return o

    def barrier(self):
        deps = list(self.last_compute.values()) + list(self.dma_since.values())
        for d in deps:
            d.need_inc = True
        for e in ENGS:
            o = Op(e, None)
            o.deps = [d for d in deps if d.is_dma or d.eng != e or e != 'pe']
            self.streams[e].append(o)
        self.dma_since = {}
        self.res = {}

    def emit(self, final_waits=()):
        nc = self.nc
        es = self.es
        npos = {}
        for e in ENGS:
            p = 0
            for o in self.streams[e]:
                if (not o.is_dma) and o.need_inc:
                    o.pos = p
                    p += 1
            npos[e] = p
        eng_sems = {}
        nsem = 0
        for e in ENGS:
            k = max(1, (npos[e] + EPOCH - 1) // EPOCH)
            eng_sems[e] = [es.enter_context(nc.semaphore(f"se_{e}_{i}")) for i in range(k)]
            nsem += k
        dma_sems = {}
        for key, cnt in self.dma_count.items():
            k = max(1, (cnt + DMA_EPOCH - 1) // DMA_EPOCH)
            dma_sems[key] = [es.enter_context(nc.semaphore(f"sd_{len(dma_sems)}_{i}")) for i in range(k)]
            nsem += k
        self.nsem = nsem
        block = es.enter_context(nc.Block())
        streams = self.streams
        final_waits = list(final_waits)

        def make_body(e):
            def body(eng):
                known = {x: -1 for x in ENGS}
                known_d = {}

                def wait_for(d):
                    if d.is_dma:
                        ep = d.dn // DMA_EPOCH
                        kk = (d.key, ep)
                        v = (d.dn % DMA_EPOCH + 1) * 16
                        if known_d.get(kk, 0) >= v:
                            return
                        known_d[kk] = v
                        eng.wait_ge(dma_sems[d.key][ep], v)
                    else:
                        if known[d.eng] >= d.pos:
                            return
                        known[d.eng] = d.pos
                        ep = d.pos // EPOCH
                        eng.wait_ge(eng_sems[d.eng][ep], d.pos % EPOCH + 1)

                for o in streams[e]:
                    for d in o.deps:
                        wait_for(d)
                    if o.fn is None:
                        continue
                    ins = o.fn(eng)
                    if o.is_dma:
                        ep = o.dn // DMA_EPOCH
                        ins.then_inc(dma_sems[o.key][ep], 16)
                    elif o.need_inc:
                        ep = o.pos // EPOCH
                        ins.then_inc(eng_sems[e][ep], 1)
                if e == 'sp':
                    for d in final_waits:
                        wait_for(d)
            return body

        block.sync(make_body('sp'))
        block.scalar(make_body('act'))
        block.tensor(make_body('pe'))
        block.vector(make_body('dve'))
        block.gpsimd(make_body('pool'))
        es.close()


class Arena:
    def __init__(self, P, nbytes):
        self.t = P.sbuf("arena", [128, nbytes // 4], F32)
        self.off = 0
        self.size = nbytes

    def alloc(self, shape, dt):
        n = 1
        for s in shape:
            n *= s
        nb = n * (4 if dt == F32 else 2)
        nb = (nb + 63) // 64 * 64
        o = self.off
        self.off += nb
        assert self.off <= self.size, ("arena overflow", self.off, self.size)
        v = self.t[:, o // 4:(o + nb) // 4]
        if dt == BF16:
            v = v.bitcast(BF16)
        v = v[:, 0:n]
        if len(shape) == 2:
            v = v.rearrange("p (a b) -> p a b", a=shape[0])
        elif len(shape) == 3:
            v = v.rearrange("p (a b c) -> p a b c", a=shape[0], b=shape[1])
        elif len(shape) == 4:
            v = v.rearrange("p (a b c d) -> p a b c d", a=shape[0], b=shape[1], c=shape[2])
        return v


def build_program(T, L, debug=False, stop_after=None):
    nc = bass.Bass("TRN2", target_bir_lowering=False)
    TK = TC + T
    NT = 512

    def din(name, shape, dt=F32):
        return nc.dram_tensor(name, list(shape), dt, kind="ExternalInput").ap()

    def dscr(name, shape, dt):
        return nc.dram_tensor(name, list(shape), dt, kind="ExternalOutput" if debug else "Internal").ap()

    x_in = din("x", [T, D])
    ctx_in = din("ctx", [TC, D])
    cvec_in = din("cvec", [128, 2, KD])
    w_mod = din("w_mod", [L, D, 6 * D])
    b_mod_t = din("b_mod_t", [128, L, 96])
    n1g_in = din("n1g", [128, L, KD])
    n2g_in = din("n2g", [128, L, KD])
    fng_in = din("fng", [128, KD])
    w_in = din("w_in", [L, D, INW])
    qg_in = din("qg", [128, L])
    kg_in = din("kg", [128, L])
    convw_in = din("convw", [128, L, 3, 4])
    lng_in = din("lng", [128, L, 4])
    lnb_in = din("lnb", [128, L, 4])
    wsT_in = din("wsT", [128, L, 4, 128])
    gmb_in = din("gmb", [128, L, 4, 128])
    w_out = din("w_out", [L, MIXW, D])
    w_up = din("w_up", [L, D, 2 * DFF])
    fcw_in = din("fcw", [128, L, 3, NFF])
    fcb_in = din("fcb", [128, L, NFF])
    w_down = din("w_down", [L, DFF, D])
    ident_in = din("ident", [128, 128])
    rrot_in = din("rrot", [128, 128])
    ropec_in = din("ropec", [128, T])
    ropes_in = din("ropes", [128, T])
    dftc_in = din("dftc", [128, 256])
    dftn_in = din("dftn", [2, T, T])
    dftnc_in = din("dftnc", [2, TC, TC])
    out = nc.dram_tensor("out", [T, D], F32, kind="ExternalOutput").ap()

    XT = dscr("XT", [KD, 128, T], F32)
    XC = dscr("XC", [KD, 128, TC], F32)
    XM = dscr("XM", [KD, 128, T], F32)
    XCM = dscr("XCM", [KD, 128, TC], F32)
    QS = dscr("QS", [8, 128, T], BF16)
    QSc = dscr("QSc", [8, 128, TC], BF16)
    KS = dscr("KS", [2, 128, TK], BF16)
    VS = dscr("VS", [2, TK, 128], BF16)
    FS = dscr("FS", [4, 128, T], BF16)
    FSc = dscr("FSc", [4, 128, TC], BF16)
    MIX = dscr("MIX", [20, 128, T], BF16)
    MIXc = dscr("MIXc", [20, 128, TC], BF16)

    Wi = nc.dram_tensor("Wi", [L, 9, 128, KD * 512], BF16, kind="Internal").ap()
    Wo = nc.dram_tensor("Wo", [L, 4, 128, 20 * 512], BF16, kind="Internal").ap()
    Wg = nc.dram_tensor("Wg", [L, 22, 128, KD * 256], BF16, kind="Internal").ap()
    Wu = nc.dram_tensor("Wu", [L, 22, 128, KD * 256], BF16, kind="Internal").ap()
    Wd = nc.dram_tensor("Wd", [L, 8, 128, NFF * 256], BF16, kind="Internal").ap()

    NKF = FOUR_NK
    DN16 = nc.dram_tensor("DN16", [2, T // NKF, 128, (T // 128) * NKF], BF16, kind="Internal").ap()

    P = Prog(nc)
    A = Arena(P, 200 * 1024)

    def cast_dft():
        ntc_ = T // 128
        for mtx in range(2):
            for kt in range(T // NKF):
                for c0_ in range(0, ntc_, 8):
                    c1_ = min(c0_ + 8, ntc_)
                    P.dma('pool', DN16[mtx, kt].rearrange("p (c k) -> p c k", c=ntc_)[:, c0_:c1_, :],
                          dftn_in[mtx, c0_ * 128:c1_ * 128, kt * NKF:(kt + 1) * NKF].rearrange("(c p) k -> p c k", p=128),
                          key='cw')

    def cast_layer(l):
        def c(dst, src, k):
            P.dma('pool', dst.rearrange("p (k n) -> p k n", k=k), src.rearrange("(k p) n -> p k n", p=128),
                  key='cw')
        for cb in range(9):
            c(Wi[l, cb], w_in[l, :, cb * 512:(cb + 1) * 512], KD)
        for cb in range(4):
            c(Wo[l, cb], w_out[l, :, cb * 512:(cb + 1) * 512], 20)
        for jb in range(22):
            c(Wg[l, jb], w_up[l, :, jb * 256:(jb + 1) * 256], KD)
            c(Wu[l, jb], w_up[l, :, DFF + jb * 256:DFF + (jb + 1) * 256], KD)
        for cb in range(8):
            c(Wd[l, cb], w_down[l, :, cb * 256:(cb + 1) * 256], NFF)

    def OP(eng, meth, *args, r=(), w=(), **kw):
        return P.add(eng, lambda e: getattr(e, meth)(*args, **kw), reads=r, writes=w)

    PP = [P.psum(f"pp{i}", [128, 1024], F32) for i in range(4)]
    BK = [PP[i // 2][:, (i % 2) * 512:(i % 2 + 1) * 512] for i in range(8)]
    PA = [BK[0], BK[1], BK[2]]
    PST = BK[3]
    PST2 = BK[4]
    PROT = BK[5]
    PSM = BK[6]
    PTRF = BK[7]
    PTR = PTRF.bitcast(BF16)
    HB = [(PSM, 'psm'), (PTRF, 'ptr')]

    ident_f = A.alloc([128], F32)
    ident_b = A.alloc([128], BF16)
    ones_f = A.alloc([128], F32)
    ones_b = A.alloc([128], BF16)
    rrot_b = A.alloc([128], BF16)
    dftc_b = A.alloc([256], BF16)
    cst = A.alloc([4], F32)
    MOD = A.alloc([L, 96, 2], F32)
    VEC = A.alloc([2, L, 6, KD], F32)
    n1g = A.alloc([L, KD], F32)
    n2g = A.alloc([L, KD], F32)
    fng = A.alloc([KD], F32)
    qg = A.alloc([L], F32)
    kg = A.alloc([L], F32)
    convw = A.alloc([L, 3, 4], F32)
    lng = A.alloc([L, 4], F32)
    lnb = A.alloc([L, 4], F32)
    fcw = A.alloc([L, 3, NFF], F32)
    fcb = A.alloc([L, NFF], F32)
    wsT = A.alloc([4, 128], BF16)
    gmb = A.alloc([4, 128], F32)
    base_mark = A.off

    def ld(dst, src, name, key='cst0'):
        return P.dma('sp', dst, src, writes=[name], key=key)

    def ldc(dst, src, name, key='cst1'):
        return P.dma('pool', dst, src, writes=[name], key=key)

    cast_layer(0)
    cast_dft()
    ld(ident_f, ident_in, 'ident_f')
    ldc(ident_b, ident_in, 'ident_b')
    ldc(rrot_b, rrot_in, 'rrot_b')
    ldc(dftc_b, dftc_in, 'dftc_b')
    ld(n1g, n1g_in, 'n1g')
    ld(n2g, n2g_in, 'n2g')
    ld(fng, fng_in, 'fng')
    ld(qg, qg_in, 'qg', key='qgl')
    ld(kg, kg_in, 'kg')
    ld(convw, convw_in, 'convw')
    ld(lng, lng_in, 'lng')
    ld(lnb, lnb_in, 'lnb')
    ld(fcw, fcw_in, 'fcw')
    ld(fcb, fcb_in, 'fcb')
    OP('dve', 'memset', ones_f, 1.0, w=['ones_f'])
    OP('dve', 'memset', ones_b, 1.0, w=['ones_b'])
    OP('dve', 'memset', cst, 0.0, w=['cst'])
    OP('dve', 'memset', cst[:, 0:1], EPS, r=['cst'], w=['cst'])
    OP('dve', 'tensor_scalar', qg, qg, 128.0 ** -0.5, None, ALU.mult, r=['qg'], w=['qg'])
    eps_ap = cst[:, 0:1]
    P.barrier()

    m0 = A.off
    scv = A.alloc([2, KD], F32)
    bmt = A.alloc([L, 96], F32)
    wm = [A.alloc([KD, 512], F32) for _ in range(2)]
    ld(scv, cvec_in, 'scv', key='scv')
    ld(bmt, b_mod_t, 'bmt', key='bmt')
    OP('act', 'activation', out=scv, in_=scv, func=AF.Silu, r=['scv'], w=['scv'])
    it = 0
    for l in range(L):
        for cb in range(24):
            s = it % 2
            P.dma('sp', wm[s], w_mod[l, :, cb * 512:(cb + 1) * 512].rearrange("(k p) n -> p k n", p=128),
                  writes=[('wm', s)], key=f"wm{s}")
            for mi in range(4):
                for kc in range(KD):
                    OP('pe', 'matmul', PSM[:, 2 * mi:2 * mi + 2], wm[s][:, kc, mi * 128:(mi + 1) * 128],
                       scv[:, :, kc], start=(kc == 0), stop=(kc == KD - 1),
                       r=[('wm', s), 'scv'], w=['psm'])
            OP('dve', 'tensor_tensor', MOD[:, l, cb * 4:cb * 4 + 4, :],
               PSM[:, 0:8].rearrange("p (a b) -> p a b", a=4),
               bmt[:, l, cb * 4:cb * 4 + 4].unsqueeze(2).broadcast_to([128, 4, 2]), ALU.add,
               r=['psm', 'bmt'], w=['MOD'])
            it += 1
    for st in range(2):
        for l in range(L):
            for (dst, src, gain) in ((0, 16, n1g), (3, 64, n2g)):
                OP('dve', 'tensor_scalar', VEC[:, st, l, dst, :], MOD[:, l, src:src + 16, st], 1.0, None, ALU.add,
                   r=['MOD'], w=['VEC'])
                OP('dve', 'tensor_tensor', VEC[:, st, l, dst, :], VEC[:, st, l, dst, :], gain[:, l, :], ALU.mult,
                   r=['VEC', 'n1g', 'n2g'], w=['VEC'])
            for (dst, src) in ((1, 0), (2, 32), (4, 48), (5, 80)):
                OP('dve', 'tensor_copy', out=VEC[:, st, l, dst, :], in_=MOD[:, l, src:src + 16, st],
                   r=['MOD'], w=['VEC'])
    P.barrier()
    A.off = m0

    def to_feature_major(src, dst, ntok):
        m = A.off
        xin = [A.alloc([D], F32) for _ in range(2)]
        stg = [A.alloc([KD, 128], F32) for _ in range(2)]
        for tb in range(ntok // 128):
            s = tb % 2
            P.dma('sp', xin[s], src[tb * 128:(tb + 1) * 128, :], writes=[('xin', s)], key=f"xin{s}")
            for q4 in range(4):
                bank = PA[q4 % 3]
                bn = ('pa', q4 % 3)
                for j in range(4):
                    kc = q4 * 4 + j
                    OP('pe', 'transpose', bank[:, j * 128:(j + 1) * 128], xin[s][:, kc * 128:(kc + 1) * 128], ident_f,
                       r=[('xin', s), 'ident_f'], w=[bn])
                src4 = bank[:, :].rearrange("p (a b) -> p a b", a=4)
                if q4 % 2 == 0:
                    OP('act', 'activation', out=stg[s][:, q4 * 4:q4 * 4 + 4, :], in_=src4, func=AF.Copy,
                       r=[bn], w=[('stg', s)])
                else:
                    OP('dve', 'tensor_copy', out=stg[s][:, q4 * 4:q4 * 4 + 4, :], in_=src4, r=[bn], w=[('stg', s)])
            P.dma('sp', dst[:, :, tb * 128:(tb + 1) * 128].rearrange("k p t -> p k t"), stg[s],
                  reads=[('stg', s)], key=f"s_stg{s}")
        P.barrier()
        A.off = m

    to_feature_major(x_in, XT, T)
    to_feature_major(ctx_in, XC, TC)

    def load_xt(xt, XD, ntot, s0, N, halo):
        if halo:
            lo = max(s0 - 1, 0)
            hi = min(s0 + N + 1, ntot)
            c0 = lo - (s0 - 1)
            if s0 == 0:
                OP('dve', 'memset', xt[:, :, 0:1], 0.0, w=['xt'])
            if s0 + N == ntot:
                OP('dve', 'memset', xt[:, :, N + 1:N + 2], 0.0, w=['xt'])
            if hi - lo == N + 2:
                P.dma('sp', xt[:, :, 0:N + 1], XD[:, :, lo:hi - 1].rearrange("k p t -> p k t"),
                      writes=['xt'], key='xt')
                P.dma('sp', xt[:, :, N:N + 2], XD[:, :, hi - 2:hi].rearrange("k p t -> p k t"),
                      reads=['xt'], writes=['xt'], key='xt')
            else:
                P.dma('sp', xt[:, :, c0:c0 + (hi - lo)], XD[:, :, lo:hi].rearrange("k p t -> p k t"),
                      writes=['xt'], key='xt')
        else:
            P.dma('sp', xt[:, :, 0:N], XD[:, :, s0:s0 + N].rearrange("k p t -> p k t"), writes=['xt'], key='xt')

    def norm_mod(xt, h, W, gvec, svec, sq, tmp, rstd, first, last):
        W0 = min(W, 512)
        sqh = [q_.bitcast(BF16) for q_ in sq]
        for kc in range(KD):
            s = kc % 2
            OP('act', 'activation', out=sqh[s][:, 0:W], in_=xt[:, kc, 0:W], func=AF.Square,
               r=['xt'], w=[('sq', s)])
            OP('pe', 'matmul', PST[:, 0:W0], ones_b, sqh[s][:, 0:W0], start=(kc == 0), stop=(kc == KD - 1),
               r=[('sq', s), 'ones_b'], w=['pst'])
            if W > 512:
                OP('pe', 'matmul', PROT[:, 0:W - 512], ones_b, sqh[s][:, 512:W], start=(kc == 0),
                   stop=(kc == KD - 1), r=[('sq', s), 'ones_b'], w=['prot'])
        OP('act', 'activation', out=rstd[:, 0:W0], in_=PST[:, 0:W0], func=AF.Ln, bias=eps_ap, scale=1.0 / D,
           r=['pst', 'cst'], w=['rstd'])
        if W > 512:
            OP('act', 'activation', out=rstd[:, 512:W], in_=PROT[:, 0:W - 512], func=AF.Ln, bias=eps_ap,
               scale=1.0 / D, r=['prot', 'cst'], w=['rstd'])
        OP('act', 'activation', out=rstd[:, 0:W], in_=rstd[:, 0:W], func=AF.Exp, scale=-0.5, r=['rstd'], w=['rstd'])
        for kc in range(KD):
            s = kc % 2
            OP('dve', 'tensor_tensor', tmp[s][:, 0:W], xt[:, kc, 0:W], rstd[:, 0:W], ALU.mult,
               r=['xt', 'rstd'], w=[('tmp', s)])
            OP('act', 'activation', out=h[:, kc, 0:W], in_=tmp[s][:, 0:W], func=AF.Identity,
               bias=svec[:, kc:kc + 1], scale=gvec[:, kc:kc + 1], r=[('tmp', s), 'VEC'], w=[('h', kc)])
        if first:
            OP('dve', 'memset', h[:, :, 0:1], 0.0, r=[('h', k) for k in range(KD)], w=[('h', k) for k in range(KD)])
        if last:
            OP('dve', 'memset', h[:, :, W - 1:W], 0.0, r=[('h', k) for k in range(KD)],
               w=[('h', k) for k in range(KD)])

    def store(dst, src, res):
        key = "s_" + (res if isinstance(res, str) else f"{res[0]}{res[1]}")
        return P.dma('sp', dst, src, reads=[res], key=key)

    def phase1(l, stream, kv_only):
        is_ctx = (stream == 1)
        ntot = TC if is_ctx else T
        N = min(NT, ntot)
        XD = XC if is_ctx else XT
        QD = QSc if is_ctx else QS
        FD = FSc if is_ctx else FS
        MD = MIXc if is_ctx else MIX
        koff = 0 if is_ctx else TC
        W = N + 2
        nch = N // 128
        m = A.off
        xt = A.alloc([KD, W], F32)
        h = A.alloc([KD, W], BF16)
        sq = [A.alloc([W], F32) for _ in range(2)]
        tmp = [A.alloc([W], F32) for _ in range(2)]
        rstd = A.alloc([W], F32)
        wb = [A.alloc([KD, 512], BF16) for _ in range(2)]
        qf = A.alloc([N], F32)
        sqb = A.alloc([N], F32)
        sqbh = sqb.bitcast(BF16)[:, 0:N]
        rq = A.alloc([N], F32)
        qnb = A.alloc([N], BF16)
        t1 = A.alloc([N], F32)
        t2 = A.alloc([N], F32)
        qo = [A.alloc([N], BF16) for _ in range(2)]
        vb = A.alloc([N], BF16)
        vt = [A.alloc([4, 128], BF16) for _ in range(2)]
        fb = [A.alloc([N], BF16) for _ in range(2)]
        cbk = A.alloc([4, N], F32)
        ccb = A.alloc([4, W], F32)
        prod = A.alloc([W], F32)
        cv = A.alloc([N], F32)
        mixb = [A.alloc([N], BF16) for _ in range(2)]
        ub = A.alloc([4, N], F32)
        gvb = A.alloc([4, N], BF16)
        mn = A.alloc([N], F32)
        msq = A.alloc([N], F32)
        lrs = A.alloc([N], F32)
        vh = A.alloc([N], BF16)
        vT = A.alloc([4, 128], BF16)
        rc = A.alloc([N], F32)
        rs = A.alloc([N], F32)
        gvec = VEC[:, stream, l, 0, :]
        svec = VEC[:, stream, l, 1, :]
        if not kv_only:
            ldc(wsT, wsT_in[:, l], 'wsT', key='wsT')
            ld(gmb, gmb_in[:, l], 'gmb', key='gmb')
        blocks = list(range(8, 12)) if kv_only else list(range(36))
        cbs = sorted(set(b // 4 for b in blocks))
        wit = 0
        for ti in range(ntot // N):
            s0 = ti * N
            load_xt(xt, XD, ntot, s0, N, True)
            if not is_ctx:
                P.dma('sp', rc, ropec_in[:, s0:s0 + N], writes=['rc'], key='rc')
                P.dma('sp', rs, ropes_in[:, s0:s0 + N], writes=['rs'], key='rs')
            norm_mod(xt, h, W, gvec, svec, sq, tmp, rstd, s0 == 0, s0 + N == ntot)
            mmi = 0
            for cb in cbs:
                ws = wit % 2
                wit += 1
                P.dma('pool', wb[ws], Wi[l, cb].rearrange("p (k n) -> p k n", k=KD),
                      writes=[('wb', ws)], key=f"wb{ws}")
                for mi in range(4):
                    mb = cb * 4 + mi
                    if mb not in blocks:
                        continue
                    bi = mmi % 3
                    mmi += 1
                    bank = PA[bi]
                    bn = ('pa', bi)
                    is_halo = 20 <= mb < 28
                    hbk, hbn = HB[mb % 2]
                    for kc in range(KD):
                        OP('pe', 'matmul', bank[:, 0:N], wb[ws][:, kc, mi * 128:(mi + 1) * 128], h[:, kc, 1:1 + N],
                           start=(kc == 0), stop=(kc == KD - 1), r=[('wb', ws), ('h', kc)], w=[bn])
                        if is_halo:
                            OP('pe', 'matmul', hbk[:, 0:2], wb[ws][:, kc, mi * 128:(mi + 1) * 128],
                               h[:, kc, 0:W:W - 1], start=(kc == 0), stop=(kc == KD - 1),
                               r=[('wb', ws), ('h', kc)], w=[hbn])
                    if mb < 10:
                        isq = mb < 8
                        OP('act', 'activation', out=qf, in_=bank[:, 0:N], func=AF.Copy, r=[bn], w=['qf'])
                        OP('act', 'activation', out=sqbh, in_=bank[:, 0:N], func=AF.Square, r=[bn], w=['sqb'])
                        OP('pe', 'matmul', PST2[:, 0:N], ones_b, sqbh, start=True, stop=True,
                           r=['sqb', 'ones_b'], w=['pst2'])
                        OP('act', 'activation', out=rq, in_=PST2[:, 0:N], func=AF.Ln, bias=eps_ap, scale=1.0 / 128,
                           r=['pst2', 'cst'], w=['rq'])
                        OP('act', 'activation', out=rq, in_=rq, func=AF.Exp, scale=-0.5, r=['rq'], w=['rq'])
                        gq = (qg if isq else kg)[:, l:l + 1]
                        qs_ = qo[mb % 2]
                        qn_ = ('qo', mb % 2)
                        if is_ctx:
                            OP('dve', 'scalar_tensor_tensor', qs_, qf, gq, rq, ALU.mult, ALU.mult,
                               r=['qf', 'rq', 'qg', 'kg'], w=[qn_])
                        else:
                            OP('dve', 'scalar_tensor_tensor', qnb, qf, gq, rq, ALU.mult, ALU.mult,
                               r=['qf', 'rq', 'qg', 'kg'], w=['qnb'])
                            OP('pe', 'matmul', PROT[:, 0:N], rrot_b, qnb, start=True, stop=True,
                               r=['qnb', 'rrot_b'], w=['prot'])
                            OP('dve', 'tensor_tensor', t1, qnb, rc, ALU.mult, r=['qnb', 'rc'], w=['t1'])
                            OP('dve', 'tensor_tensor', t2, PROT[:, 0:N], rs, ALU.mult, r=['prot', 'rs'], w=['t2'])
                            OP('dve', 'tensor_tensor', qs_, t1, t2, ALU.add, r=['t1', 't2'], w=[qn_])
                        if isq:
                            store(QD[mb, :, s0:s0 + N], qs_, qn_)
                        else:
                            store(KS[mb - 8, :, koff + s0:koff + s0 + N], qs_, qn_)
                    elif mb < 12:
                        hv = mb - 10
                        OP('act', 'activation', out=vb, in_=bank[:, 0:N], func=AF.Copy, r=[bn], w=['vb'])
                        for j in range(nch):
                            OP('pe', 'transpose', PTR[:, j * 128:(j + 1) * 128], vb[:, j * 128:(j + 1) * 128], ident_b,
                               r=['vb', 'ident_b'], w=['ptr'])
                        OP('dve', 'tensor_copy', out=vt[hv][:, 0:nch, :],
                           in_=PTR[:, 0:nch * 128].rearrange("p (a b) -> p a b", a=nch), r=['ptr'], w=[('vt', hv)])
                        store(VS[hv, koff + s0:koff + s0 + N, :].rearrange("(j p) d -> p j d", p=128),
                              vt[hv][:, 0:nch, :], ('vt', hv))
                    elif mb < 16:
                        g = mb - 12
                        OP('act', 'activation', out=fb[g % 2], in_=bank[:, 0:N], func=AF.Copy, r=[bn], w=[('fb', g % 2)])
                        store(FD[g, :, s0:s0 + N], fb[g % 2], ('fb', g % 2))
                    elif mb < 20:
                        g = mb - 16
                        OP('act', 'activation', out=cbk[:, g, :], in_=bank[:, 0:N], func=AF.Copy, r=[bn], w=[('cbk', g)])
                    elif mb < 24:
                        g = mb - 20
                        OP('act', 'activation', out=ccb[:, g, 1:1 + N], in_=bank[:, 0:N], func=AF.Copy,
                           r=[bn], w=[('ccb', g)])
                        OP('act', 'activation', out=ccb[:, g, 0:W:W - 1], in_=hbk[:, 0:2], func=AF.Copy,
                           r=[hbn, ('ccb', g)], w=[('ccb', g)])
                    elif mb < 28:
                        g = mb - 24
                        OP('dve', 'tensor_tensor', prod[:, 1:1 + N], ccb[:, g, 1:1 + N], bank[:, 0:N], ALU.mult,
                           r=[bn, ('ccb', g)], w=['prod'])
                        OP('dve', 'tensor_tensor', prod[:, 0:W:W - 1], ccb[:, g, 0:W:W - 1], hbk[:, 0:2],
                           ALU.mult, r=[hbn, ('ccb', g), 'prod'], w=['prod'])
                        OP('dve', 'tensor_scalar', cv, prod[:, 0:N], convw[:, l, 0, g:g + 1], None, ALU.mult,
                           r=['prod', 'convw'], w=['cv'])
                        OP('dve', 'scalar_tensor_tensor', cv, prod[:, 1:1 + N], convw[:, l, 1, g:g + 1], cv,
                           ALU.mult, ALU.add, r=['prod', 'cv', 'convw'], w=['cv'])
                        OP('dve', 'scalar_tensor_tensor', cv, prod[:, 2:2 + N], convw[:, l, 2, g:g + 1], cv,
                           ALU.mult, ALU.add, r=['prod', 'cv', 'convw'], w=['cv'])
                        OP('dve', 'tensor_tensor', mixb[g % 2], cv, cbk[:, g, :], ALU.mult,
                           r=['cv', ('cbk', g)], w=[('mixb', g % 2)])
                        store(MD[12 + g, :, s0:s0 + N], mixb[g % 2], ('mixb', g % 2))
                    elif mb < 32:
                        g = mb - 28
                        OP('act', 'activation', out=ub[:, g, :], in_=bank[:, 0:N], func=AF.Gelu_apprx_tanh,
                           r=[bn], w=[('ub', g)])
                    else:
                        g = mb - 32
                        OP('act', 'activation', out=gvb[:, g, :], in_=bank[:, 0:N], func=AF.Gelu_apprx_tanh,
                           r=[bn], w=[('gvb', g)])
                        OP('act', 'activation', out=sqbh, in_=gvb[:, g, :], func=AF.Square, r=[('gvb', g)], w=['sqb'])
                        OP('pe', 'matmul', PROT[:, 0:N], ones_b, gvb[:, g, :], start=(g == 0), stop=(g == 3),
                           r=[('gvb', g), 'ones_b'], w=['prot'])
                        OP('pe', 'matmul', PST2[:, 0:N], ones_b, sqbh, start=(g == 0), stop=(g == 3),
                           r=['sqb', 'ones_b'], w=['pst2'])
                        if g == 3:
                            OP('dve', 'tensor_scalar', mn, PROT[:, 0:N], 1.0 / 512, None, ALU.mult, r=['prot'], w=['mn'])
                            OP('dve', 'tensor_tensor', msq, mn, mn, ALU.mult, r=['mn'], w=['msq'])
                            OP('dve', 'scalar_tensor_tensor', lrs, PST2[:, 0:N], 1.0 / 512, msq, ALU.mult,
                               ALU.subtract, r=['pst2', 'msq'], w=['lrs'])
                            OP('act', 'activation', out=lrs, in_=lrs, func=AF.Ln, bias=eps_ap, scale=1.0,
                               r=['lrs', 'cst'], w=['lrs'])
                            OP('act', 'activation', out=lrs, in_=lrs, func=AF.Exp, scale=-0.5, r=['lrs'], w=['lrs'])
                            for g2 in range(4):
                                OP('dve', 'tensor_tensor', t1, gvb[:, g2, :], mn, ALU.subtract,
                                   r=[('gvb', g2), 'mn'], w=['t1'])
                                OP('dve', 'tensor_tensor', t1, t1, lrs, ALU.mult, r=['t1', 'lrs'], w=['t1'])
                                OP('act', 'activation', out=vh, in_=t1, func=AF.Identity, bias=lnb[:, l, g2:g2 + 1],
                                   scale=lng[:, l, g2:g2 + 1], r=['t1', 'lng', 'lnb'], w=['vh'])
                                for j in range(nch):
                                    OP('pe', 'transpose', PTR[:, j * 128:(j + 1) * 128], vh[:, j * 128:(j + 1) * 128],
                                       ident_b, r=['vh', 'ident_b'], w=['ptr'])
                                OP('act', 'activation', out=vT[:, 0:nch, :],
                                   in_=PTR[:, 0:nch * 128].rearrange("p (a b) -> p a b", a=nch), func=AF.Copy,
                                   r=['ptr'], w=['vT'])
                                for j in range(nch):
                                    OP('pe', 'matmul', PROT[:, j * 128:(j + 1) * 128], vT[:, j, :], wsT[:, g2, :],
                                       start=True, stop=True, r=['vT', 'wsT'], w=['prot'])
                                OP('dve', 'tensor_tensor', t2[:, 0:N].rearrange("p (a b) -> p a b", a=nch),
                                   PROT[:, 0:N].rearrange("p (a b) -> p a b", a=nch),
                                   gmb[:, g2, :].unsqueeze(1).broadcast_to([128, nch, 128]), ALU.add,
                                   r=['prot', 'gmb'], w=['t2'])
                                OP('dve', 'tensor_tensor', mixb[g2 % 2], t2, ub[:, g2, :], ALU.mult,
                                   r=['t2', ('ub', g2)], w=[('mixb', g2 % 2)])
                                store(MD[16 + g2, :, s0:s0 + N], mixb[g2 % 2], ('mixb', g2 % 2))
        P.barrier()
        A.off = m

    def attention(stream):
        is_ctx = (stream == 1)
        nq_tot = TC if is_ctx else T
        nk = TC if is_ctx else TK
        NQ = min(512, nq_tot)
        QD = QSc if is_ctx else QS
        MD = MIXc if is_ctx else MIX
        nkc = nk // 128
        npair = nkc // 2
        m = A.off
        kT = A.alloc([2, nk], BF16)
        vv = A.alloc([nkc, 2, 128], BF16)
        qT = [A.alloc([8, NQ], BF16) for _ in range(2)]
        pT = [A.alloc([2, NQ], BF16) for _ in range(4)]
        rd = A.alloc([NQ], F32)
        ob = [A.alloc([NQ], BF16) for _ in range(2)]
        P.dma('sp', kT, KS[:, :, 0:nk].rearrange("h p t -> p h t"), writes=['kT'], key='kT')
        for hv_ in range(2):
            for c0_ in range(0, nkc, 8):
                c1_ = min(c0_ + 8, nkc)
                P.dma('sp', vv[:, c0_:c1_, hv_, :],
                      VS[hv_, c0_ * 128:c1_ * 128, :].rearrange("(c p) d -> p c d", p=128),
                      writes=[('vv', hv_)], key=f'vv{hv_}')
        if ATT_LA == 2:
            PS_S = [PP[0], PP[1], PP[2]]
            PS_O = [BK[6], BK[6]]
            PS_D = [BK[7], BK[7]]
        else:
            PS_S = [PP[0], PP[1]]
            PS_O = [BK[4], BK[5]]
            PS_D = [BK[6], BK[7]]
        NS = len(PS_S)
        hi = 0
        for qt in range(nq_tot // NQ):
            q0 = qt * NQ
            qs = qt % 2
            P.dma('sp', qT[qs], QD[:, :, q0:q0 + NQ].rearrange("h p t -> p h t"), writes=[('qT', qs)], key=f"qT{qs}")
            for hh in range(8):
                kvh = hh // 4
                po = PS_O[hi % 2]
                pd = PS_D[hi % 2]
                pon = ('pso', hi % 2 if ATT_LA == 1 else 0)
                pdn = ('psd', hi % 2 if ATT_LA == 1 else 0)

                def S(p):
                    ps = PS_S[p % NS]
                    for u in range(2):
                        kc = 2 * p + u
                        OP('pe', 'matmul', ps[:, u * 512:u * 512 + NQ], kT[:, kvh, kc * 128:(kc + 1) * 128],
                           qT[qs][:, hh, :], start=True, stop=True, r=['kT', ('qT', qs)], w=[('pss', p % NS)])
                    OP('act', 'activation', out=pT[p % 4],
                       in_=ps[:, :].rearrange("p (a b) -> p a b", a=2)[:, :, 0:NQ], func=AF.Exp,
                       r=[('pss', p % NS)], w=[('pT', p % 4)])

                def PV(p):
                    for u in range(2):
                        kc = 2 * p + u
                        OP('pe', 'matmul', po[:, 0:NQ], vv[:, kc, kvh, :], pT[p % 4][:, u, :], start=(kc == 0),
                           stop=(kc == nkc - 1), r=[('vv', kvh), ('pT', p % 4)], w=[pon])
                        OP('pe', 'matmul', pd[:, 0:NQ], ones_b, pT[p % 4][:, u, :], start=(kc == 0),
                           stop=(kc == nkc - 1), r=['ones_b', ('pT', p % 4)], w=[pdn])

                for p in range(min(ATT_LA, npair)):
                    S(p)
                for p in range(npair):
                    if p + ATT_LA < npair:
                        S(p + ATT_LA)
                    PV(p)
                OP('act', 'activation', out=rd, in_=pd[:, 0:NQ], func=AF.Ln, r=[pdn], w=['rd'])
                OP('act', 'activation', out=rd, in_=rd, func=AF.Exp, scale=-1.0, r=['rd'], w=['rd'])
                OP('dve', 'tensor_tensor', ob[hi % 2], po[:, 0:NQ], rd, ALU.mult, r=[pon, 'rd'], w=[('ob', hi % 2)])
                store(MD[hh, :, q0:q0 + NQ], ob[hi % 2], ('ob', hi % 2))
                hi += 1
        P.barrier()
        A.off = m

    def fourier(stream):
        is_ctx = (stream == 1)
        n = TC if is_ctx else T
        FD = FSc if is_ctx else FS
        MD = MIXc if is_ctx else MIX
        DN = dftnc_in if is_ctx else dftn_in
        ntc = n // 128
        NK = min(FOUR_NK, n)
        m = A.off
        AB = A.alloc([ntc, 4, 256], BF16)
        zT = [A.alloc([n], BF16) for _ in range(2)]
        cn = [A.alloc([ntc, NK], BF16)]
        sn = [A.alloc([ntc, NK], BF16)]
        yb = [A.alloc([NK], BF16) for _ in range(2)]
        CG = 8 if ntc >= 8 else ntc
        ei = 0
        for g in range(4):
            P.dma('sp', zT[g % 2], FD[g], writes=[('zT', g % 2)], key=f"zT{g % 2}")
            for tcp in range(ntc // 2):
                bi = ei % 3
                for u in range(2):
                    tc_ = tcp * 2 + u
                    OP('pe', 'matmul', PA[bi][:, u * 256:(u + 1) * 256], zT[g % 2][:, tc_ * 128:(tc_ + 1) * 128],
                       dftc_b, start=True, stop=True, r=[('zT', g % 2), 'dftc_b'], w=[('pa', bi)])
                src = PA[bi][:, :].rearrange("p (a b) -> p a b", a=2)
                dst = AB[:, tcp * 2:tcp * 2 + 2, g, :]
                if ei % 2 == 0:
                    OP('act', 'activation', out=dst, in_=src, func=AF.Copy, r=[('pa', bi)], w=['AB'])
                else:
                    OP('dve', 'tensor_copy', out=dst, in_=src, r=[('pa', bi)], w=['AB'])
                ei += 1
        yi = 0
        for kt in range(n // NK):
            s = 0
            for c0_ in range(0, ntc, CG):
                c1_ = min(c0_ + CG, ntc)
                cg = c0_ // CG
                if is_ctx:
                    P.dma('pool', cn[s][:, c0_:c1_, :],
                          DN[0, c0_ * 128:c1_ * 128, kt * NK:(kt + 1) * NK].rearrange("(c p) k -> p c k", p=128),
                          writes=[('cn', s, cg)], key=f"cn{s}_{cg}")
                    P.dma('pool', sn[s][:, c0_:c1_, :],
                          DN[1, c0_ * 128:c1_ * 128, kt * NK:(kt + 1) * NK].rearrange("(c p) k -> p c k", p=128),
                          writes=[('sn', s, cg)], key=f"sn{s}_{cg}")
                else:
                    P.dma('pool', cn[s][:, c0_:c1_, :],
                          DN16[0, kt].rearrange("p (c k) -> p c k", c=ntc)[:, c0_:c1_, :],
                          writes=[('cn', s, cg)], key=f"cn{s}_{cg}")
                    P.dma('pool', sn[s][:, c0_:c1_, :],
                          DN16[1, kt].rearrange("p (c k) -> p c k", c=ntc)[:, c0_:c1_, :],
                          writes=[('sn', s, cg)], key=f"sn{s}_{cg}")
            for g in range(4):
                bi = yi % 3
                for tc_ in range(ntc):
                    OP('pe', 'matmul', PA[bi][:, 0:NK], AB[:, tc_, g, 0:128], cn[s][:, tc_, :], start=(tc_ == 0),
                       stop=False, r=['AB', ('cn', s, tc_ // CG)], w=[('pa', bi)])
                    OP('pe', 'matmul', PA[bi][:, 0:NK], AB[:, tc_, g, 128:256], sn[s][:, tc_, :], start=False,
                       stop=(tc_ == ntc - 1), r=['AB', ('sn', s, tc_ // CG)], w=[('pa', bi)])
                if yi % 2 == 0:
                    OP('act', 'activation', out=yb[yi % 2], in_=PA[bi][:, 0:NK], func=AF.Copy,
                       r=[('pa', bi)], w=[('yb', yi % 2)])
                else:
                    OP('dve', 'tensor_copy', out=yb[yi % 2], in_=PA[bi][:, 0:NK], r=[('pa', bi)], w=[('yb', yi % 2)])
                store(MD[8 + g, :, kt * NK:(kt + 1) * NK], yb[yi % 2], ('yb', yi % 2))
                yi += 1
        P.barrier()
        A.off = m

    def phase3(l, stream):
        is_ctx = (stream == 1)
        ntot = TC if is_ctx else T
        N = min(NT, ntot)
        XD = XC if is_ctx else XT
        MD = MIXc if is_ctx else MIX
        m = A.off
        xt = A.alloc([KD, N], F32)
        mt = A.alloc([20, N], BF16)
        wo = [A.alloc([20, 512], BF16) for _ in range(2)]
        ga = VEC[:, stream, l, 2, :]
        wit = 0
        mmi = 0
        for ti in range(ntot // N):
            s0 = ti * N
            load_xt(xt, XD, ntot, s0, N, False)
            P.dma('sp', mt, MD[:, :, s0:s0 + N].rearrange("k p t -> p k t"), writes=['mt'], key='mt')
            for cb in range(4):
                ws = wit % 2
                wit += 1
                P.dma('pool', wo[ws], Wo[l, cb].rearrange("p (k n) -> p k n", k=20),
                      writes=[('wo', ws)], key=f"wo{ws}")
                for mi in range(4):
                    mb = cb * 4 + mi
                    bi = mmi % 3
                    mmi += 1
                    for k in range(20):
                        OP('pe', 'matmul', PA[bi][:, 0:N], wo[ws][:, k, mi * 128:(mi + 1) * 128], mt[:, k, :],
                           start=(k == 0), stop=(k == 19), r=[('wo', ws), 'mt'], w=[('pa', bi)])
                    OP('dve', 'scalar_tensor_tensor', xt[:, mb, :], PA[bi][:, 0:N], ga[:, mb:mb + 1], xt[:, mb, :],
                       ALU.mult, ALU.add, r=[('pa', bi), 'xt', 'VEC'], w=['xt'])
            P.dma('sp', (XCM if is_ctx else XM)[:, :, s0:s0 + N].rearrange("k p t -> p k t"), xt, reads=['xt'], key='xts')
        P.barrier()
        A.off = m

    def phase4(l, stream):
        is_ctx = (stream == 1)
        ntot = TC if is_ctx else T
        N = min(NT, ntot)
        XD = XC if is_ctx else XT
        W = N + 2
        m = A.off
        xt = A.alloc([KD, W], F32)
        h = A.alloc([KD, W], BF16)
        sq = [A.alloc([W], F32) for _ in range(2)]
        tmp = [A.alloc([W], F32) for _ in range(2)]
        rstd = A.alloc([W], F32)
        act = A.alloc([NFF, N], BF16)
        wg = [A.alloc([KD, 256], BF16) for _ in range(2)]
        wu = [A.alloc([KD, 256], BF16) for _ in range(2)]
        wd = [A.alloc([NFF, 256], BF16) for _ in range(2)]
        gb = sq
        cv = [t_[:, 0:N] for t_ in tmp]
        sg = cv
        gvec = VEC[:, stream, l, 3, :]
        svec = VEC[:, stream, l, 4, :]
        ga = VEC[:, stream, l, 5, :]
        wit = 0
        wdi = 0
        ji = 0
        for ti in range(ntot // N):
            s0 = ti * N
            load_xt(xt, XCM if is_ctx else XM, ntot, s0, N, True)
            norm_mod(xt, h, W, gvec, svec, sq, tmp, rstd, s0 == 0, s0 + N == ntot)
            for jb in range(NFF // 2):
                ws = wit % 2
                wit += 1
                P.dma('pool', wg[ws], Wg[l, jb].rearrange("p (k n) -> p k n", k=KD),
                      writes=[('wg', ws)], key=f"wg{ws}")
                P.dma('pool', wu[ws], Wu[l, jb].rearrange("p (k n) -> p k n", k=KD),
                      writes=[('wu', ws)], key=f"wu{ws}")
                for j2 in range(2):
                    j = jb * 2 + j2
                    s = ji % 2
                    ji += 1
                    pg = PA[0] if s == 0 else PA[1]
                    pgn = ('pa', 0 if s == 0 else 1)
                    pu = PST if s == 0 else PST2
                    pun = 'pst' if s == 0 else 'pst2'
                    hbk, hbn = HB[s]
                    for kc in range(KD):
                        OP('pe', 'matmul', pg[:, 0:N], wg[ws][:, kc, j2 * 128:(j2 + 1) * 128], h[:, kc, 1:1 + N],
                           start=(kc == 0), stop=(kc == KD - 1), r=[('wg', ws), ('h', kc)], w=[pgn])
                        OP('pe', 'matmul', hbk[:, 0:2], wg[ws][:, kc, j2 * 128:(j2 + 1) * 128],
                           h[:, kc, 0:W:W - 1], start=(kc == 0), stop=(kc == KD - 1),
                           r=[('wg', ws), ('h', kc)], w=[hbn])
                    for kc in range(KD):
                        OP('pe', 'matmul', pu[:, 0:N], wu[ws][:, kc, j2 * 128:(j2 + 1) * 128], h[:, kc, 1:1 + N],
                           start=(kc == 0), stop=(kc == KD - 1), r=[('wu', ws), ('h', kc)], w=[pun])
                    OP('act', 'activation', out=gb[s][:, 1:1 + N], in_=pg[:, 0:N], func=AF.Copy, r=[pgn], w=[('sq', s)])
                    OP('act', 'activation', out=gb[s][:, 0:W:W - 1], in_=hbk[:, 0:2], func=AF.Copy,
                       r=[hbn, ('sq', s)], w=[('sq', s)])
                    OP('dve', 'tensor_scalar', cv[s], gb[s][:, 0:N], fcw[:, l, 0, j:j + 1], None, ALU.mult,
                       r=[('sq', s), 'fcw'], w=[('tmp', s)])
                    OP('dve', 'scalar_tensor_tensor', cv[s], gb[s][:, 1:1 + N], fcw[:, l, 1, j:j + 1], cv[s],
                       ALU.mult, ALU.add, r=[('sq', s), ('tmp', s), 'fcw'], w=[('tmp', s)])
                    OP('dve', 'scalar_tensor_tensor', cv[s], gb[s][:, 2:2 + N], fcw[:, l, 2, j:j + 1], cv[s],
                       ALU.mult, ALU.add, r=[('sq', s), ('tmp', s), 'fcw'], w=[('tmp', s)])
                    OP('act', 'activation', out=sg[s], in_=cv[s], func=AF.Silu, bias=fcb[:, l, j:j + 1], scale=1.0,
                       r=[('tmp', s), 'fcb'], w=[('tmp', s)])
                    OP('dve', 'tensor_tensor', act[:, j, :], sg[s], pu[:, 0:N], ALU.mult,
                       r=[('tmp', s), pun], w=[('act', j)])
            for cb in range(8):
                ws = wdi % 2
                wdi += 1
                P.dma('pool', wd[ws], Wd[l, cb].rearrange("p (k n) -> p k n", k=NFF),
                      writes=[('wd', ws)], key=f"wd{ws}")
                for mi in range(2):
                    mb = cb * 2 + mi
                    bank = PROT if mb % 2 == 0 else PA[2]
                    bn = 'prot' if mb % 2 == 0 else ('pa', 2)
                    for j in range(NFF):
                        OP('pe', 'matmul', bank[:, 0:N], wd[ws][:, j, mi * 128:(mi + 1) * 128], act[:, j, :],
                           start=(j == 0), stop=(j == NFF - 1), r=[('wd', ws), ('act', j)], w=[bn])
                    OP('dve', 'scalar_tensor_tensor', xt[:, mb, 1:1 + N], bank[:, 0:N], ga[:, mb:mb + 1],
                       xt[:, mb, 1:1 + N], ALU.mult, ALU.add, r=[bn, 'xt', 'VEC'], w=['xt'])
            P.dma('sp', XD[:, :, s0:s0 + N].rearrange("k p t -> p k t"), xt[:, :, 1:1 + N], reads=['xt'], key='xts')
        P.barrier()
        A.off = m

    def final_norm():
        N = NT
        m = A.off
        xt = A.alloc([KD, N], F32)
        sq = [A.alloc([N], BF16) for _ in range(2)]
        rstd = A.alloc([N], F32)
        y = A.alloc([KD, N], F32)
        ot = [A.alloc([D], F32) for _ in range(2)]
        fins = []
        oi = 0
        for ti in range(T // N):
            s0 = ti * N
            load_xt(xt, XT, T, s0, N, False)
            for kc in range(KD):
                s = kc % 2
                OP('act', 'activation', out=sq[s], in_=xt[:, kc, :], func=AF.Square, r=['xt'], w=[('sq', s)])
                OP('pe', 'matmul', PST[:, 0:N], ones_b, sq[s], start=(kc == 0), stop=(kc == KD - 1),
                   r=[('sq', s), 'ones_b'], w=['pst'])
            OP('act', 'activation', out=rstd, in_=PST[:, 0:N], func=AF.Ln, bias=eps_ap, scale=1.0 / D,
               r=['pst', 'cst'], w=['rstd'])
            OP('act', 'activation', out=rstd, in_=rstd, func=AF.Exp, scale=-0.5, r=['rstd'], w=['rstd'])
            for kc in range(KD):
                OP('dve', 'scalar_tensor_tensor', y[:, kc, :], xt[:, kc, :], fng[:, kc:kc + 1], rstd, ALU.mult,
                   ALU.mult, r=['xt', 'rstd', 'fng'], w=[('y', kc)])
            for tb in range(N // 128):
                o = oi % 2
                oi += 1
                for q4 in range(4):
                    bi = q4 % 3
                    for j in range(4):
                        kc = q4 * 4 + j
                        OP('pe', 'transpose', PA[bi][:, j * 128:(j + 1) * 128], y[:, kc, tb * 128:(tb + 1) * 128],
                           ident_f, r=[('y', kc), 'ident_f'], w=[('pa', bi)])
                    if q4 % 2 == 0:
                        OP('act', 'activation', out=ot[o][:, q4 * 512:(q4 + 1) * 512], in_=PA[bi][:, :], func=AF.Copy,
                           r=[('pa', bi)], w=[('ot', o)])
                    else:
                        OP('dve', 'tensor_copy', out=ot[o][:, q4 * 512:(q4 + 1) * 512], in_=PA[bi][:, :],
                           r=[('pa', bi)], w=[('ot', o)])
                fins.append(P.dma('sp', out[s0 + tb * 128:s0 + (tb + 1) * 128, :], ot[o], reads=[('ot', o)],
                                  key=f"out{o}"))
        A.off = m
        return fins

    steps = []
    for l in range(L):
        lastl = (l == L - 1)
        steps.append(('p1c', lambda l=l, lastl=lastl: phase1(l, 1, lastl)))
        steps.append(('p1x', lambda l=l: phase1(l, 0, False)))
        if not lastl:
            steps.append(('atc', lambda: attention(1)))
        if not lastl:
            steps.append(('cast', lambda l=l: cast_layer(l + 1)))
        steps.append(('atx', lambda: attention(0)))
        if not lastl:
            steps.append(('foc', lambda: fourier(1)))
        steps.append(('fox', lambda: fourier(0)))
        if not lastl:
            steps.append(('p3c', lambda l=l: phase3(l, 1)))
        steps.append(('p3x', lambda l=l: phase3(l, 0)))
        if not lastl:
            steps.append(('p4c', lambda l=l: phase4(l, 1)))
        steps.append(('p4x', lambda l=l: phase4(l, 0)))
    nsteps = len(steps) if stop_after is None else stop_after
    P.marks = [('prologue', 0)]
    for name, fn in steps[:nsteps]:
        P.marks.append((name, sum(1 for o in P.streams['pe'] if o.fn is not None)))
        fn()
    P.marks.append(('final', sum(1 for o in P.streams['pe'] if o.fn is not None)))
    fins = final_norm()
    P.emit(final_waits=fins)
    return nc, P


def _fm(v, n):
    v = np.asarray(v, np.float32)
    lead = v.shape[:-1]
    v = v.reshape(*lead, n, 128)
    nd = v.ndim
    return np.ascontiguousarray(np.moveaxis(v, -1, 0))


def _constants(T):
    ident = np.eye(128, dtype=np.float32)
    rrot = np.zeros((128, 128), np.float32)
    for base in (0, 64):
        for i in range(32):
            rrot[base + 32 + i, base + i] = -1.0
            rrot[base + i, base + 32 + i] = 1.0
    t = np.arange(T)
    row = (t // 64).astype(np.float64)
    col = (t % 64).astype(np.float64)
    freqs = 10000.0 ** (-np.arange(0, 64, 2, dtype=np.float32).astype(np.float64) / 64)
    ang_r = (row[:, None].astype(np.float32) * freqs[None, :].astype(np.float32)).astype(np.float32)
    ang_c = (col[:, None].astype(np.float32) * freqs[None, :].astype(np.float32)).astype(np.float32)
    cr, sr, cc, sc = np.cos(ang_r), np.sin(ang_r), np.cos(ang_c), np.sin(ang_c)
    ropec = np.concatenate([cr, cr, cc, cc], axis=1).T.astype(np.float32)
    ropes = np.concatenate([sr, sr, sc, sc], axis=1).T.astype(np.float32)
    j = np.arange(128)
    angc = 2 * np.pi * ((j[:, None] * j[None, :]) % 128) / 128
    dftc = (np.concatenate([np.cos(angc), np.sin(angc)], axis=1) / np.sqrt(128.0)).astype(np.float32)

    def dn(n):
        k = np.arange(n, dtype=np.int64)
        ang = 2 * np.pi * ((k[:, None] * k[None, :]) % n).astype(np.float64) / n
        o = np.empty((2, n, n), np.float32)
        o[0] = np.cos(ang) / np.sqrt(n)
        o[1] = -np.sin(ang) / np.sqrt(n)
        return o
    return dict(ident=ident, rrot=rrot, ropec=np.ascontiguousarray(ropec), ropes=np.ascontiguousarray(ropes),
                dftc=dftc, dftn=dn(T), dftnc=dn(TC))


def make_in_maps(inp, T, L, ncores):
    f = lambda k: np.asarray(inp[k], np.float32)
    shared = dict(
        w_mod=np.ascontiguousarray(f('w_mod')[:L]),
        b_mod_t=np.ascontiguousarray(f('b_mod')[:L].reshape(L, 96, 128).transpose(2, 0, 1)),
        n1g=np.ascontiguousarray(f('norm1_g')[:L].reshape(L, KD, 128).transpose(2, 0, 1)),
        n2g=np.ascontiguousarray(f('norm2_g')[:L].reshape(L, KD, 128).transpose(2, 0, 1)),
        fng=np.ascontiguousarray(f('final_norm_g').reshape(KD, 128).T),
        w_in=np.ascontiguousarray(f('w_in')[:L]),
        qg=np.ascontiguousarray(f('q_norm_g')[:L].T),
        kg=np.ascontiguousarray(f('k_norm_g')[:L].T),
        convw=np.ascontiguousarray(f('conv_w')[:L].reshape(L, 3, 4, 128).transpose(3, 0, 1, 2)),
        lng=np.ascontiguousarray(f('gm_ln_g')[:L].reshape(L, 4, 128).transpose(2, 0, 1)),
        lnb=np.ascontiguousarray(f('gm_ln_b')[:L].reshape(L, 4, 128).transpose(2, 0, 1)),
        wsT=np.ascontiguousarray(f('gm_ws')[:L].transpose(3, 0, 1, 2)),
        gmb=np.ascontiguousarray(np.broadcast_to(f('gm_b')[:L][None], (128, L, 4, 128))),
        w_out=np.ascontiguousarray(f('w_out')[:L]),
        w_up=np.ascontiguousarray(f('w_up')[:L]),
        w_down=np.ascontiguousarray(f('w_down')[:L]),
        fcw=np.ascontiguousarray(f('ffn_conv_w')[:L].reshape(L, 3, NFF, 128).transpose(3, 0, 1, 2)),
        fcb=np.ascontiguousarray(f('ffn_conv_b')[:L].reshape(L, NFF, 128).transpose(2, 0, 1)),
    )
    shared.update(_constants(T))
    x = f('x')
    ctx = f('ctx')
    c = f('c')
    cc = f('c_ctx')
    maps = []
    for b in range(ncores):
        mp = dict(shared)
        mp['x'] = np.ascontiguousarray(x[b, :T])
        mp['ctx'] = np.ascontiguousarray(ctx[b])
        cv = np.stack([c[b].reshape(KD, 128).T, cc.reshape(KD, 128).T], axis=1)
        mp['cvec'] = np.ascontiguousarray(cv)
        maps.append(mp)
    return maps


def kernel(**inputs):
    T = 4096
    L = 4
    nc, _ = build_program(T, L)
    maps = make_in_maps(inputs, T, L, NCORES)
    res = run_bass_kernel_spmd(nc, maps, core_ids=list(range(NCORES)))
    return np.stack([np.asarray(res.results[b]["out"], np.float32) for b in range(NCORES)], axis=0)
```

```python
from contextlib import ExitStack
import math
import numpy as np
import concourse.bass as bass
import concourse.mybir as mybir
from concourse.bass_utils import run_bass_kernel_spmd

F32 = mybir.dt.float32
BF16 = mybir.dt.bfloat16
AF = mybir.ActivationFunctionType
ALU = mybir.AluOpType

EPOCH = 30000
DMA_EPOCH = 1800
ENGS = ('sp', 'act', 'pe', 'dve', 'pool')

D = 2048
KD = 16
INW = 4608
MIXW = 2560
DFF = 5632
NFF = 44
EPS = 1e-6
TC = 256
NCORES = 4
ATT_LA = 1
FOUR_NK = 512


class Op:
    __slots__ = ('eng', 'fn', 'deps', 'need_inc', 'pos', 'key', 'dn', 'is_dma')

    def __init__(self, eng, fn, is_dma=False, key=None):
        self.eng = eng
        self.fn = fn
        self.deps = ()
        self.need_inc = False
        self.pos = -1
        self.key = key
        self.dn = -1
        self.is_dma = is_dma


class Prog:
    def __init__(self, nc):
        self.nc = nc
        self.es = ExitStack()
        self.streams = {e: [] for e in ENGS}
        self.res = {}
        self.dma_count = {}
        self.dma_since = {}
        self.last_compute = {}
        self.n_ops = 0

    def sbuf(self, name, shape, dt):
        return self.es.enter_context(self.nc.sbuf_tensor(name, list(shape), dt))

    def psum(self, name, shape, dt):
        return self.es.enter_context(self.nc.psum_tensor(name, list(shape), dt))

    def _track(self, o, reads, writes):
        deps = {}
        res = self.res
        for r in reads:
            st = res.get(r)
            if st is not None and st[0] is not None:
                deps[id(st[0])] = st[0]
        for w in writes:
            st = res.get(w)
            if st is not None:
                if st[0] is not None:
                    deps[id(st[0])] = st[0]
                for d in st[1].values():
                    deps[id(d)] = d
                for d in st[2]:
                    deps[id(d)] = d
        for r in reads:
            st = res.get(r)
            if st is None:
                st = res[r] = [None, {}, []]
            if o.is_dma:
                st[2].append(o)
            else:
                st[1][o.eng] = o
        for w in writes:
            res[w] = [o, {}, []]
        dl = []
        for d in deps.values():
            if d is o:
                continue
            if (not d.is_dma) and d.eng == 'pe' and o.eng == 'pe' and not o.is_dma:
                continue
            d.need_inc = True
            dl.append(d)
        o.deps = dl

    def add(self, eng, fn, reads=(), writes=()):
        o = Op(eng, fn)
        self._track(o, reads, writes)
        self.streams[eng].append(o)
        self.last_compute[eng] = o
        self.n_ops += 1
        return o

    def dma(self, eng, out, in_, reads=(), writes=(), key=None, fn=None):
        assert key is not None
        if fn is None:
            fn = lambda e: e.dma_start(out=out, in_=in_)
        o = Op(eng, fn, is_dma=True, key=key)
        n = self.dma_count.get(key, 0)
        o.dn = n
        self.dma_count[key] = n + 1
        self._track(o, reads, writes)
        self.streams[eng].append(o)
        self.dma_since[key] = o
        self.n_ops += 1
        return o

    def barrier(self):
        deps = list(self.last_compute.values()) + list(self.dma_since.values())
        for d in deps:
            d.need_inc = True
        for e in ENGS:
            o = Op(e, None)
            o.deps = [d for d in deps if d.is_dma or d.eng != e or e != 'pe']
            self.streams[e].append(o)
        self.dma_since = {}
        self.res = {}

    def emit(self, final_waits=()):
        nc = self.nc
        es = self.es
        npos = {}
        for e in ENGS:
            p = 0
            for o in self.streams[e]:
                if (not o.is_dma) and o.need_inc:
                    o.pos = p
                    p += 1
            npos[e] = p
        eng_sems = {}
        nsem = 0
        for e in ENGS:
            k = max(1, (npos[e] + EPOCH - 1) // EPOCH)
            eng_sems[e] = [es.enter_context(nc.semaphore(f"se_{e}_{i}")) for i in range(k)]
            nsem += k
        dma_sems = {}
        for key, cnt in self.dma_count.items():
            k = max(1, (cnt + DMA_EPOCH - 1) // DMA_EPOCH)
            dma_sems[key] = [es.enter_context(nc.semaphore(f"sd_{len(dma_sems)}_{i}")) for i in range(k)]
            nsem += k
        self.nsem = nsem
        block = es.enter_context(nc.Block())
        streams = self.streams
        final_waits = list(final_waits)

        def make_body(e):
            def body(eng):
                known = {x: -1 for x in ENGS}
                known_d = {}

                def wait_for(d):
                    if d.is_dma:
                        ep = d.dn // DMA_EPOCH
                        kk = (d.key, ep)
                        v = (d.dn % DMA_EPOCH + 1) * 16
                        if known_d.get(kk, 0) >= v:
                            return
                        known_d[kk] = v
                        eng.wait_ge(dma_sems[d.key][ep], v)
                    else:
                        if known[d.eng] >= d.pos:
                            return
                        known[d.eng] = d.pos
                        ep = d.pos // EPOCH
                        eng.wait_ge(eng_sems[d.eng][ep], d.pos % EPOCH + 1)

                for o in streams[e]:
                    for d in o.deps:
                        wait_for(d)
                    if o.fn is None:
                        continue
                    ins = o.fn(eng)
                    if o.is_dma:
                        ep = o.dn // DMA_EPOCH
                        ins.then_inc(dma_sems[o.key][ep], 16)
                    elif o.need_inc:
                        ep = o.pos // EPOCH
                        ins.then_inc(eng_sems[e][ep], 1)
                if e == 'sp':
                    for d in final_waits:
                        wait_for(d)
            return body

        block.sync(make_body('sp'))
        block.scalar(make_body('act'))
        block.tensor(make_body('pe'))
        block.vector(make_body('dve'))
        block.gpsimd(make_body('pool'))
        es.close()


class Arena:
    def __init__(self, P, nbytes):
        self.t = P.sbuf("arena", [128, nbytes // 4], F32)
        self.off = 0
        self.size = nbytes

    def alloc(self, shape, dt):
        n = 1
        for s in shape:
            n *= s
        nb = n * (4 if dt == F32 else 2)
        nb = (nb + 63) // 64 * 64
        o = self.off
        self.off += nb
        assert self.off <= self.size, ("arena overflow", self.off, self.size)
        v = self.t[:, o // 4:(o + nb) // 4]
        if dt == BF16:
            v = v.bitcast(BF16)
        v = v[:, 0:n]
        if len(shape) == 2:
            v = v.rearrange("p (a b) -> p a b", a=shape[0])
        elif len(shape) == 3:
            v = v.rearrange("p (a b c) -> p a b c", a=shape[0], b=shape[1])
        elif len(shape) == 4:
            v = v.rearrange("p (a b c d) -> p a b c d", a=shape[0], b=shape[1], c=shape[2])
        return v


def build_program(T, L, debug=False, stop_after=None):
    nc = bass.Bass("TRN2", target_bir_lowering=False)
    TK = TC + T
    NT = 512

    def din(name, shape, dt=F32):
        return nc.dram_tensor(name, list(shape), dt, kind="ExternalInput").ap()

    def dscr(name, shape, dt):
        return nc.dram_tensor(name, list(shape), dt, kind="ExternalOutput" if debug else "Internal").ap()

    x_in = din("x", [T, D])
    ctx_in = din("ctx", [TC, D])
    cvec_in = din("cvec", [128, 2, KD])
    w_mod = din("w_mod", [L, D, 6 * D])
    b_mod_t = din("b_mod_t", [128, L, 96])
    n1g_in = din("n1g", [128, L, KD])
    n2g_in = din("n2g", [128, L, KD])
    fng_in = din("fng", [128, KD])
    w_in = din("w_in", [L, D, INW])
    qg_in = din("qg", [128, L])
    kg_in = din("kg", [128, L])
    convw_in = din("convw", [128, L, 3, 4])
    lng_in = din("lng", [128, L, 4])
    lnb_in = din("lnb", [128, L, 4])
    wsT_in = din("wsT", [128, L, 4, 128])
    gmb_in = din("gmb", [128, L, 4, 128])
    w_out = din("w_out", [L, MIXW, D])
    w_up = din("w_up", [L, D, 2 * DFF])
    fcw_in = din("fcw", [128, L, 3, NFF])
    fcb_in = din("fcb", [128, L, NFF])
    w_down = din("w_down", [L, DFF, D])
    ident_in = din("ident", [128, 128])
    rrot_in = din("rrot", [128, 128])
    ropec_in = din("ropec", [128, T])
    ropes_in = din("ropes", [128, T])
    dftc_in = din("dftc", [128, 256])
    dftn_in = din("dftn", [2, T, T])
    dftnc_in = din("dftnc", [2, TC, TC])
    out = nc.dram_tensor("out", [T, D], F32, kind="ExternalOutput").ap()

    XT = dscr("XT", [KD, 128, T], F32)
    XC = dscr("XC", [KD, 128, TC], F32)
    XM = dscr("XM", [KD, 128, T], F32)
    XCM = dscr("XCM", [KD, 128, TC], F32)
    QS = dscr("QS", [8, 128, T], BF16)
    QSc = dscr("QSc", [8, 128, TC], BF16)
    KS = dscr("KS", [2, 128, TK], BF16)
    VS = dscr("VS", [2, TK, 128], BF16)
    FS = dscr("FS", [4, 128, T], BF16)
    FSc = dscr("FSc", [4, 128, TC], BF16)
    MIX = dscr("MIX", [20, 128, T], BF16)
    MIXc = dscr("MIXc", [20, 128, TC], BF16)

    Wi = nc.dram_tensor("Wi", [L, 9, 128, KD * 512], BF16, kind="Internal").ap()
    Wo = nc.dram_tensor("Wo", [L, 4, 128, 20 * 512], BF16, kind="Internal").ap()
    Wg = nc.dram_tensor("Wg", [L, 22, 128, KD * 256], BF16, kind="Internal").ap()
    Wu = nc.dram_tensor("Wu", [L, 22, 128, KD * 256], BF16, kind="Internal").ap()
    Wd = nc.dram_tensor("Wd", [L, 8, 128, NFF * 256], BF16, kind="Internal").ap()

    NKF = FOUR_NK
    DN16 = nc.dram_tensor("DN16", [2, T // NKF, 128, (T // 128) * NKF], BF16, kind="Internal").ap()

    P = Prog(nc)
    A = Arena(P, 200 * 1024)

    def cast_dft():
        ntc_ = T // 128
        for mtx in range(2):
            for kt in range(T // NKF):
                for c0_ in range(0, ntc_, 8):
                    c1_ = min(c0_ + 8, ntc_)
                    P.dma('pool', DN16[mtx, kt].rearrange("p (c k) -> p c k", c=ntc_)[:, c0_:c1_, :],
                          dftn_in[mtx, c0_ * 128:c1_ * 128, kt * NKF:(kt + 1) * NKF].rearrange("(c p) k -> p c k", p=128),
                          key='cw')

    def cast_layer(l):
        def c(dst, src, k):
            P.dma('pool', dst.rearrange("p (k n) -> p k n", k=k), src.rearrange("(k p) n -> p k n", p=128),
                  key='cw')
        for cb in range(9):
            c(Wi[l, cb], w_in[l, :, cb * 512:(cb + 1) * 512], KD)
        for cb in range(4):
            c(Wo[l, cb], w_out[l, :, cb * 512:(cb + 1) * 512], 20)
        for jb in range(22):
            c(Wg[l, jb], w_up[l, :, jb * 256:(jb + 1) * 256], KD)
            c(Wu[l, jb], w_up[l, :, DFF + jb * 256:DFF + (jb + 1) * 256], KD)
        for cb in range(8):
            c(Wd[l, cb], w_down[l, :, cb * 256:(cb + 1) * 256], NFF)

    def OP(eng, meth, *args, r=(), w=(), **kw):
        return P.add(eng, lambda e: getattr(e, meth)(*args, **kw), reads=r, writes=w)

    PP = [P.psum(f"pp{i}", [128, 1024], F32) for i in range(4)]
    BK = [PP[i // 2][:, (i % 2) * 512:(i % 2 + 1) * 512] for i in range(8)]
    PA = [BK[0], BK[1], BK[2]]
    PST = BK[3]
    PST2 = BK[4]
    PROT = BK[5]
    PSM = BK[6]
    PTRF = BK[7]
    PTR = PTRF.bitcast(BF16)
    HB = [(PSM, 'psm'), (PTRF, 'ptr')]

    ident_f = A.alloc([128], F32)
    ident_b = A.alloc([128], BF16)
    ones_f = A.alloc([128], F32)
    ones_b = A.alloc([128], BF16)
    rrot_b = A.alloc([128], BF16)
    dftc_b = A.alloc([256], BF16)
    cst = A.alloc([4], F32)
    MOD = A.alloc([L, 96, 2], F32)
    VEC = A.alloc([2, L, 6, KD], F32)
    n1g = A.alloc([L, KD], F32)
    n2g = A.alloc([L, KD], F32)
    fng = A.alloc([KD], F32)
    qg = A.alloc([L], F32)
    kg = A.alloc([L], F32)
    convw = A.alloc([L, 3, 4], F32)
    lng = A.alloc([L, 4], F32)
    lnb = A.alloc([L, 4], F32)
    fcw = A.alloc([L, 3, NFF], F32)
    fcb = A.alloc([L, NFF], F32)
    wsT = A.alloc([4, 128], BF16)
    gmb = A.alloc([4, 128], F32)
    base_mark = A.off

    def ld(dst, src, name, key='cst0'):
        return P.dma('sp', dst, src, writes=[name], key=key)

    def ldc(dst, src, name, key='cst1'):
        return P.dma('pool', dst, src, writes=[name], key=key)

    cast_layer(0)
    cast_dft()
    ld(ident_f, ident_in, 'ident_f')
    ldc(ident_b, ident_in, 'ident_b')
    ldc(rrot_b, rrot_in, 'rrot_b')
    ldc(dftc_b, dftc_in, 'dftc_b')
    ld(n1g, n1g_in, 'n1g')
    ld(n2g, n2g_in, 'n2g')
    ld(fng, fng_in, 'fng')
    ld(qg, qg_in, 'qg', key='qgl')
    ld(kg, kg_in, 'kg')
    ld(convw, convw_in, 'convw')
    ld(lng, lng_in, 'lng')
    ld(lnb, lnb_in, 'lnb')
    ld(fcw, fcw_in, 'fcw')
    ld(fcb, fcb_in, 'fcb')
    OP('dve', 'memset', ones_f, 1.0, w=['ones_f'])
    OP('dve', 'memset', ones_b, 1.0, w=['ones_b'])
    OP('dve', 'memset', cst, 0.0, w=['cst'])
    OP('dve', 'memset', cst[:, 0:1], EPS, r=['cst'], w=['cst'])
    OP('dve', 'tensor_scalar', qg, qg, 128.0 ** -0.5, None, ALU.mult, r=['qg'], w=['qg'])
    eps_ap = cst[:, 0:1]
    P.barrier()

    m0 = A.off
    scv = A.alloc([2, KD], F32)
    bmt = A.alloc([L, 96], F32)
    wm = [A.alloc([KD, 512], F32) for _ in range(2)]
    ld(scv, cvec_in, 'scv', key='scv')
    ld(bmt, b_mod_t, 'bmt', key='bmt')
    OP('act', 'activation', out=scv, in_=scv, func=AF.Silu, r=['scv'], w=['scv'])
    it = 0
    for l in range(L):
        for cb in range(24):
            s = it % 2
            P.dma('sp', wm[s], w_mod[l, :, cb * 512:(cb + 1) * 512].rearrange("(k p) n -> p k n", p=128),
                  writes=[('wm', s)], key=f"wm{s}")
            for mi in range(4):
                for kc in range(KD):
                    OP('pe', 'matmul', PSM[:, 2 * mi:2 * mi + 2], wm[s][:, kc, mi * 128:(mi + 1) * 128],
                       scv[:, :, kc], start=(kc == 0), stop=(kc == KD - 1),
                       r=[('wm', s), 'scv'], w=['psm'])
            OP('dve', 'tensor_tensor', MOD[:, l, cb * 4:cb * 4 + 4, :],
               PSM[:, 0:8].rearrange("p (a b) -> p a b", a=4),
               bmt[:, l, cb * 4:cb * 4 + 4].unsqueeze(2).broadcast_to([128, 4, 2]), ALU.add,
               r=['psm', 'bmt'], w=['MOD'])
            it += 1
    for st in range(2):
        for l in range(L):
            for (dst, src, gain) in ((0, 16, n1g), (3, 64, n2g)):
                OP('dve', 'tensor_scalar', VEC[:, st, l, dst, :], MOD[:, l, src:src + 16, st], 1.0, None, ALU.add,
                   r=['MOD'], w=['VEC'])
                OP('dve', 'tensor_tensor', VEC[:, st, l, dst, :], VEC[:, st, l, dst, :], gain[:, l, :], ALU.mult,
                   r=['VEC', 'n1g', 'n2g'], w=['VEC'])
            for (dst, src) in ((1, 0), (2, 32), (4, 48), (5, 80)):
                OP('dve', 'tensor_copy', out=VEC[:, st, l, dst, :], in_=MOD[:, l, src:src + 16, st],
                   r=['MOD'], w=['VEC'])
    P.barrier()
    A.off = m0

    def to_feature_major(src, dst, ntok):
        m = A.off
        xin = [A.alloc([D], F32) for _ in range(2)]
        stg = [A.alloc([KD, 128], F32) for _ in range(2)]
        for tb in range(ntok // 128):
            s = tb % 2
            P.dma('sp', xin[s], src[tb * 128:(tb + 1) * 128, :], writes=[('xin', s)], key=f"xin{s}")
            for q4 in range(4):
                bank = PA[q4 % 3]
                bn = ('pa', q4 % 3)
                for j in range(4):
                    kc = q4 * 4 + j
                    OP('pe', 'transpose', bank[:, j * 128:(j + 1) * 128], xin[s][:, kc * 128:(kc + 1) * 128], ident_f,
                       r=[('xin', s), 'ident_f'], w=[bn])
                src4 = bank[:, :].rearrange("p (a b) -> p a b", a=4)
                if q4 % 2 == 0:
                    OP('act', 'activation', out=stg[s][:, q4 * 4:q4 * 4 + 4, :], in_=src4, func=AF.Copy,
                       r=[bn], w=[('stg', s)])
                else:
                    OP('dve', 'tensor_copy', out=stg[s][:, q4 * 4:q4 * 4 + 4, :], in_=src4, r=[bn], w=[('stg', s)])
            P.dma('sp', dst[:, :, tb * 128:(tb + 1) * 128].rearrange("k p t -> p k t"), stg[s],
                  reads=[('stg', s)], key=f"s_stg{s}")
        P.barrier()
        A.off = m

    to_feature_major(x_in, XT, T)
    to_feature_major(ctx_in, XC, TC)

    def load_xt(xt, XD, ntot, s0, N, halo):
        if halo:
            lo = max(s0 - 1, 0)
            hi = min(s0 + N + 1, ntot)
            c0 = lo - (s0 - 1)
            if s0 == 0:
                OP('dve', 'memset', xt[:, :, 0:1], 0.0, w=['xt'])
            if s0 + N == ntot:
                OP('dve', 'memset', xt[:, :, N + 1:N + 2], 0.0, w=['xt'])
            if hi - lo == N + 2:
                P.dma('sp', xt[:, :, 0:N + 1], XD[:, :, lo:hi - 1].rearrange("k p t -> p k t"),
                      writes=['xt'], key='xt')
                P.dma('sp', xt[:, :, N:N + 2], XD[:, :, hi - 2:hi].rearrange("k p t -> p k t"),
                      reads=['xt'], writes=['xt'], key='xt')
            else:
                P.dma('sp', xt[:, :, c0:c0 + (hi - lo)], XD[:, :, lo:hi].rearrange("k p t -> p k t"),
                      writes=['xt'], key='xt')
        else:
            P.dma('sp', xt[:, :, 0:N], XD[:, :, s0:s0 + N].rearrange("k p t -> p k t"), writes=['xt'], key='xt')

    def norm_mod(xt, h, W, gvec, svec, sq, tmp, rstd, first, last):
        W0 = min(W, 512)
        sqh = [q_.bitcast(BF16) for q_ in sq]
        for kc in range(KD):
            s = kc % 2
            OP('act', 'activation', out=sqh[s][:, 0:W], in_=xt[:, kc, 0:W], func=AF.Square,
               r=['xt'], w=[('sq', s)])
            OP('pe', 'matmul', PST[:, 0:W0], ones_b, sqh[s][:, 0:W0], start=(kc == 0), stop=(kc == KD - 1),
               r=[('sq', s), 'ones_b'], w=['pst'])
            if W > 512:
                OP('pe', 'matmul', PROT[:, 0:W - 512], ones_b, sqh[s][:, 512:W], start=(kc == 0),
                   stop=(kc == KD - 1), r=[('sq', s), 'ones_b'], w=['prot'])
        OP('act', 'activation', out=rstd[:, 0:W0], in_=PST[:, 0:W0], func=AF.Ln, bias=eps_ap, scale=1.0 / D,
           r=['pst', 'cst'], w=['rstd'])
        if W > 512:
            OP('act', 'activation', out=rstd[:, 512:W], in_=PROT[:, 0:W - 512], func=AF.Ln, bias=eps_ap,
               scale=1.0 / D, r=['prot', 'cst'], w=['rstd'])
        OP('act', 'activation', out=rstd[:, 0:W], in_=rstd[:, 0:W], func=AF.Exp, scale=-0.5, r=['rstd'], w=['rstd'])
        for kc in range(KD):
            s = kc % 2
            OP('dve', 'tensor_tensor', tmp[s][:, 0:W], xt[:, kc, 0:W], rstd[:, 0:W], ALU.mult,
               r=['xt', 'rstd'], w=[('tmp', s)])
            OP('act', 'activation', out=h[:, kc, 0:W], in_=tmp[s][:, 0:W], func=AF.Identity,
               bias=svec[:, kc:kc + 1], scale=gvec[:, kc:kc + 1], r=[('tmp', s), 'VEC'], w=[('h', kc)])
        if first:
            OP('dve', 'memset', h[:, :, 0:1], 0.0, r=[('h', k) for k in range(KD)], w=[('h', k) for k in range(KD)])
        if last:
            OP('dve', 'memset', h[:, :, W - 1:W], 0.0, r=[('h', k) for k in range(KD)],
               w=[('h', k) for k in range(KD)])

    def store(dst, src, res):
        key = "s_" + (res if isinstance(res, str) else f"{res[0]}{res[1]}")
        return P.dma('sp', dst, src, reads=[res], key=key)

    def phase1(l, stream, kv_only):
        is_ctx = (stream == 1)
        ntot = TC if is_ctx else T
        N = min(NT, ntot)
        XD = XC if is_ctx else XT
        QD = QSc if is_ctx else QS
        FD = FSc if is_ctx else FS
        MD = MIXc if is_ctx else MIX
        koff = 0 if is_ctx else TC
        W = N + 2
        nch = N // 128
        m = A.off
        xt = A.alloc([KD, W], F32)
        h = A.alloc([KD, W], BF16)
        sq = [A.alloc([W], F32) for _ in range(2)]
        tmp = [A.alloc([W], F32) for _ in range(2)]
        rstd = A.alloc([W], F32)
        wb = [A.alloc([KD, 512], BF16) for _ in range(2)]
        qf = A.alloc([N], F32)
        sqb = A.alloc([N], F32)
        sqbh = sqb.bitcast(BF16)[:, 0:N]
        rq = A.alloc([N], F32)
        qnb = A.alloc([N], BF16)
        t1 = A.alloc([N], F32)
        t2 = A.alloc([N], F32)
        qo = [A.alloc([N], BF16) for _ in range(2)]
        vb = A.alloc([N], BF16)
        vt = [A.alloc([4, 128], BF16) for _ in range(2)]
        fb = [A.alloc([N], BF16) for _ in range(2)]
        cbk = A.alloc([4, N], F32)
        ccb = A.alloc([4, W], F32)
        prod = A.alloc([W], F32)
        cv = A.alloc([N], F32)
        mixb = [A.alloc([N], BF16) for _ in range(2)]
        ub = A.alloc([4, N], F32)
        gvb = A.alloc([4, N], BF16)
        mn = A.alloc([N], F32)
        msq = A.alloc([N], F32)
        lrs = A.alloc([N], F32)
        vh = A.alloc([N], BF16)
        vT = A.alloc([4, 128], BF16)
        rc = A.alloc([N], F32)
        rs = A.alloc([N], F32)
        gvec = VEC[:, stream, l, 0, :]
        svec = VEC[:, stream, l, 1, :]
        if not kv_only:
            ldc(wsT, wsT_in[:, l], 'wsT', key='wsT')
            ld(gmb, gmb_in[:, l], 'gmb', key='gmb')
        blocks = list(range(8, 12)) if kv_only else list(range(36))
        cbs = sorted(set(b // 4 for b in blocks))
        wit = 0
        for ti in range(ntot // N):
            s0 = ti * N
            load_xt(xt, XD, ntot, s0, N, True)
            if not is_ctx:
                P.dma('sp', rc, ropec_in[:, s0:s0 + N], writes=['rc'], key='rc')
                P.dma('sp', rs, ropes_in[:, s0:s0 + N], writes=['rs'], key='rs')
            norm_mod(xt, h, W, gvec, svec, sq, tmp, rstd, s0 == 0, s0 + N == ntot)
            mmi = 0
            for cb in cbs:
                ws = wit % 2
                wit += 1
                P.dma('pool', wb[ws], Wi[l, cb].rearrange("p (k n) -> p k n", k=KD),
                      writes=[('wb', ws)], key=f"wb{ws}")
                for mi in range(4):
                    mb = cb * 4 + mi
                    if mb not in blocks:
                        continue
                    bi = mmi % 3
                    mmi += 1
                    bank = PA[bi]
                    bn = ('pa', bi)
                    is_halo = 20 <= mb < 28
                    hbk, hbn = HB[mb % 2]
                    for kc in range(KD):
                        OP('pe', 'matmul', bank[:, 0:N], wb[ws][:, kc, mi * 128:(mi + 1) * 128], h[:, kc, 1:1 + N],
                           start=(kc == 0), stop=(kc == KD - 1), r=[('wb', ws), ('h', kc)], w=[bn])
                        if is_halo:
                            OP('pe', 'matmul', hbk[:, 0:2], wb[ws][:, kc, mi * 128:(mi + 1) * 128],
                               h[:, kc, 0:W:W - 1], start=(kc == 0), stop=(kc == KD - 1),
                               r=[('wb', ws), ('h', kc)], w=[hbn])
                    if mb < 10:
                        isq = mb < 8
                        OP('act', 'activation', out=qf, in_=bank[:, 0:N], func=AF.Copy, r=[bn], w=['qf'])
                        OP('act', 'activation', out=sqbh, in_=bank[:, 0:N], func=AF.Square, r=[bn], w=['sqb'])
                        OP('pe', 'matmul', PST2[:, 0:N], ones_b, sqbh, start=True, stop=True,
                           r=['sqb', 'ones_b'], w=['pst2'])
                        OP('act', 'activation', out=rq, in_=PST2[:, 0:N], func=AF.Ln, bias=eps_ap, scale=1.0 / 128,
                           r=['pst2', 'cst'], w=['rq'])
                        OP('act', 'activation', out=rq, in_=rq, func=AF.Exp, scale=-0.5, r=['rq'], w=['rq'])
                        gq = (qg if isq else kg)[:, l:l + 1]
                        qs_ = qo[mb % 2]
                        qn_ = ('qo', mb % 2)
                        if is_ctx:
                            OP('dve', 'scalar_tensor_tensor', qs_, qf, gq, rq, ALU.mult, ALU.mult,
                               r=['qf', 'rq', 'qg', 'kg'], w=[qn_])
                        else:
                            OP('dve', 'scalar_tensor_tensor', qnb, qf, gq, rq, ALU.mult, ALU.mult,
                               r=['qf', 'rq', 'qg', 'kg'], w=['qnb'])
                            OP('pe', 'matmul', PROT[:, 0:N], rrot_b, qnb, start=True, stop=True,
                               r=['qnb', 'rrot_b'], w=['prot'])
                            OP('dve', 'tensor_tensor', t1, qnb, rc, ALU.mult, r=['qnb', 'rc'], w=['t1'])
                            OP('dve', 'tensor_tensor', t2, PROT[:, 0:N], rs, ALU.mult, r=['prot', 'rs'], w=['t2'])
                            OP('dve', 'tensor_tensor', qs_, t1, t2, ALU.add, r=['t1', 't2'], w=[qn_])
                        if isq:
                            store(QD[mb, :, s0:s0 + N], qs_, qn_)
                        else:
                            store(KS[mb - 8, :, koff + s0:koff + s0 + N], qs_, qn_)
                    elif mb < 12:
                        hv = mb - 10
                        OP('act', 'activation', out=vb, in_=bank[:, 0:N], func=AF.Copy, r=[bn], w=['vb'])
                        for j in range(nch):
                            OP('pe', 'transpose', PTR[:, j * 128:(j + 1) * 128], vb[:, j * 128:(j + 1) * 128], ident_b,
                               r=['vb', 'ident_b'], w=['ptr'])
                        OP('dve', 'tensor_copy', out=vt[hv][:, 0:nch, :],
                           in_=PTR[:, 0:nch * 128].rearrange("p (a b) -> p a b", a=nch), r=['ptr'], w=[('vt', hv)])
                        store(VS[hv, koff + s0:koff + s0 + N, :].rearrange("(j p) d -> p j d", p=128),
                              vt[hv][:, 0:nch, :], ('vt', hv))
                    elif mb < 16:
                        g = mb - 12
                        OP('act', 'activation', out=fb[g % 2], in_=bank[:, 0:N], func=AF.Copy, r=[bn], w=[('fb', g % 2)])
                        store(FD[g, :, s0:s0 + N], fb[g % 2], ('fb', g % 2))
                    elif mb < 20:
                        g = mb - 16
                        OP('act', 'activation', out=cbk[:, g, :], in_=bank[:, 0:N], func=AF.Copy, r=[bn], w=[('cbk', g)])
                    elif mb < 24:
                        g = mb - 20
                        OP('act', 'activation', out=ccb[:, g, 1:1 + N], in_=bank[:, 0:N], func=AF.Copy,
                           r=[bn], w=[('ccb', g)])
                        OP('act', 'activation', out=ccb[:, g, 0:W:W - 1], in_=hbk[:, 0:2], func=AF.Copy,
                           r=[hbn, ('ccb', g)], w=[('ccb', g)])
                    elif mb < 28:
                        g = mb - 24
                        OP('dve', 'tensor_tensor', prod[:, 1:1 + N], ccb[:, g, 1:1 + N], bank[:, 0:N], ALU.mult,
                           r=[bn, ('ccb', g)], w=['prod'])
                        OP('dve', 'tensor_tensor', prod[:, 0:W:W - 1], ccb[:, g, 0:W:W - 1], hbk[:, 0:2],
                           ALU.mult, r=[hbn, ('ccb', g), 'prod'], w=['prod'])
                        OP('dve', 'tensor_scalar', cv, prod[:, 0:N], convw[:, l, 0, g:g + 1], None, ALU.mult,
                           r=['prod', 'convw'], w=['cv'])
                        OP('dve', 'scalar_tensor_tensor', cv, prod[:, 1:1 + N], convw[:, l, 1, g:g + 1], cv,
                           ALU.mult, ALU.add, r=['prod', 'cv', 'convw'], w=['cv'])
                        OP('dve', 'scalar_tensor_tensor', cv, prod[:, 2:2 + N], convw[:, l, 2, g:g + 1], cv,
                           ALU.mult, ALU.add, r=['prod', 'cv', 'convw'], w=['cv'])
                        OP('dve', 'tensor_tensor', mixb[g % 2], cv, cbk[:, g, :], ALU.mult,
                           r=['cv', ('cbk', g)], w=[('mixb', g % 2)])
                        store(MD[12 + g, :, s0:s0 + N], mixb[g % 2], ('mixb', g % 2))
                    elif mb < 32:
                        g = mb - 28
                        OP('act', 'activation', out=ub[:, g, :], in_=bank[:, 0:N], func=AF.Gelu_apprx_tanh,
                           r=[bn], w=[('ub', g)])
                    else:
                        g = mb - 32
                        OP('act', 'activation', out=gvb[:, g, :], in_=bank[:, 0:N], func=AF.Gelu_apprx_tanh,
                           r=[bn], w=[('gvb', g)])
                        OP('act', 'activation', out=sqbh, in_=gvb[:, g, :], func=AF.Square, r=[('gvb', g)], w=['sqb'])
                        OP('pe', 'matmul', PROT[:, 0:N], ones_b, gvb[:, g, :], start=(g == 0), stop=(g == 3),
                           r=[('gvb', g), 'ones_b'], w=['prot'])
                        OP('pe', 'matmul', PST2[:, 0:N], ones_b, sqbh, start=(g == 0), stop=(g == 3),
                           r=['sqb', 'ones_b'], w=['pst2'])
                        if g == 3:
                            OP('dve', 'tensor_scalar', mn, PROT[:, 0:N], 1.0 / 512, None, ALU.mult, r=['prot'], w=['mn'])
                            OP('dve', 'tensor_tensor', msq, mn, mn, ALU.mult, r=['mn'], w=['msq'])
                            OP('dve', 'scalar_tensor_tensor', lrs, PST2[:, 0:N], 1.0 / 512, msq, ALU.mult,
                               ALU.subtract, r=['pst2', 'msq'], w=['lrs'])
                            OP('act', 'activation', out=lrs, in_=lrs, func=AF.Ln, bias=eps_ap, scale=1.0,
                               r=['lrs', 'cst'], w=['lrs'])
                            OP('act', 'activation', out=lrs, in_=lrs, func=AF.Exp, scale=-0.5, r=['lrs'], w=['lrs'])
                            for g2 in range(4):
                                OP('dve', 'tensor_tensor', t1, gvb[:, g2, :], mn, ALU.subtract,
                                   r=[('gvb', g2), 'mn'], w=['t1'])
                                OP('dve', 'tensor_tensor', t1, t1, lrs, ALU.mult, r=['t1', 'lrs'], w=['t1'])
                                OP('act', 'activation', out=vh, in_=t1, func=AF.Identity, bias=lnb[:, l, g2:g2 + 1],
                                   scale=lng[:, l, g2:g2 + 1], r=['t1', 'lng', 'lnb'], w=['vh'])
                                for j in range(nch):
                                    OP('pe', 'transpose', PTR[:, j * 128:(j + 1) * 128], vh[:, j * 128:(j + 1) * 128],
                                       ident_b, r=['vh', 'ident_b'], w=['ptr'])
                                OP('act', 'activation', out=vT[:, 0:nch, :],
                                   in_=PTR[:, 0:nch * 128].rearrange("p (a b) -> p a b", a=nch), func=AF.Copy,
                                   r=['ptr'], w=['vT'])
                                for j in range(nch):
                                    OP('pe', 'matmul', PROT[:, j * 128:(j + 1) * 128], vT[:, j, :], wsT[:, g2, :],
                                       start=True, stop=True, r=['vT', 'wsT'], w=['prot'])
                                OP('dve', 'tensor_tensor', t2[:, 0:N].rearrange("p (a b) -> p a b", a=nch),
                                   PROT[:, 0:N].rearrange("p (a b) -> p a b", a=nch),
                                   gmb[:, g2, :].unsqueeze(1).broadcast_to([128, nch, 128]), ALU.add,
                                   r=['prot', 'gmb'], w=['t2'])
                                OP('dve', 'tensor_tensor', mixb[g2 % 2], t2, ub[:, g2, :], ALU.mult,
                                   r=['t2', ('ub', g2)], w=[('mixb', g2 % 2)])
                                store(MD[16 + g2, :, s0:s0 + N], mixb[g2 % 2], ('mixb', g2 % 2))
        P.barrier()
        A.off = m

    def attention(stream):
        is_ctx = (stream == 1)
        nq_tot = TC if is_ctx else T
        nk = TC if is_ctx else TK
        NQ = min(512, nq_tot)
        QD = QSc if is_ctx else QS
        MD = MIXc if is_ctx else MIX
        nkc = nk // 128
        npair = nkc // 2
        m = A.off
        kT = A.alloc([2, nk], BF16)
        vv = A.alloc([nkc, 2, 128], BF16)
        qT = [A.alloc([8, NQ], BF16) for _ in range(2)]
        pT = [A.alloc([2, NQ], BF16) for _ in range(4)]
        rd = A.alloc([NQ], F32)
        ob = [A.alloc([NQ], BF16) for _ in range(2)]
        P.dma('sp', kT, KS[:, :, 0:nk].rearrange("h p t -> p h t"), writes=['kT'], key='kT')
        for hv_ in range(2):
            for c0_ in range(0, nkc, 8):
                c1_ = min(c0_ + 8, nkc)
                P.dma('sp', vv[:, c0_:c1_, hv_, :],
                      VS[hv_, c0_ * 128:c1_ * 128, :].rearrange("(c p) d -> p c d", p=128),
                      writes=[('vv', hv_)], key=f'vv{hv_}')
        if ATT_LA == 2:
            PS_S = [PP[0], PP[1], PP[2]]
            PS_O = [BK[6], BK[6]]
            PS_D = [BK[7], BK[7]]
        else:
            PS_S = [PP[0], PP[1]]
            PS_O = [BK[4], BK[5]]
            PS_D = [BK[6], BK[7]]
        NS = len(PS_S)
        hi = 0
        for qt in range(nq_tot // NQ):
            q0 = qt * NQ
            qs = qt % 2
            P.dma('sp', qT[qs], QD[:, :, q0:q0 + NQ].rearrange("h p t -> p h t"), writes=[('qT', qs)], key=f"qT{qs}")
            for hh in range(8):
                kvh = hh // 4
                po = PS_O[hi % 2]
                pd = PS_D[hi % 2]
                pon = ('pso', hi % 2 if ATT_LA == 1 else 0)
                pdn = ('psd', hi % 2 if ATT_LA == 1 else 0)

                def S(p):
                    ps = PS_S[p % NS]
                    for u in range(2):
                        kc = 2 * p + u
                        OP('pe', 'matmul', ps[:, u * 512:u * 512 + NQ], kT[:, kvh, kc * 128:(kc + 1) * 128],
                           qT[qs][:, hh, :], start=True, stop=True, r=['kT', ('qT', qs)], w=[('pss', p % NS)])
                    OP('act', 'activation', out=pT[p % 4],
                       in_=ps[:, :].rearrange("p (a b) -> p a b", a=2)[:, :, 0:NQ], func=AF.Exp,
                       r=[('pss', p % NS)], w=[('pT', p % 4)])

                def PV(p):
                    for u in range(2):
                        kc = 2 * p + u
                        OP('pe', 'matmul', po[:, 0:NQ], vv[:, kc, kvh, :], pT[p % 4][:, u, :], start=(kc == 0),
                           stop=(kc == nkc - 1), r=[('vv', kvh), ('pT', p % 4)], w=[pon])
                        OP('pe', 'matmul', pd[:, 0:NQ], ones_b, pT[p % 4][:, u, :], start=(kc == 0),
                           stop=(kc == nkc - 1), r=['ones_b', ('pT', p % 4)], w=[pdn])

                for p in range(min(ATT_LA, npair)):
                    S(p)
                for p in range(npair):
                    if p + ATT_LA < npair:
                        S(p + ATT_LA)
                    PV(p)
                OP('act', 'activation', out=rd, in_=pd[:, 0:NQ], func=AF.Ln, r=[pdn], w=['rd'])
                OP('act', 'activation', out=rd, in_=rd, func=AF.Exp, scale=-1.0, r=['rd'], w=['rd'])
                OP('dve', 'tensor_tensor', ob[hi % 2], po[:, 0:NQ], rd, ALU.mult, r=[pon, 'rd'], w=[('ob', hi % 2)])
                store(MD[hh, :, q0:q0 + NQ], ob[hi % 2], ('ob', hi % 2))
                hi += 1
        P.barrier()
        A.off = m

    def fourier(stream):
        is_ctx = (stream == 1)
        n = TC if is_ctx else T
        FD = FSc if is_ctx else FS
        MD = MIXc if is_ctx else MIX
        DN = dftnc_in if is_ctx else dftn_in
        ntc = n // 128
        NK = min(FOUR_NK, n)
        m = A.off
        AB = A.alloc([ntc, 4, 256], BF16)
        zT = [A.alloc([n], BF16) for _ in range(2)]
        cn = [A.alloc([ntc, NK], BF16)]
        sn = [A.alloc([ntc, NK], BF16)]
        yb = [A.alloc([NK], BF16) for _ in range(2)]
        CG = 8 if ntc >= 8 else ntc
        ei = 0
        for g in range(4):
            P.dma('sp', zT[g % 2], FD[g], writes=[('zT', g % 2)], key=f"zT{g % 2}")
            for tcp in range(ntc // 2):
                bi = ei % 3
                for u in range(2):
                    tc_ = tcp * 2 + u
                    OP('pe', 'matmul', PA[bi][:, u * 256:(u + 1) * 256], zT[g % 2][:, tc_ * 128:(tc_ + 1) * 128],
                       dftc_b, start=True, stop=True, r=[('zT', g % 2), 'dftc_b'], w=[('pa', bi)])
                src = PA[bi][:, :].rearrange("p (a b) -> p a b", a=2)
                dst = AB[:, tcp * 2:tcp * 2 + 2, g, :]
                if ei % 2 == 0:
                    OP('act', 'activation', out=dst, in_=src, func=AF.Copy, r=[('pa', bi)], w=['AB'])
                else:
                    OP('dve', 'tensor_copy', out=dst, in_=src, r=[('pa', bi)], w=['AB'])
                ei += 1
        yi = 0
        for kt in range(n // NK):
            s = 0
            for c0_ in range(0, ntc, CG):
                c1_ = min(c0_ + CG, ntc)
                cg = c0_ // CG
                if is_ctx:
                    P.dma('pool', cn[s][:, c0_:c1_, :],
                          DN[0, c0_ * 128:c1_ * 128, kt * NK:(kt + 1) * NK].rearrange("(c p) k -> p c k", p=128),
                          writes=[('cn', s, cg)], key=f"cn{s}_{cg}")
                    P.dma('pool', sn[s][:, c0_:c1_, :],
                          DN[1, c0_ * 128:c1_ * 128, kt * NK:(kt + 1) * NK].rearrange("(c p) k -> p c k", p=128),
                          writes=[('sn', s, cg)], key=f"sn{s}_{cg}")
                else:
                    P.dma('pool', cn[s][:, c0_:c1_, :],
                          DN16[0, kt].rearrange("p (c k) -> p c k", c=ntc)[:, c0_:c1_, :],
                          writes=[('cn', s, cg)], key=f"cn{s}_{cg}")
                    P.dma('pool', sn[s][:, c0_:c1_, :],
                          DN16[1, kt].rearrange("p (c k) -> p c k", c=ntc)[:, c0_:c1_, :],
                          writes=[('sn', s, cg)], key=f"sn{s}_{cg}")
            for g in range(4):
                bi = yi % 3
                for tc_ in range(ntc):
                    OP('pe', 'matmul', PA[bi][:, 0:NK], AB[:, tc_, g, 0:128], cn[s][:, tc_, :], start=(tc_ == 0),
                       stop=False, r=['AB', ('cn', s, tc_ // CG)], w=[('pa', bi)])
                    OP('pe', 'matmul', PA[bi][:, 0:NK], AB[:, tc_, g, 128:256], sn[s][:, tc_, :], start=False,
                       stop=(tc_ == ntc - 1), r=['AB', ('sn', s, tc_ // CG)], w=[('pa', bi)])
                if yi % 2 == 0:
                    OP('act', 'activation', out=yb[yi % 2], in_=PA[bi][:, 0:NK], func=AF.Copy,
                       r=[('pa', bi)], w=[('yb', yi % 2)])
                else:
                    OP('dve', 'tensor_copy', out=yb[yi % 2], in_=PA[bi][:, 0:NK], r=[('pa', bi)], w=[('yb', yi % 2)])
                store(MD[8 + g, :, kt * NK:(kt + 1) * NK], yb[yi % 2], ('yb', yi % 2))
                yi += 1
        P.barrier()
        A.off = m

    def phase3(l, stream):
        is_ctx = (stream == 1)
        ntot = TC if is_ctx else T
        N = min(NT, ntot)
        XD = XC if is_ctx else XT
        MD = MIXc if is_ctx else MIX
        m = A.off
        xt = A.alloc([KD, N], F32)
        mt = A.alloc([20, N], BF16)
        wo = [A.alloc([20, 512], BF16) for _ in range(2)]
        ga = VEC[:, stream, l, 2, :]
        wit = 0
        mmi = 0
        for ti in range(ntot // N):
            s0 = ti * N
            load_xt(xt, XD, ntot, s0, N, False)
            P.dma('sp', mt, MD[:, :, s0:s0 + N].rearrange("k p t -> p k t"), writes=['mt'], key='mt')
            for cb in range(4):
                ws = wit % 2
                wit += 1
                P.dma('pool', wo[ws], Wo[l, cb].rearrange("p (k n) -> p k n", k=20),
                      writes=[('wo', ws)], key=f"wo{ws}")
                for mi in range(4):
                    mb = cb * 4 + mi
                    bi = mmi % 3
                    mmi += 1
                    for k in range(20):
                        OP('pe', 'matmul', PA[bi][:, 0:N], wo[ws][:, k, mi * 128:(mi + 1) * 128], mt[:, k, :],
                           start=(k == 0), stop=(k == 19), r=[('wo', ws), 'mt'], w=[('pa', bi)])
                    OP('dve', 'scalar_tensor_tensor', xt[:, mb, :], PA[bi][:, 0:N], ga[:, mb:mb + 1], xt[:, mb, :],
                       ALU.mult, ALU.add, r=[('pa', bi), 'xt', 'VEC'], w=['xt'])
            P.dma('sp', (XCM if is_ctx else XM)[:, :, s0:s0 + N].rearrange("k p t -> p k t"), xt, reads=['xt'], key='xts')
        P.barrier()
        A.off = m

    def phase4(l, stream):
        is_ctx = (stream == 1)
        ntot = TC if is_ctx else T
        N = min(NT, ntot)
        XD = XC if is_ctx else XT
        W = N + 2
        m = A.off
        xt = A.alloc([KD, W], F32)
        h = A.alloc([KD, W], BF16)
        sq = [A.alloc([W], F32) for _ in range(2)]
        tmp = [A.alloc([W], F32) for _ in range(2)]
        rstd = A.alloc([W], F32)
        act = A.alloc([NFF, N], BF16)
        wg = [A.alloc([KD, 256], BF16) for _ in range(2)]
        wu = [A.alloc([KD, 256], BF16) for _ in range(2)]
        wd = [A.alloc([NFF, 256], BF16) for _ in range(2)]
        gb = sq
        cv = [t_[:, 0:N] for t_ in tmp]
        sg = cv
        gvec = VEC[:, stream, l, 3, :]
        svec = VEC[:, stream, l, 4, :]
        ga = VEC[:, stream, l, 5, :]
        wit = 0
        wdi = 0
        ji = 0
        for ti in range(ntot // N):
            s0 = ti * N
            load_xt(xt, XCM if is_ctx else XM, ntot, s0, N, True)
            norm_mod(xt, h, W, gvec, svec, sq, tmp, rstd, s0 == 0, s0 + N == ntot)
            for jb in range(NFF // 2):
                ws = wit % 2
                wit += 1
                P.dma('pool', wg[ws], Wg[l, jb].rearrange("p (k n) -> p k n", k=KD),
                      writes=[('wg', ws)], key=f"wg{ws}")
                P.dma('pool', wu[ws], Wu[l, jb].rearrange("p (k n) -> p k n", k=KD),
                      writes=[('wu', ws)], key=f"wu{ws}")
                for j2 in range(2):
                    j = jb * 2 + j2
                    s = ji % 2
                    ji += 1
                    pg = PA[0] if s == 0 else PA[1]
                    pgn = ('pa', 0 if s == 0 else 1)
                    pu = PST if s == 0 else PST2
                    pun = 'pst' if s == 0 else 'pst2'
                    hbk, hbn = HB[s]
                    for kc in range(KD):
                        OP('pe', 'matmul', pg[:, 0:N], wg[ws][:, kc, j2 * 128:(j2 + 1) * 128], h[:, kc, 1:1 + N],
                           start=(kc == 0), stop=(kc == KD - 1), r=[('wg', ws), ('h', kc)], w=[pgn])
                        OP('pe', 'matmul', hbk[:, 0:2], wg[ws][:, kc, j2 * 128:(j2 + 1) * 128],
                           h[:, kc, 0:W:W - 1], start=(kc == 0), stop=(kc == KD - 1),
                           r=[('wg', ws), ('h', kc)], w=[hbn])
                    for kc in range(KD):
                        OP('pe', 'matmul', pu[:, 0:N], wu[ws][:, kc, j2 * 128:(j2 + 1) * 128], h[:, kc, 1:1 + N],
                           start=(kc == 0), stop=(kc == KD - 1), r=[('wu', ws), ('h', kc)], w=[pun])
                    OP('act', 'activation', out=gb[s][:, 1:1 + N], in_=pg[:, 0:N], func=AF.Copy, r=[pgn], w=[('sq', s)])
                    OP('act', 'activation', out=gb[s][:, 0:W:W - 1], in_=hbk[:, 0:2], func=AF.Copy,
                       r=[hbn, ('sq', s)], w=[('sq', s)])
                    OP('dve', 'tensor_scalar', cv[s], gb[s][:, 0:N], fcw[:, l, 0, j:j + 1], None, ALU.mult,
                       r=[('sq', s), 'fcw'], w=[('tmp', s)])
                    OP('dve', 'scalar_tensor_tensor', cv[s], gb[s][:, 1:1 + N], fcw[:, l, 1, j:j + 1], cv[s],
                       ALU.mult, ALU.add, r=[('sq', s), ('tmp', s), 'fcw'], w=[('tmp', s)])
                    OP('dve', 'scalar_tensor_tensor', cv[s], gb[s][:, 2:2 + N], fcw[:, l, 2, j:j + 1], cv[s],
                       ALU.mult, ALU.add, r=[('sq', s), ('tmp', s), 'fcw'], w=[('tmp', s)])
                    OP('act', 'activation', out=sg[s], in_=cv[s], func=AF.Silu, bias=fcb[:, l, j:j + 1], scale=1.0,
                       r=[('tmp', s), 'fcb'], w=[('tmp', s)])
                    OP('dve', 'tensor_tensor', act[:, j, :], sg[s], pu[:, 0:N], ALU.mult,
                       r=[('tmp', s), pun], w=[('act', j)])
            for cb in range(8):
                ws = wdi % 2
                wdi += 1
                P.dma('pool', wd[ws], Wd[l, cb].rearrange("p (k n) -> p k n", k=NFF),
                      writes=[('wd', ws)], key=f"wd{ws}")
                for mi in range(2):
                    mb = cb * 2 + mi
                    bank = PROT if mb % 2 == 0 else PA[2]
                    bn = 'prot' if mb % 2 == 0 else ('pa', 2)
                    for j in range(NFF):
                        OP('pe', 'matmul', bank[:, 0:N], wd[ws][:, j, mi * 128:(mi + 1) * 128], act[:, j, :],
                           start=(j == 0), stop=(j == NFF - 1), r=[('wd', ws), ('act', j)], w=[bn])
                    OP('dve', 'scalar_tensor_tensor', xt[:, mb, 1:1 + N], bank[:, 0:N], ga[:, mb:mb + 1],
                       xt[:, mb, 1:1 + N], ALU.mult, ALU.add, r=[bn, 'xt', 'VEC'], w=['xt'])
            P.dma('sp', XD[:, :, s0:s0 + N].rearrange("k p t -> p k t"), xt[:, :, 1:1 + N], reads=['xt'], key='xts')
        P.barrier()
        A.off = m

    def final_norm():
        N = NT
        m = A.off
        xt = A.alloc([KD, N], F32)
        sq = [A.alloc([N], BF16) for _ in range(2)]
        rstd = A.alloc([N], F32)
        y = A.alloc([KD, N], F32)
        ot = [A.alloc([D], F32) for _ in range(2)]
        fins = []
        oi = 0
        for ti in range(T // N):
            s0 = ti * N
            load_xt(xt, XT, T, s0, N, False)
            for kc in range(KD):
                s = kc % 2
                OP('act', 'activation', out=sq[s], in_=xt[:, kc, :], func=AF.Square, r=['xt'], w=[('sq', s)])
                OP('pe', 'matmul', PST[:, 0:N], ones_b, sq[s], start=(kc == 0), stop=(kc == KD - 1),
                   r=[('sq', s), 'ones_b'], w=['pst'])
            OP('act', 'activation', out=rstd, in_=PST[:, 0:N], func=AF.Ln, bias=eps_ap, scale=1.0 / D,
               r=['pst', 'cst'], w=['rstd'])
            OP('act', 'activation', out=rstd, in_=rstd, func=AF.Exp, scale=-0.5, r=['rstd'], w=['rstd'])
            for kc in range(KD):
                OP('dve', 'scalar_tensor_tensor', y[:, kc, :], xt[:, kc, :], fng[:, kc:kc + 1], rstd, ALU.mult,
                   ALU.mult, r=['xt', 'rstd', 'fng'], w=[('y', kc)])
            for tb in range(N // 128):
                o = oi % 2
                oi += 1
                for q4 in range(4):
                    bi = q4 % 3
                    for j in range(4):
                        kc = q4 * 4 + j
                        OP('pe', 'transpose', PA[bi][:, j * 128:(j + 1) * 128], y[:, kc, tb * 128:(tb + 1) * 128],
                           ident_f, r=[('y', kc), 'ident_f'], w=[('pa', bi)])
                    if q4 % 2 == 0:
                        OP('act', 'activation', out=ot[o][:, q4 * 512:(q4 + 1) * 512], in_=PA[bi][:, :], func=AF.Copy,
                           r=[('pa', bi)], w=[('ot', o)])
                    else:
                        OP('dve', 'tensor_copy', out=ot[o][:, q4 * 512:(q4 + 1) * 512], in_=PA[bi][:, :],
                           r=[('pa', bi)], w=[('ot', o)])
                fins.append(P.dma('sp', out[s0 + tb * 128:s0 + (tb + 1) * 128, :], ot[o], reads=[('ot', o)],
                                  key=f"out{o}"))
        A.off = m
        return fins

    steps = []
    for l in range(L):
        lastl = (l == L - 1)
        steps.append(('p1c', lambda l=l, lastl=lastl: phase1(l, 1, lastl)))
        steps.append(('p1x', lambda l=l: phase1(l, 0, False)))
        if not lastl:
            steps.append(('atc', lambda: attention(1)))
        if not lastl:
            steps.append(('cast', lambda l=l: cast_layer(l + 1)))
        steps.append(('atx', lambda: attention(0)))
        if not lastl:
            steps.append(('foc', lambda: fourier(1)))
        steps.append(('fox', lambda: fourier(0)))
        if not lastl:
            steps.append(('p3c', lambda l=l: phase3(l, 1)))
        steps.append(('p3x', lambda l=l: phase3(l, 0)))
        if not lastl:
            steps.append(('p4c', lambda l=l: phase4(l, 1)))
        steps.append(('p4x', lambda l=l: phase4(l, 0)))
    nsteps = len(steps) if stop_after is None else stop_after
    P.marks = [('prologue', 0)]
    for name, fn in steps[:nsteps]:
        P.marks.append((name, sum(1 for o in P.streams['pe'] if o.fn is not None)))
        fn()
    P.marks.append(('final', sum(1 for o in P.streams['pe'] if o.fn is not None)))
    fins = final_norm()
    P.emit(final_waits=fins)
    return nc, P


def _fm(v, n):
    v = np.asarray(v, np.float32)
    lead = v.shape[:-1]
    v = v.reshape(*lead, n, 128)
    nd = v.ndim
    return np.ascontiguousarray(np.moveaxis(v, -1, 0))


def _constants(T):
    ident = np.eye(128, dtype=np.float32)
    rrot = np.zeros((128, 128), np.float32)
    for base in (0, 64):
        for i in range(32):
            rrot[base + 32 + i, base + i] = -1.0
            rrot[base + i, base + 32 + i] = 1.0
    t = np.arange(T)
    row = (t // 64).astype(np.float64)
    col = (t % 64).astype(np.float64)
    freqs = 10000.0 ** (-np.arange(0, 64, 2, dtype=np.float32).astype(np.float64) / 64)
    ang_r = (row[:, None].astype(np.float32) * freqs[None, :].astype(np.float32)).astype(np.float32)
    ang_c = (col[:, None].astype(np.float32) * freqs[None, :].astype(np.float32)).astype(np.float32)
    cr, sr, cc, sc = np.cos(ang_r), np.sin(ang_r), np.cos(ang_c), np.sin(ang_c)
    ropec = np.concatenate([cr, cr, cc, cc], axis=1).T.astype(np.float32)
    ropes = np.concatenate([sr, sr, sc, sc], axis=1).T.astype(np.float32)
    j = np.arange(128)
    angc = 2 * np.pi * ((j[:, None] * j[None, :]) % 128) / 128
    dftc = (np.concatenate([np.cos(angc), np.sin(angc)], axis=1) / np.sqrt(128.0)).astype(np.float32)

    def dn(n):
        k = np.arange(n, dtype=np.int64)
        ang = 2 * np.pi * ((k[:, None] * k[None, :]) % n).astype(np.float64) / n
        o = np.empty((2, n, n), np.float32)
        o[0] = np.cos(ang) / np.sqrt(n)
        o[1] = -np.sin(ang) / np.sqrt(n)
        return o
    return dict(ident=ident, rrot=rrot, ropec=np.ascontiguousarray(ropec), ropes=np.ascontiguousarray(ropes),
                dftc=dftc, dftn=dn(T), dftnc=dn(TC))


def make_in_maps(inp, T, L, ncores):
    f = lambda k: np.asarray(inp[k], np.float32)
    shared = dict(
        w_mod=np.ascontiguousarray(f('w_mod')[:L]),
        b_mod_t=np.ascontiguousarray(f('b_mod')[:L].reshape(L, 96, 128).transpose(2, 0, 1)),
        n1g=np.ascontiguousarray(f('norm1_g')[:L].reshape(L, KD, 128).transpose(2, 0, 1)),
        n2g=np.ascontiguousarray(f('norm2_g')[:L].reshape(L, KD, 128).transpose(2, 0, 1)),
        fng=np.ascontiguousarray(f('final_norm_g').reshape(KD, 128).T),
        w_in=np.ascontiguousarray(f('w_in')[:L]),
        qg=np.ascontiguousarray(f('q_norm_g')[:L].T),
        kg=np.ascontiguousarray(f('k_norm_g')[:L].T),
        convw=np.ascontiguousarray(f('conv_w')[:L].reshape(L, 3, 4, 128).transpose(3, 0, 1, 2)),
        lng=np.ascontiguousarray(f('gm_ln_g')[:L].reshape(L, 4, 128).transpose(2, 0, 1)),
        lnb=np.ascontiguousarray(f('gm_ln_b')[:L].reshape(L, 4, 128).transpose(2, 0, 1)),
        wsT=np.ascontiguousarray(f('gm_ws')[:L].transpose(3, 0, 1, 2)),
        gmb=np.ascontiguousarray(np.broadcast_to(f('gm_b')[:L][None], (128, L, 4, 128))),
        w_out=np.ascontiguousarray(f('w_out')[:L]),
        w_up=np.ascontiguousarray(f('w_up')[:L]),
        w_down=np.ascontiguousarray(f('w_down')[:L]),
        fcw=np.ascontiguousarray(f('ffn_conv_w')[:L].reshape(L, 3, NFF, 128).transpose(3, 0, 1, 2)),
        fcb=np.ascontiguousarray(f('ffn_conv_b')[:L].reshape(L, NFF, 128).transpose(2, 0, 1)),
    )
    shared.update(_constants(T))
    x = f('x')
    ctx = f('ctx')
    c = f('c')
    cc = f('c_ctx')
    maps = []
    for b in range(ncores):
        mp = dict(shared)
        mp['x'] = np.ascontiguousarray(x[b, :T])
        mp['ctx'] = np.ascontiguousarray(ctx[b])
        cv = np.stack([c[b].reshape(KD, 128).T, cc.reshape(KD, 128).T], axis=1)
        mp['cvec'] = np.ascontiguousarray(cv)
        maps.append(mp)
    return maps


ACTIVE_SLOTS = (0, 1, 4, 5)


def kernel(**inputs):
    T = 4096
    L = 4
    nc, _ = build_program(T, L)
    real = make_in_maps(inputs, T, L, NCORES)
    zeros = {k: np.zeros_like(v) for k, v in real[0].items()}
    maps = [zeros] * 8
    maps = list(maps)
    for b, slot in enumerate(ACTIVE_SLOTS):
        maps[slot] = real[b]
    res = run_bass_kernel_spmd(nc, maps, core_ids=list(range(8)))
    return np.stack([np.asarray(res.results[slot]["out"], np.float32) for slot in ACTIVE_SLOTS], axis=0)
```

```python
from contextlib import ExitStack
import math
import numpy as np
import concourse.bass as bass
import concourse.mybir as mybir
from concourse.bass_utils import run_bass_kernel_spmd

F32 = mybir.dt.float32
BF16 = mybir.dt.bfloat16
AF = mybir.ActivationFunctionType
ALU = mybir.AluOpType

EPOCH = 30000
DMA_EPOCH = 1800
ENGS = ('sp', 'act', 'pe', 'dve', 'pool')

D = 2048
KD = 16
INW = 4608
MIXW = 2560
DFF = 5632
NFF = 44
EPS = 1e-6
TC = 256
NCORES = 4
ATT_LA = 1
FOUR_NK = 512


class Op:
    __slots__ = ('eng', 'fn', 'deps', 'need_inc', 'pos', 'key', 'dn', 'is_dma')

    def __init__(self, eng, fn, is_dma=False, key=None):
        self.eng = eng
        self.fn = fn
        self.deps = ()
        self.need_inc = False
        self.pos = -1
        self.key = key
        self.dn = -1
        self.is_dma = is_dma


class Prog:
    def __init__(self, nc):
        self.nc = nc
        self.es = ExitStack()
        self.streams = {e: [] for e in ENGS}
        self.res = {}
        self.dma_count = {}
        self.dma_since = {}
        self.last_compute = {}
        self.n_ops = 0

    def sbuf(self, name, shape, dt):
        return self.es.enter_context(self.nc.sbuf_tensor(name, list(shape), dt))

    def psum(self, name, shape, dt):
        return self.es.enter_context(self.nc.psum_tensor(name, list(shape), dt))

    def _track(self, o, reads, writes):
        deps = {}
        res = self.res
        for r in reads:
            st = res.get(r)
            if st is not None and st[0] is not None:
                deps[id(st[0])] = st[0]
        for w in writes:
            st = res.get(w)
            if st is not None:
                if st[0] is not None:
                    deps[id(st[0])] = st[0]
                for d in st[1].values():
                    deps[id(d)] = d
                for d in st[2]:
                    deps[id(d)] = d
        for r in reads:
            st = res.get(r)
            if st is None:
                st = res[r] = [None, {}, []]
            if o.is_dma:
                st[2].append(o)
            else:
                st[1][o.eng] = o
        for w in writes:
            res[w] = [o, {}, []]
        dl = []
        for d in deps.values():
            if d is o:
                continue
            if (not d.is_dma) and d.eng == 'pe' and o.eng == 'pe' and not o.is_dma:
                continue
            d.need_inc = True
            dl.append(d)
        o.deps = dl

    def add(self, eng, fn, reads=(), writes=()):
        o = Op(eng, fn)
        self._track(o, reads, writes)
        self.streams[eng].append(o)
        self.last_compute[eng] = o
        self.n_ops += 1
        return o

    def dma(self, eng, out, in_, reads=(), writes=(), key=None, fn=None):
        assert key is not None
        if fn is None:
            fn = lambda e: e.dma_start(out=out, in_=in_)
        o = Op(eng, fn, is_dma=True, key=key)
        n = self.dma_count.get(key, 0)
        o.dn = n
        self.dma_count[key] = n + 1
        self._track(o, reads, writes)
        self.streams[eng].append(o)
        self.dma_since[key] = o
        self.n_ops += 1
        return o

    def barrier(self):
        deps = list(self.last_compute.values()) + list(self.dma_since.values())
        for d in deps:
            d.need_inc = True
        for e in ENGS:
            o = Op(e, None)
            o.deps = [d for d in deps if d.is_dma or d.eng != e or e != 'pe']
            self.streams[e].append(o)
        self.dma_since = {}
        self.res = {}

    def emit(self, final_waits=()):
        nc = self.nc
        es = self.es
        npos = {}
        for e in ENGS:
            p = 0
            for o in self.streams[e]:
                if (not o.is_dma) and o.need_inc:
                    o.pos = p
                    p += 1
            npos[e] = p
        eng_sems = {}
        nsem = 0
        for e in ENGS:
            k = max(1, (npos[e] + EPOCH - 1) // EPOCH)
            eng_sems[e] = [es.enter_context(nc.semaphore(f"se_{e}_{i}")) for i in range(k)]
            nsem += k
        dma_sems = {}
        for key, cnt in self.dma_count.items():
            k = max(1, (cnt + DMA_EPOCH - 1) // DMA_EPOCH)
            dma_sems[key] = [es.enter_context(nc.semaphore(f"sd_{len(dma_sems)}_{i}")) for i in range(k)]
            nsem += k
        self.nsem = nsem
        block = es.enter_context(nc.Block())
        streams = self.streams
        final_waits = list(final_waits)

        def make_body(e):
            def body(eng):
                known = {x: -1 for x in ENGS}
                known_d = {}

                def wait_for(d):
                    if d.is_dma:
                        ep = d.dn // DMA_EPOCH
                        kk = (d.key, ep)
                        v = (d.dn % DMA_EPOCH + 1) * 16
                        if known_d.get(kk, 0) >= v:
                            return
                        known_d[kk] = v
                        eng.wait_ge(dma_sems[d.key][ep], v)
                    else:
                        if known[d.eng] >= d.pos:
                            return
                        known[d.eng] = d.pos
                        ep = d.pos // EPOCH
                        eng.wait_ge(eng_sems[d.eng][ep], d.pos % EPOCH + 1)

                for o in streams[e]:
                    for d in o.deps:
                        wait_for(d)
                    if o.fn is None:
                        continue
                    ins = o.fn(eng)
                    if o.is_dma:
                        ep = o.dn // DMA_EPOCH
                        ins.then_inc(dma_sems[o.key][ep], 16)
                    elif o.need_inc:
                        ep = o.pos // EPOCH
                        ins.then_inc(eng_sems[e][ep], 1)
                if e == 'sp':
                    for d in final_waits:
                        wait_for(d)
            return body

        block.sync(make_body('sp'))
        block.scalar(make_body('act'))
        block.tensor(make_body('pe'))
        block.vector(make_body('dve'))
        block.gpsimd(make_body('pool'))
        es.close()


class Arena:
    def __init__(self, P, nbytes):
        self.t = P.sbuf("arena", [128, nbytes // 4], F32)
        self.off = 0
        self.size = nbytes

    def alloc(self, shape, dt):
        n = 1
        for s in shape:
            n *= s
        nb = n * (4 if dt == F32 else 2)
        nb = (nb + 63) // 64 * 64
        o = self.off
        self.off += nb
        assert self.off <= self.size, ("arena overflow", self.off, self.size)
        v = self.t[:, o // 4:(o + nb) // 4]
        if dt == BF16:
            v = v.bitcast(BF16)
        v = v[:, 0:n]
        if len(shape) == 2:
            v = v.rearrange("p (a b) -> p a b", a=shape[0])
        elif len(shape) == 3:
            v = v.rearrange("p (a b c) -> p a b c", a=shape[0], b=shape[1])
        elif len(shape) == 4:
            v = v.rearrange("p (a b c d) -> p a b c d", a=shape[0], b=shape[1], c=shape[2])
        return v


def build_program(T, L, debug=False, stop_after=None):
    nc = bass.Bass("TRN2", target_bir_lowering=False)
    TK = TC + T
    NT = 512

    def din(name, shape, dt=F32):
        return nc.dram_tensor(name, list(shape), dt, kind="ExternalInput").ap()

    def dscr(name, shape, dt):
        return nc.dram_tensor(name, list(shape), dt, kind="ExternalOutput" if debug else "Internal").ap()

    x_in = din("x", [T, D])
    ctx_in = din("ctx", [TC, D])
    cvec_in = din("cvec", [128, 2, KD])
    w_mod = din("w_mod", [L, D, 6 * D])
    b_mod_t = din("b_mod_t", [128, L, 96])
    n1g_in = din("n1g", [128, L, KD])
    n2g_in = din("n2g", [128, L, KD])
    fng_in = din("fng", [128, KD])
    w_in = din("w_in", [L, D, INW])
    qg_in = din("qg", [128, L])
    kg_in = din("kg", [128, L])
    convw_in = din("convw", [128, L, 3, 4])
    lng_in = din("lng", [128, L, 4])
    lnb_in = din("lnb", [128, L, 4])
    wsT_in = din("wsT", [128, L, 4, 128])
    gmb_in = din("gmb", [128, L, 4, 128])
    w_out = din("w_out", [L, MIXW, D])
    w_up = din("w_up", [L, D, 2 * DFF])
    fcw_in = din("fcw", [128, L, 3, NFF])
    fcb_in = din("fcb", [128, L, NFF])
    w_down = din("w_down", [L, DFF, D])
    ident_in = din("ident", [128, 128])
    rrot_in = din("rrot", [128, 128])
    ropec_in = din("ropec", [128, T])
    ropes_in = din("ropes", [128, T])
    dftc_in = din("dftc", [128, 256])
    dftn_in = din("dftn", [2, T, T])
    dftnc_in = din("dftnc", [2, TC, TC])
    out = nc.dram_tensor("out", [T, D], F32, kind="ExternalOutput").ap()

    XT = dscr("XT", [KD, 128, T], F32)
    XC = dscr("XC", [KD, 128, TC], F32)
    XM = dscr("XM", [KD, 128, T], F32)
    XCM = dscr("XCM", [KD, 128, TC], F32)
    QS = dscr("QS", [8, 128, T], BF16)
    QSc = dscr("QSc", [8, 128, TC], BF16)
    KS = dscr("KS", [2, 128, TK], BF16)
    VS = dscr("VS", [2, TK, 128], BF16)
    FS = dscr("FS", [4, 128, T], BF16)
    FSc = dscr("FSc", [4, 128, TC], BF16)
    MIX = dscr("MIX", [20, 128, T], BF16)
    MIXc = dscr("MIXc", [20, 128, TC], BF16)

    Wi = nc.dram_tensor("Wi", [L, 9, 128, KD * 512], BF16, kind="Internal").ap()
    Wo = nc.dram_tensor("Wo", [L, 4, 128, 20 * 512], BF16, kind="Internal").ap()
    Wg = nc.dram_tensor("Wg", [L, 22, 128, KD * 256], BF16, kind="Internal").ap()
    Wu = nc.dram_tensor("Wu", [L, 22, 128, KD * 256], BF16, kind="Internal").ap()
    Wd = nc.dram_tensor("Wd", [L, 8, 128, NFF * 256], BF16, kind="Internal").ap()

    NKF = FOUR_NK
    DN16 = nc.dram_tensor("DN16", [2, T // NKF, 128, (T // 128) * NKF], BF16, kind="Internal").ap()

    P = Prog(nc)
    A = Arena(P, 200 * 1024)

    def cast_dft():
        ntc_ = T // 128
        for mtx in range(2):
            for kt in range(T // NKF):
                for c0_ in range(0, ntc_, 8):
                    c1_ = min(c0_ + 8, ntc_)
                    P.dma('pool', DN16[mtx, kt].rearrange("p (c k) -> p c k", c=ntc_)[:, c0_:c1_, :],
                          dftn_in[mtx, c0_ * 128:c1_ * 128, kt * NKF:(kt + 1) * NKF].rearrange("(c p) k -> p c k", p=128),
                          key='cw')

    def cast_layer(l):
        def c(dst, src, k):
            P.dma('pool', dst.rearrange("p (k n) -> p k n", k=k), src.rearrange("(k p) n -> p k n", p=128),
                  key='cw')
        for cb in range(9):
            c(Wi[l, cb], w_in[l, :, cb * 512:(cb + 1) * 512], KD)
        for cb in range(4):
            c(Wo[l, cb], w_out[l, :, cb * 512:(cb + 1) * 512], 20)
        for jb in range(22):
            c(Wg[l, jb], w_up[l, :, jb * 256:(jb + 1) * 256], KD)
            c(Wu[l, jb], w_up[l, :, DFF + jb * 256:DFF + (jb + 1) * 256], KD)
        for cb in range(8):
            c(Wd[l, cb], w_down[l, :, cb * 256:(cb + 1) * 256], NFF)

    def OP(eng, meth, *args, r=(), w=(), **kw):
        return P.add(eng, lambda e: getattr(e, meth)(*args, **kw), reads=r, writes=w)

    PP = [P.psum(f"pp{i}", [128, 1024], F32) for i in range(4)]
    BK = [PP[i // 2][:, (i % 2) * 512:(i % 2 + 1) * 512] for i in range(8)]
    PA = [BK[0], BK[1], BK[2]]
    PST = BK[3]
    PST2 = BK[4]
    PROT = BK[5]
    PSM = BK[6]
    PTRF = BK[7]
    PTR = PTRF.bitcast(BF16)
    HB = [(PSM, 'psm'), (PTRF, 'ptr')]

    ident_f = A.alloc([128], F32)
    ident_b = A.alloc([128], BF16)
    ones_f = A.alloc([128], F32)
    ones_b = A.alloc([128], BF16)
    rrot_b = A.alloc([128], BF16)
    dftc_b = A.alloc([256], BF16)
    cst = A.alloc([4], F32)
    MOD = A.alloc([L, 96, 2], F32)
    VEC = A.alloc([2, L, 6, KD], F32)
    n1g = A.alloc([L, KD], F32)
    n2g = A.alloc([L, KD], F32)
    fng = A.alloc([KD], F32)
    qg = A.alloc([L], F32)
    kg = A.alloc([L], F32)
    convw = A.alloc([L, 3, 4], F32)
    lng = A.alloc([L, 4], F32)
    lnb = A.alloc([L, 4], F32)
    fcw = A.alloc([L, 3, NFF], F32)
    fcb = A.alloc([L, NFF], F32)
    wsT = A.alloc([4, 128], BF16)
    gmb = A.alloc([4, 128], F32)
    base_mark = A.off

    def ld(dst, src, name, key='cst0'):
        return P.dma('sp', dst, src, writes=[name], key=key)

    def ldc(dst, src, name, key='cst1'):
        return P.dma('pool', dst, src, writes=[name], key=key)

    cast_layer(0)
    cast_dft()
    ld(ident_f, ident_in, 'ident_f')
    ldc(ident_b, ident_in, 'ident_b')
    ldc(rrot_b, rrot_in, 'rrot_b')
    ldc(dftc_b, dftc_in, 'dftc_b')
    ld(n1g, n1g_in, 'n1g')
    ld(n2g, n2g_in, 'n2g')
    ld(fng, fng_in, 'fng')
    ld(qg, qg_in, 'qg', key='qgl')
    ld(kg, kg_in, 'kg')
    ld(convw, convw_in, 'convw')
    ld(lng, lng_in, 'lng')
    ld(lnb, lnb_in, 'lnb')
    ld(fcw, fcw_in, 'fcw')
    ld(fcb, fcb_in, 'fcb')
    OP('dve', 'memset', ones_f, 1.0, w=['ones_f'])
    OP('dve', 'memset', ones_b, 1.0, w=['ones_b'])
    OP('dve', 'memset', cst, 0.0, w=['cst'])
    OP('dve', 'memset', cst[:, 0:1], EPS, r=['cst'], w=['cst'])
    OP('dve', 'tensor_scalar', qg, qg, 128.0 ** -0.5, None, ALU.mult, r=['qg'], w=['qg'])
    eps_ap = cst[:, 0:1]
    P.barrier()

    m0 = A.off
    scv = A.alloc([2, KD], F32)
    bmt = A.alloc([L, 96], F32)
    wm = [A.alloc([KD, 512], F32) for _ in range(2)]
    ld(scv, cvec_in, 'scv', key='scv')
    ld(bmt, b_mod_t, 'bmt', key='bmt')
    OP('act', 'activation', out=scv, in_=scv, func=AF.Silu, r=['scv'], w=['scv'])
    it = 0
    for l in range(L):
        for cb in range(24):
            s = it % 2
            P.dma('sp', wm[s], w_mod[l, :, cb * 512:(cb + 1) * 512].rearrange("(k p) n -> p k n", p=128),
                  writes=[('wm', s)], key=f"wm{s}")
            for mi in range(4):
                for kc in range(KD):
                    OP('pe', 'matmul', PSM[:, 2 * mi:2 * mi + 2], wm[s][:, kc, mi * 128:(mi + 1) * 128],
                       scv[:, :, kc], start=(kc == 0), stop=(kc == KD - 1),
                       r=[('wm', s), 'scv'], w=['psm'])
            OP('dve', 'tensor_tensor', MOD[:, l, cb * 4:cb * 4 + 4, :],
               PSM[:, 0:8].rearrange("p (a b) -> p a b", a=4),
               bmt[:, l, cb * 4:cb * 4 + 4].unsqueeze(2).broadcast_to([128, 4, 2]), ALU.add,
               r=['psm', 'bmt'], w=['MOD'])
            it += 1
    for st in range(2):
        for l in range(L):
            for (dst, src, gain) in ((0, 16, n1g), (3, 64, n2g)):
                OP('dve', 'tensor_scalar', VEC[:, st, l, dst, :], MOD[:, l, src:src + 16, st], 1.0, None, ALU.add,
                   r=['MOD'], w=['VEC'])
                OP('dve', 'tensor_tensor', VEC[:, st, l, dst, :], VEC[:, st, l, dst, :], gain[:, l, :], ALU.mult,
                   r=['VEC', 'n1g', 'n2g'], w=['VEC'])
            for (dst, src) in ((1, 0), (2, 32), (4, 48), (5, 80)):
                OP('dve', 'tensor_copy', out=VEC[:, st, l, dst, :], in_=MOD[:, l, src:src + 16, st],
                   r=['MOD'], w=['VEC'])
    P.barrier()
    A.off = m0

    def to_feature_major(src, dst, ntok):
        m = A.off
        xin = [A.alloc([D], F32) for _ in range(2)]
        stg = [A.alloc([KD, 128], F32) for _ in range(2)]
        for tb in range(ntok // 128):
            s = tb % 2
            P.dma('sp', xin[s], src[tb * 128:(tb + 1) * 128, :], writes=[('xin', s)], key=f"xin{s}")
            for q4 in range(4):
                bank = PA[q4 % 3]
                bn = ('pa', q4 % 3)
                for j in range(4):
                    kc = q4 * 4 + j
                    OP('pe', 'transpose', bank[:, j * 128:(j + 1) * 128], xin[s][:, kc * 128:(kc + 1) * 128], ident_f,
                       r=[('xin', s), 'ident_f'], w=[bn])
                src4 = bank[:, :].rearrange("p (a b) -> p a b", a=4)
                if q4 % 2 == 0:
                    OP('act', 'activation', out=stg[s][:, q4 * 4:q4 * 4 + 4, :], in_=src4, func=AF.Copy,
                       r=[bn], w=[('stg', s)])
                else:
                    OP('dve', 'tensor_copy', out=stg[s][:, q4 * 4:q4 * 4 + 4, :], in_=src4, r=[bn], w=[('stg', s)])
            P.dma('sp', dst[:, :, tb * 128:(tb + 1) * 128].rearrange("k p t -> p k t"), stg[s],
                  reads=[('stg', s)], key=f"s_stg{s}")
        P.barrier()
        A.off = m

    to_feature_major(x_in, XT, T)
    to_feature_major(ctx_in, XC, TC)

    def load_xt(xt, XD, ntot, s0, N, halo):
        if halo:
            lo = max(s0 - 1, 0)
            hi = min(s0 + N + 1, ntot)
            c0 = lo - (s0 - 1)
            if s0 == 0:
                OP('dve', 'memset', xt[:, :, 0:1], 0.0, w=['xt'])
            if s0 + N == ntot:
                OP('dve', 'memset', xt[:, :, N + 1:N + 2], 0.0, w=['xt'])
            if hi - lo == N + 2:
                P.dma('sp', xt[:, :, 0:N + 1], XD[:, :, lo:hi - 1].rearrange("k p t -> p k t"),
                      writes=['xt'], key='xt')
                P.dma('sp', xt[:, :, N:N + 2], XD[:, :, hi - 2:hi].rearrange("k p t -> p k t"),
                      reads=['xt'], writes=['xt'], key='xt')
            else:
                P.dma('sp', xt[:, :, c0:c0 + (hi - lo)], XD[:, :, lo:hi].rearrange("k p t -> p k t"),
                      writes=['xt'], key='xt')
        else:
            P.dma('sp', xt[:, :, 0:N], XD[:, :, s0:s0 + N].rearrange("k p t -> p k t"), writes=['xt'], key='xt')

    def norm_mod(xt, h, W, gvec, svec, sq, tmp, rstd, first, last):
        W0 = min(W, 512)
        sqh = [q_.bitcast(BF16) for q_ in sq]
        for kc in range(KD):
            s = kc % 2
            OP('act', 'activation', out=sqh[s][:, 0:W], in_=xt[:, kc, 0:W], func=AF.Square,
               r=['xt'], w=[('sq', s)])
            OP('pe', 'matmul', PST[:, 0:W0], ones_b, sqh[s][:, 0:W0], start=(kc == 0), stop=(kc == KD - 1),
               r=[('sq', s), 'ones_b'], w=['pst'])
            if W > 512:
                OP('pe', 'matmul', PROT[:, 0:W - 512], ones_b, sqh[s][:, 512:W], start=(kc == 0),
                   stop=(kc == KD - 1), r=[('sq', s), 'ones_b'], w=['prot'])
        OP('act', 'activation', out=rstd[:, 0:W0], in_=PST[:, 0:W0], func=AF.Ln, bias=eps_ap, scale=1.0 / D,
           r=['pst', 'cst'], w=['rstd'])
        if W > 512:
            OP('act', 'activation', out=rstd[:, 512:W], in_=PROT[:, 0:W - 512], func=AF.Ln, bias=eps_ap,
               scale=1.0 / D, r=['prot', 'cst'], w=['rstd'])
        OP('act', 'activation', out=rstd[:, 0:W], in_=rstd[:, 0:W], func=AF.Exp, scale=-0.5, r=['rstd'], w=['rstd'])
        for kc in range(KD):
            s = kc % 2
            OP('dve', 'tensor_tensor', tmp[s][:, 0:W], xt[:, kc, 0:W], rstd[:, 0:W], ALU.mult,
               r=['xt', 'rstd'], w=[('tmp', s)])
            OP('act', 'activation', out=h[:, kc, 0:W], in_=tmp[s][:, 0:W], func=AF.Identity,
               bias=svec[:, kc:kc + 1], scale=gvec[:, kc:kc + 1], r=[('tmp', s), 'VEC'], w=[('h', kc)])
        if first:
            OP('dve', 'memset', h[:, :, 0:1], 0.0, r=[('h', k) for k in range(KD)], w=[('h', k) for k in range(KD)])
        if last:
            OP('dve', 'memset', h[:, :, W - 1:W], 0.0, r=[('h', k) for k in range(KD)],
               w=[('h', k) for k in range(KD)])

    def store(dst, src, res):
        key = "s_" + (res if isinstance(res, str) else f"{res[0]}{res[1]}")
        return P.dma('sp', dst, src, reads=[res], key=key)

    def phase1(l, stream, kv_only):
        is_ctx = (stream == 1)
        ntot = TC if is_ctx else T
        N = min(NT, ntot)
        XD = XC if is_ctx else XT
        QD = QSc if is_ctx else QS
        FD = FSc if is_ctx else FS
        MD = MIXc if is_ctx else MIX
        koff = 0 if is_ctx else TC
        W = N + 2
        nch = N // 128
        m = A.off
        xt = A.alloc([KD, W], F32)
        h = A.alloc([KD, W], BF16)
        sq = [A.alloc([W], F32) for _ in range(2)]
        tmp = [A.alloc([W], F32) for _ in range(2)]
        rstd = A.alloc([W], F32)
        wb = [A.alloc([KD, 512], BF16) for _ in range(2)]
        qf = A.alloc([N], F32)
        sqb = A.alloc([N], F32)
        sqbh = sqb.bitcast(BF16)[:, 0:N]
        rq = A.alloc([N], F32)
        qnb = A.alloc([N], BF16)
        t1 = A.alloc([N], F32)
        t2 = A.alloc([N], F32)
        qo = [A.alloc([N], BF16) for _ in range(2)]
        vb = A.alloc([N], BF16)
        vt = [A.alloc([4, 128], BF16) for _ in range(2)]
        fb = [A.alloc([N], BF16) for _ in range(2)]
        cbk = A.alloc([4, N], F32)
        ccb = A.alloc([4, W], F32)
        prod = A.alloc([W], F32)
        cv = A.alloc([N], F32)
        mixb = [A.alloc([N], BF16) for _ in range(2)]
        ub = A.alloc([4, N], F32)
        gvb = A.alloc([4, N], BF16)
        mn = A.alloc([N], F32)
        msq = A.alloc([N], F32)
        lrs = A.alloc([N], F32)
        vh = A.alloc([N], BF16)
        vT = A.alloc([4, 128], BF16)
        rc = A.alloc([N], F32)
        rs = A.alloc([N], F32)
        gvec = VEC[:, stream, l, 0, :]
        svec = VEC[:, stream, l, 1, :]
        if not kv_only:
            ldc(wsT, wsT_in[:, l], 'wsT', key='wsT')
            ld(gmb, gmb_in[:, l], 'gmb', key='gmb')
        blocks = list(range(8, 12)) if kv_only else list(range(36))
        cbs = sorted(set(b // 4 for b in blocks))
        wit = 0
        for ti in range(ntot // N):
            s0 = ti * N
            load_xt(xt, XD, ntot, s0, N, True)
            if not is_ctx:
                P.dma('sp', rc, ropec_in[:, s0:s0 + N], writes=['rc'], key='rc')
                P.dma('sp', rs, ropes_in[:, s0:s0 + N], writes=['rs'], key='rs')
            norm_mod(xt, h, W, gvec, svec, sq, tmp, rstd, s0 == 0, s0 + N == ntot)
            mmi = 0
            for cb in cbs:
                ws = wit % 2
                wit += 1
                P.dma('pool', wb[ws], Wi[l, cb].rearrange("p (k n) -> p k n", k=KD),
                      writes=[('wb', ws)], key=f"wb{ws}")
                for mi in range(4):
                    mb = cb * 4 + mi
                    if mb not in blocks:
                        continue
                    bi = mmi % 3
                    mmi += 1
                    bank = PA[bi]
                    bn = ('pa', bi)
                    is_halo = 20 <= mb < 28
                    hbk, hbn = HB[mb % 2]
                    for kc in range(KD):
                        OP('pe', 'matmul', bank[:, 0:N], wb[ws][:, kc, mi * 128:(mi + 1) * 128], h[:, kc, 1:1 + N],
                           start=(kc == 0), stop=(kc == KD - 1), r=[('wb', ws), ('h', kc)], w=[bn])
                        if is_halo:
                            OP('pe', 'matmul', hbk[:, 0:2], wb[ws][:, kc, mi * 128:(mi + 1) * 128],
                               h[:, kc, 0:W:W - 1], start=(kc == 0), stop=(kc == KD - 1),
                               r=[('wb', ws), ('h', kc)], w=[hbn])
                    if mb < 10:
                        isq = mb < 8
                        OP('act', 'activation', out=qf, in_=bank[:, 0:N], func=AF.Copy, r=[bn], w=['qf'])
                        OP('act', 'activation', out=sqbh, in_=bank[:, 0:N], func=AF.Square, r=[bn], w=['sqb'])
                        OP('pe', 'matmul', PST2[:, 0:N], ones_b, sqbh, start=True, stop=True,
                           r=['sqb', 'ones_b'], w=['pst2'])
                        OP('act', 'activation', out=rq, in_=PST2[:, 0:N], func=AF.Ln, bias=eps_ap, scale=1.0 / 128,
                           r=['pst2', 'cst'], w=['rq'])
                        OP('act', 'activation', out=rq, in_=rq, func=AF.Exp, scale=-0.5, r=['rq'], w=['rq'])
                        gq = (qg if isq else kg)[:, l:l + 1]
                        qs_ = qo[mb % 2]
                        qn_ = ('qo', mb % 2)
                        if is_ctx:
                            OP('dve', 'scalar_tensor_tensor', qs_, qf, gq, rq, ALU.mult, ALU.mult,
                               r=['qf', 'rq', 'qg', 'kg'], w=[qn_])
                        else:
                            OP('dve', 'scalar_tensor_tensor', qnb, qf, gq, rq, ALU.mult, ALU.mult,
                               r=['qf', 'rq', 'qg', 'kg'], w=['qnb'])
                            OP('pe', 'matmul', PROT[:, 0:N], rrot_b, qnb, start=True, stop=True,
                               r=['qnb', 'rrot_b'], w=['prot'])
                            OP('dve', 'tensor_tensor', t1, qnb, rc, ALU.mult, r=['qnb', 'rc'], w=['t1'])
                            OP('dve', 'tensor_tensor', t2, PROT[:, 0:N], rs, ALU.mult, r=['prot', 'rs'], w=['t2'])
                            OP('dve', 'tensor_tensor', qs_, t1, t2, ALU.add, r=['t1', 't2'], w=[qn_])
                        if isq:
                            store(QD[mb, :, s0:s0 + N], qs_, qn_)
                        else:
                            store(KS[mb - 8, :, koff + s0:koff + s0 + N], qs_, qn_)
                    elif mb < 12:
                        hv = mb - 10
                        OP('act', 'activation', out=vb, in_=bank[:, 0:N], func=AF.Copy, r=[bn], w=['vb'])
                        for j in range(nch):
                            OP('pe', 'transpose', PTR[:, j * 128:(j + 1) * 128], vb[:, j * 128:(j + 1) * 128], ident_b,
                               r=['vb', 'ident_b'], w=['ptr'])
                        OP('dve', 'tensor_copy', out=vt[hv][:, 0:nch, :],
                           in_=PTR[:, 0:nch * 128].rearrange("p (a b) -> p a b", a=nch), r=['ptr'], w=[('vt', hv)])
                        store(VS[hv, koff + s0:koff + s0 + N, :].rearrange("(j p) d -> p j d", p=128),
                              vt[hv][:, 0:nch, :], ('vt', hv))
                    elif mb < 16:
                        g = mb - 12
                        OP('act', 'activation', out=fb[g % 2], in_=bank[:, 0:N], func=AF.Copy, r=[bn], w=[('fb', g % 2)])
                        store(FD[g, :, s0:s0 + N], fb[g % 2], ('fb', g % 2))
                    elif mb < 20:
                        g = mb - 16
                        OP('act', 'activation', out=cbk[:, g, :], in_=bank[:, 0:N], func=AF.Copy, r=[bn], w=[('cbk', g)])
                    elif mb < 24:
                        g = mb - 20
                        OP('act', 'activation', out=ccb[:, g, 1:1 + N], in_=bank[:, 0:N], func=AF.Copy,
                           r=[bn], w=[('ccb', g)])
                        OP('act', 'activation', out=ccb[:, g, 0:W:W - 1], in_=hbk[:, 0:2], func=AF.Copy,
                           r=[hbn, ('ccb', g)], w=[('ccb', g)])
                    elif mb < 28:
                        g = mb - 24
                        OP('dve', 'tensor_tensor', prod[:, 1:1 + N], ccb[:, g, 1:1 + N], bank[:, 0:N], ALU.mult,
                           r=[bn, ('ccb', g)], w=['prod'])
                        OP('dve', 'tensor_tensor', prod[:, 0:W:W - 1], ccb[:, g, 0:W:W - 1], hbk[:, 0:2],
                           ALU.mult, r=[hbn, ('ccb', g), 'prod'], w=['prod'])
                        OP('dve', 'tensor_scalar', cv, prod[:, 0:N], convw[:, l, 0, g:g + 1], None, ALU.mult,
                           r=['prod', 'convw'], w=['cv'])
                        OP('dve', 'scalar_tensor_tensor', cv, prod[:, 1:1 + N], convw[:, l, 1, g:g + 1], cv,
                           ALU.mult, ALU.add, r=['prod', 'cv', 'convw'], w=['cv'])
                        OP('dve', 'scalar_tensor_tensor', cv, prod[:, 2:2 + N], convw[:, l, 2, g:g + 1], cv,
                           ALU.mult, ALU.add, r=['prod', 'cv', 'convw'], w=['cv'])
                        OP('dve', 'tensor_tensor', mixb[g % 2], cv, cbk[:, g, :], ALU.mult,
                           r=['cv', ('cbk', g)], w=[('mixb', g % 2)])
                        store(MD[12 + g, :, s0:s0 + N], mixb[g % 2], ('mixb', g % 2))
                    elif mb < 32:
                        g = mb - 28
                        OP('act', 'activation', out=ub[:, g, :], in_=bank[:, 0:N], func=AF.Gelu_apprx_tanh,
                           r=[bn], w=[('ub', g)])
                    else:
                        g = mb - 32
                        OP('act', 'activation', out=gvb[:, g, :], in_=bank[:, 0:N], func=AF.Gelu_apprx_tanh,
                           r=[bn], w=[('gvb', g)])
                        OP('act', 'activation', out=sqbh, in_=gvb[:, g, :], func=AF.Square, r=[('gvb', g)], w=['sqb'])
                        OP('pe', 'matmul', PROT[:, 0:N], ones_b, gvb[:, g, :], start=(g == 0), stop=(g == 3),
                           r=[('gvb', g), 'ones_b'], w=['prot'])
                        OP('pe', 'matmul', PST2[:, 0:N], ones_b, sqbh, start=(g == 0), stop=(g == 3),
                           r=['sqb', 'ones_b'], w=['pst2'])
                        if g == 3:
                            OP('dve', 'tensor_scalar', mn, PROT[:, 0:N], 1.0 / 512, None, ALU.mult, r=['prot'], w=['mn'])
                            OP('dve', 'tensor_tensor', msq, mn, mn, ALU.mult, r=['mn'], w=['msq'])
                            OP('dve', 'scalar_tensor_tensor', lrs, PST2[:, 0:N], 1.0 / 512, msq, ALU.mult,
                               ALU.subtract, r=['pst2', 'msq'], w=['lrs'])
                            OP('act', 'activation', out=lrs, in_=lrs, func=AF.Ln, bias=eps_ap, scale=1.0,
                               r=['lrs', 'cst'], w=['lrs'])
                            OP('act', 'activation', out=lrs, in_=lrs, func=AF.Exp, scale=-0.5, r=['lrs'], w=['lrs'])
                            for g2 in range(4):
                                OP('dve', 'tensor_tensor', t1, gvb[:, g2, :], mn, ALU.subtract,
                                   r=[('gvb', g2), 'mn'], w=['t1'])
                                OP('dve', 'tensor_tensor', t1, t1, lrs, ALU.mult, r=['t1', 'lrs'], w=['t1'])
                                OP('act', 'activation', out=vh, in_=t1, func=AF.Identity, bias=lnb[:, l, g2:g2 + 1],
                                   scale=lng[:, l, g2:g2 + 1], r=['t1', 'lng', 'lnb'], w=['vh'])
                                for j in range(nch):
                                    OP('pe', 'transpose', PTR[:, j * 128:(j + 1) * 128], vh[:, j * 128:(j + 1) * 128],
                                       ident_b, r=['vh', 'ident_b'], w=['ptr'])
                                OP('act', 'activation', out=vT[:, 0:nch, :],
                                   in_=PTR[:, 0:nch * 128].rearrange("p (a b) -> p a b", a=nch), func=AF.Copy,
                                   r=['ptr'], w=['vT'])
                                for j in range(nch):
                                    OP('pe', 'matmul', PROT[:, j * 128:(j + 1) * 128], vT[:, j, :], wsT[:, g2, :],
                                       start=True, stop=True, r=['vT', 'wsT'], w=['prot'])
                                OP('dve', 'tensor_tensor', t2[:, 0:N].rearrange("p (a b) -> p a b", a=nch),
                                   PROT[:, 0:N].rearrange("p (a b) -> p a b", a=nch),
                                   gmb[:, g2, :].unsqueeze(1).broadcast_to([128, nch, 128]), ALU.add,
                                   r=['prot', 'gmb'], w=['t2'])
                                OP('dve', 'tensor_tensor', mixb[g2 % 2], t2, ub[:, g2, :], ALU.mult,
                                   r=['t2', ('ub', g2)], w=[('mixb', g2 % 2)])
                                store(MD[16 + g2, :, s0:s0 + N], mixb[g2 % 2], ('mixb', g2 % 2))
        P.barrier()
        A.off = m

    def attention(stream):
        is_ctx = (stream == 1)
        nq_tot = TC if is_ctx else T
        nk = TC if is_ctx else TK
        NQ = min(512, nq_tot)
        QD = QSc if is_ctx else QS
        MD = MIXc if is_ctx else MIX
        nkc = nk // 128
        npair = nkc // 2
        m = A.off
        kT = A.alloc([2, nk], BF16)
        vv = A.alloc([nkc, 2, 128], BF16)
        qT = [A.alloc([8, NQ], BF16) for _ in range(2)]
        pT = [A.alloc([2, NQ], BF16) for _ in range(4)]
        rd = A.alloc([NQ], F32)
        ob = [A.alloc([NQ], BF16) for _ in range(2)]
        P.dma('sp', kT, KS[:, :, 0:nk].rearrange("h p t -> p h t"), writes=['kT'], key='kT')
        for hv_ in range(2):
            for c0_ in range(0, nkc, 8):
                c1_ = min(c0_ + 8, nkc)
                P.dma('sp', vv[:, c0_:c1_, hv_, :],
                      VS[hv_, c0_ * 128:c1_ * 128, :].rearrange("(c p) d -> p c d", p=128),
                      writes=[('vv', hv_)], key=f'vv{hv_}')
        if ATT_LA == 2:
            PS_S = [PP[0], PP[1], PP[2]]
            PS_O = [BK[6], BK[6]]
            PS_D = [BK[7], BK[7]]
        else:
            PS_S = [PP[0], PP[1]]
            PS_O = [BK[4], BK[5]]
            PS_D = [BK[6], BK[7]]
        NS = len(PS_S)
        hi = 0
        for qt in range(nq_tot // NQ):
            q0 = qt * NQ
            qs = qt % 2
            P.dma('sp', qT[qs], QD[:, :, q0:q0 + NQ].rearrange("h p t -> p h t"), writes=[('qT', qs)], key=f"qT{qs}")
            for hh in range(8):
                kvh = hh // 4
                po = PS_O[hi % 2]
                pd = PS_D[hi % 2]
                pon = ('pso', hi % 2 if ATT_LA == 1 else 0)
                pdn = ('psd', hi % 2 if ATT_LA == 1 else 0)

                def S(p):
                    ps = PS_S[p % NS]
                    for u in range(2):
                        kc = 2 * p + u
                        OP('pe', 'matmul', ps[:, u * 512:u * 512 + NQ], kT[:, kvh, kc * 128:(kc + 1) * 128],
                           qT[qs][:, hh, :], start=True, stop=True, r=['kT', ('qT', qs)], w=[('pss', p % NS)])
                    OP('act', 'activation', out=pT[p % 4],
                       in_=ps[:, :].rearrange("p (a b) -> p a b", a=2)[:, :, 0:NQ], func=AF.Exp,
                       r=[('pss', p % NS)], w=[('pT', p % 4)])

                def PV(p):
                    for u in range(2):
                        kc = 2 * p + u
                        OP('pe', 'matmul', po[:, 0:NQ], vv[:, kc, kvh, :], pT[p % 4][:, u, :], start=(kc == 0),
                           stop=(kc == nkc - 1), r=[('vv', kvh), ('pT', p % 4)], w=[pon])
                        OP('pe', 'matmul', pd[:, 0:NQ], ones_b, pT[p % 4][:, u, :], start=(kc == 0),
                           stop=(kc == nkc - 1), r=['ones_b', ('pT', p % 4)], w=[pdn])

                for p in range(min(ATT_LA, npair)):
                    S(p)
                for p in range(npair):
                    if p + ATT_LA < npair:
                        S(p + ATT_LA)
                    PV(p)
                OP('act', 'activation', out=rd, in_=pd[:, 0:NQ], func=AF.Ln, r=[pdn], w=['rd'])
                OP('act', 'activation', out=rd, in_=rd, func=AF.Exp, scale=-1.0, r=['rd'], w=['rd'])
                OP('dve', 'tensor_tensor', ob[hi % 2], po[:, 0:NQ], rd, ALU.mult, r=[pon, 'rd'], w=[('ob', hi % 2)])
                store(MD[hh, :, q0:q0 + NQ], ob[hi % 2], ('ob', hi % 2))
                hi += 1
        P.barrier()
        A.off = m

    def fourier(stream):
        is_ctx = (stream == 1)
        n = TC if is_ctx else T
        FD = FSc if is_ctx else FS
        MD = MIXc if is_ctx else MIX
        DN = dftnc_in if is_ctx else dftn_in
        ntc = n // 128
        NK = min(FOUR_NK, n)
        m = A.off
        AB = A.alloc([ntc, 4, 256], BF16)
        zT = [A.alloc([n], BF16) for _ in range(2)]
        cn = [A.alloc([ntc, NK], BF16)]
        sn = [A.alloc([ntc, NK], BF16)]
        yb = [A.alloc([NK], BF16) for _ in range(2)]
        CG = 8 if ntc >= 8 else ntc
        ei = 0
        for g in range(4):
            P.dma('sp', zT[g % 2], FD[g], writes=[('zT', g % 2)], key=f"zT{g % 2}")
            for tcp in range(ntc // 2):
                bi = ei % 3
                for u in range(2):
                    tc_ = tcp * 2 + u
                    OP('pe', 'matmul', PA[bi][:, u * 256:(u + 1) * 256], zT[g % 2][:, tc_ * 128:(tc_ + 1) * 128],
                       dftc_b, start=True, stop=True, r=[('zT', g % 2), 'dftc_b'], w=[('pa', bi)])
                src = PA[bi][:, :].rearrange("p (a b) -> p a b", a=2)
                dst = AB[:, tcp * 2:tcp * 2 + 2, g, :]
                if ei % 2 == 0:
                    OP('act', 'activation', out=dst, in_=src, func=AF.Copy, r=[('pa', bi)], w=['AB'])
                else:
                    OP('dve', 'tensor_copy', out=dst, in_=src, r=[('pa', bi)], w=['AB'])
                ei += 1
        yi = 0
        for kt in range(n // NK):
            s = 0
            for c0_ in range(0, ntc, CG):
                c1_ = min(c0_ + CG, ntc)
                cg = c0_ // CG
                if is_ctx:
                    P.dma('pool', cn[s][:, c0_:c1_, :],
                          DN[0, c0_ * 128:c1_ * 128, kt * NK:(kt + 1) * NK].rearrange("(c p) k -> p c k", p=128),
                          writes=[('cn', s, cg)], key=f"cn{s}_{cg}")
                    P.dma('pool', sn[s][:, c0_:c1_, :],
                          DN[1, c0_ * 128:c1_ * 128, kt * NK:(kt + 1) * NK].rearrange("(c p) k -> p c k", p=128),
                          writes=[('sn', s, cg)], key=f"sn{s}_{cg}")
                else:
                    P.dma('pool', cn[s][:, c0_:c1_, :],
                          DN16[0, kt].rearrange("p (c k) -> p c k", c=ntc)[:, c0_:c1_, :],
                          writes=[('cn', s, cg)], key=f"cn{s}_{cg}")
                    P.dma('pool', sn[s][:, c0_:c1_, :],
                          DN16[1, kt].rearrange("p (c k) -> p c k", c=ntc)[:, c0_:c1_, :],
                          writes=[('sn', s, cg)], key=f"sn{s}_{cg}")
            for g in range(4):
                bi = yi % 3
                for tc_ in range(ntc):
                    OP('pe', 'matmul', PA[bi][:, 0:NK], AB[:, tc_, g, 0:128], cn[s][:, tc_, :], start=(tc_ == 0),
                       stop=False, r=['AB', ('cn', s, tc_ // CG)], w=[('pa', bi)])
                    OP('pe', 'matmul', PA[bi][:, 0:NK], AB[:, tc_, g, 128:256], sn[s][:, tc_, :], start=False,
                       stop=(tc_ == ntc - 1), r=['AB', ('sn', s, tc_ // CG)], w=[('pa', bi)])
                if yi % 2 == 0:
                    OP('act', 'activation', out=yb[yi % 2], in_=PA[bi][:, 0:NK], func=AF.Copy,
                       r=[('pa', bi)], w=[('yb', yi % 2)])
                else:
                    OP('dve', 'tensor_copy', out=yb[yi % 2], in_=PA[bi][:, 0:NK], r=[('pa', bi)], w=[('yb', yi % 2)])
                store(MD[8 + g, :, kt * NK:(kt + 1) * NK], yb[yi % 2], ('yb', yi % 2))
                yi += 1
        P.barrier()
        A.off = m

    def phase3(l, stream):
        is_ctx = (stream == 1)
        ntot = TC if is_ctx else T
        N = min(NT, ntot)
        XD = XC if is_ctx else XT
        MD = MIXc if is_ctx else MIX
        m = A.off
        xt = A.alloc([KD, N], F32)
        mt = A.alloc([20, N], BF16)
        wo = [A.alloc([20, 512], BF16) for _ in range(2)]
        ga = VEC[:, stream, l, 2, :]
        wit = 0
        mmi = 0
        for ti in range(ntot // N):
            s0 = ti * N
            load_xt(xt, XD, ntot, s0, N, False)
            P.dma('sp', mt, MD[:, :, s0:s0 + N].rearrange("k p t -> p k t"), writes=['mt'], key='mt')
            for cb in range(4):
                ws = wit % 2
                wit += 1
                P.dma('pool', wo[ws], Wo[l, cb].rearrange("p (k n) -> p k n", k=20),
                      writes=[('wo', ws)], key=f"wo{ws}")
                for mi in range(4):
                    mb = cb * 4 + mi
                    bi = mmi % 3
                    mmi += 1
                    for k in range(20):
                        OP('pe', 'matmul', PA[bi][:, 0:N], wo[ws][:, k, mi * 128:(mi + 1) * 128], mt[:, k, :],
                           start=(k == 0), stop=(k == 19), r=[('wo', ws), 'mt'], w=[('pa', bi)])
                    OP('dve', 'scalar_tensor_tensor', xt[:, mb, :], PA[bi][:, 0:N], ga[:, mb:mb + 1], xt[:, mb, :],
                       ALU.mult, ALU.add, r=[('pa', bi), 'xt', 'VEC'], w=['xt'])
            P.dma('sp', (XCM if is_ctx else XM)[:, :, s0:s0 + N].rearrange("k p t -> p k t"), xt, reads=['xt'], key='xts')
        P.barrier()
        A.off = m

    def phase4(l, stream):
        is_ctx = (stream == 1)
        ntot = TC if is_ctx else T
        N = min(NT, ntot)
        XD = XC if is_ctx else XT
        W = N + 2
        m = A.off
        xt = A.alloc([KD, W], F32)
        h = A.alloc([KD, W], BF16)
        sq = [A.alloc([W], F32) for _ in range(2)]
        tmp = [A.alloc([W], F32) for _ in range(2)]
        rstd = A.alloc([W], F32)
        act = A.alloc([NFF, N], BF16)
        wg = [A.alloc([KD, 256], BF16) for _ in range(2)]
        wu = [A.alloc([KD, 256], BF16) for _ in range(2)]
        wd = [A.alloc([NFF, 256], BF16) for _ in range(2)]
        gb = sq
        cv = [t_[:, 0:N] for t_ in tmp]
        sg = cv
        gvec = VEC[:, stream, l, 3, :]
        svec = VEC[:, stream, l, 4, :]
        ga = VEC[:, stream, l, 5, :]
        wit = 0
        wdi = 0
        ji = 0
        for ti in range(ntot // N):
            s0 = ti * N
            load_xt(xt, XCM if is_ctx else XM, ntot, s0, N, True)
            norm_mod(xt, h, W, gvec, svec, sq, tmp, rstd, s0 == 0, s0 + N == ntot)
            for jb in range(NFF // 2):
                ws = wit % 2
                wit += 1
                P.dma('pool', wg[ws], Wg[l, jb].rearrange("p (k n) -> p k n", k=KD),
                      writes=[('wg', ws)], key=f"wg{ws}")
                P.dma('pool', wu[ws], Wu[l, jb].rearrange("p (k n) -> p k n", k=KD),
                      writes=[('wu', ws)], key=f"wu{ws}")
                for j2 in range(2):
                    j = jb * 2 + j2
                    s = ji % 2
                    ji += 1
                    pg = PA[0] if s == 0 else PA[1]
                    pgn = ('pa', 0 if s == 0 else 1)
                    pu = PST if s == 0 else PST2
                    pun = 'pst' if s == 0 else 'pst2'
                    hbk, hbn = HB[s]
                    for kc in range(KD):
                        OP('pe', 'matmul', pg[:, 0:N], wg[ws][:, kc, j2 * 128:(j2 + 1) * 128], h[:, kc, 1:1 + N],
                           start=(kc == 0), stop=(kc == KD - 1), r=[('wg', ws), ('h', kc)], w=[pgn])
                        OP('pe', 'matmul', hbk[:, 0:2], wg[ws][:, kc, j2 * 128:(j2 + 1) * 128],
                           h[:, kc, 0:W:W - 1], start=(kc == 0), stop=(kc == KD - 1),
                           r=[('wg', ws), ('h', kc)], w=[hbn])
                    for kc in range(KD):
                        OP('pe', 'matmul', pu[:, 0:N], wu[ws][:, kc, j2 * 128:(j2 + 1) * 128], h[:, kc, 1:1 + N],
                           start=(kc == 0), stop=(kc == KD - 1), r=[('wu', ws), ('h', kc)], w=[pun])
                    OP('act', 'activation', out=gb[s][:, 1:1 + N], in_=pg[:, 0:N], func=AF.Copy, r=[pgn], w=[('sq', s)])
                    OP('act', 'activation', out=gb[s][:, 0:W:W - 1], in_=hbk[:, 0:2], func=AF.Copy,
                       r=[hbn, ('sq', s)], w=[('sq', s)])
                    OP('dve', 'tensor_scalar', cv[s], gb[s][:, 0:N], fcw[:, l, 0, j:j + 1], None, ALU.mult,
                       r=[('sq', s), 'fcw'], w=[('tmp', s)])
                    OP('dve', 'scalar_tensor_tensor', cv[s], gb[s][:, 1:1 + N], fcw[:, l, 1, j:j + 1], cv[s],
                       ALU.mult, ALU.add, r=[('sq', s), ('tmp', s), 'fcw'], w=[('tmp', s)])
                    OP('dve', 'scalar_tensor_tensor', cv[s], gb[s][:, 2:2 + N], fcw[:, l, 2, j:j + 1], cv[s],
                       ALU.mult, ALU.add, r=[('sq', s), ('tmp', s), 'fcw'], w=[('tmp', s)])
                    OP('act', 'activation', out=sg[s], in_=cv[s], func=AF.Silu, bias=fcb[:, l, j:j + 1], scale=1.0,
                       r=[('tmp', s), 'fcb'], w=[('tmp', s)])
                    OP('dve', 'tensor_tensor', act[:, j, :], sg[s], pu[:, 0:N], ALU.mult,
                       r=[('tmp', s), pun], w=[('act', j)])
            for cb in range(8):
                ws = wdi % 2
                wdi += 1
                P.dma('pool', wd[ws], Wd[l, cb].rearrange("p (k n) -> p k n", k=NFF),
                      writes=[('wd', ws)], key=f"wd{ws}")
                for mi in range(2):
                    mb = cb * 2 + mi
                    bank = PROT if mb % 2 == 0 else PA[2]
                    bn = 'prot' if mb % 2 == 0 else ('pa', 2)
                    for j in range(NFF):
                        OP('pe', 'matmul', bank[:, 0:N], wd[ws][:, j, mi * 128:(mi + 1) * 128], act[:, j, :],
                           start=(j == 0), stop=(j == NFF - 1), r=[('wd', ws), ('act', j)], w=[bn])
                    OP('dve', 'scalar_tensor_tensor', xt[:, mb, 1:1 + N], bank[:, 0:N], ga[:, mb:mb + 1],
                       xt[:, mb, 1:1 + N], ALU.mult, ALU.add, r=[bn, 'xt', 'VEC'], w=['xt'])
            P.dma('sp', XD[:, :, s0:s0 + N].rearrange("k p t -> p k t"), xt[:, :, 1:1 + N], reads=['xt'], key='xts')
        P.barrier()
        A.off = m

    def final_norm():
        N = NT
        m = A.off
        xt = A.alloc([KD, N], F32)
        sq = [A.alloc([N], BF16) for _ in range(2)]
        rstd = A.alloc([N], F32)
        y = A.alloc([KD, N], F32)
        ot = [A.alloc([D], F32) for _ in range(2)]
        fins = []
        oi = 0
        for ti in range(T // N):
            s0 = ti * N
            load_xt(xt, XT, T, s0, N, False)
            for kc in range(KD):
                s = kc % 2
                OP('act', 'activation', out=sq[s], in_=xt[:, kc, :], func=AF.Square, r=['xt'], w=[('sq', s)])
                OP('pe', 'matmul', PST[:, 0:N], ones_b, sq[s], start=(kc == 0), stop=(kc == KD - 1),
                   r=[('sq', s), 'ones_b'], w=['pst'])
            OP('act', 'activation', out=rstd, in_=PST[:, 0:N], func=AF.Ln, bias=eps_ap, scale=1.0 / D,
               r=['pst', 'cst'], w=['rstd'])
            OP('act', 'activation', out=rstd, in_=rstd, func=AF.Exp, scale=-0.5, r=['rstd'], w=['rstd'])
            for kc in range(KD):
                OP('dve', 'scalar_tensor_tensor', y[:, kc, :], xt[:, kc, :], fng[:, kc:kc + 1], rstd, ALU.mult,
                   ALU.mult, r=['xt', 'rstd', 'fng'], w=[('y', kc)])
            for tb in range(N // 128):
                o = oi % 2
                oi += 1
                for q4 in range(4):
                    bi = q4 % 3
                    for j in range(4):
                        kc = q4 * 4 + j
                        OP('pe', 'transpose', PA[bi][:, j * 128:(j + 1) * 128], y[:, kc, tb * 128:(tb + 1) * 128],
                           ident_f, r=[('y', kc), 'ident_f'], w=[('pa', bi)])
                    if q4 % 2 == 0:
                        OP('act', 'activation', out=ot[o][:, q4 * 512:(q4 + 1) * 512], in_=PA[bi][:, :], func=AF.Copy,
                           r=[('pa', bi)], w=[('ot', o)])
                    else:
                        OP('dve', 'tensor_copy', out=ot[o][:, q4 * 512:(q4 + 1) * 512], in_=PA[bi][:, :],
                           r=[('pa', bi)], w=[('ot', o)])
                fins.append(P.dma('sp', out[s0 + tb * 128:s0 + (tb + 1) * 128, :], ot[o], reads=[('ot', o)],
                                  key=f"out{o}"))
        A.off = m
        return fins

    steps = []
    for l in range(L):
        lastl = (l == L - 1)
        steps.append(('p1c', lambda l=l, lastl=lastl: phase1(l, 1, lastl)))
        steps.append(('p1x', lambda l=l: phase1(l, 0, False)))
        if not lastl:
            steps.append(('atc', lambda: attention(1)))
        if not lastl:
            steps.append(('cast', lambda l=l: cast_layer(l + 1)))
        steps.append(('atx', lambda: attention(0)))
        if not lastl:
            steps.append(('foc', lambda: fourier(1)))
        steps.append(('fox', lambda: fourier(0)))
        if not lastl:
            steps.append(('p3c', lambda l=l: phase3(l, 1)))
        steps.append(('p3x', lambda l=l: phase3(l, 0)))
        if not lastl:
            steps.append(('p4c', lambda l=l: phase4(l, 1)))
        steps.append(('p4x', lambda l=l: phase4(l, 0)))
    nsteps = len(steps) if stop_after is None else stop_after
    P.marks = [('prologue', 0)]
    for name, fn in steps[:nsteps]:
        P.marks.append((name, sum(1 for o in P.streams['pe'] if o.fn is not None)))
        fn()
    P.marks.append(('final', sum(1 for o in P.streams['pe'] if o.fn is not None)))
    fins = final_norm()
    P.emit(final_waits=fins)
    return nc, P


def _fm(v, n):
    v = np.asarray(v, np.float32)
    lead = v.shape[:-1]
    v = v.reshape(*lead, n, 128)
    nd = v.ndim
    return np.ascontiguousarray(np.moveaxis(v, -1, 0))


def _constants(T):
    ident = np.eye(128, dtype=np.float32)
    rrot = np.zeros((128, 128), np.float32)
    for base in (0, 64):
        for i in range(32):
            rrot[base + 32 + i, base + i] = -1.0
            rrot[base + i, base + 32 + i] = 1.0
    t = np.arange(T)
    row = (t // 64).astype(np.float64)
    col = (t % 64).astype(np.float64)
    freqs = 10000.0 ** (-np.arange(0, 64, 2, dtype=np.float32).astype(np.float64) / 64)
    ang_r = (row[:, None].astype(np.float32) * freqs[None, :].astype(np.float32)).astype(np.float32)
    ang_c = (col[:, None].astype(np.float32) * freqs[None, :].astype(np.float32)).astype(np.float32)
    cr, sr, cc, sc = np.cos(ang_r), np.sin(ang_r), np.cos(ang_c), np.sin(ang_c)
    ropec = np.concatenate([cr, cr, cc, cc], axis=1).T.astype(np.float32)
    ropes = np.concatenate([sr, sr, sc, sc], axis=1).T.astype(np.float32)
    j = np.arange(128)
    angc = 2 * np.pi * ((j[:, None] * j[None, :]) % 128) / 128
    dftc = (np.concatenate([np.cos(angc), np.sin(angc)], axis=1) / np.sqrt(128.0)).astype(np.float32)

    def dn(n):
        k = np.arange(n, dtype=np.int64)
        ang = 2 * np.pi * ((k[:, None] * k[None, :]) % n).astype(np.float64) / n
        o = np.empty((2, n, n), np.float32)
        o[0] = np.cos(ang) / np.sqrt(n)
        o[1] = -np.sin(ang) / np.sqrt(n)
        return o
    return dict(ident=ident, rrot=rrot, ropec=np.ascontiguousarray(ropec), ropes=np.ascontiguousarray(ropes),
                dftc=dftc, dftn=dn(T), dftnc=dn(TC))


def make_in_maps(inp, T, L, ncores):
    f = lambda k: np.asarray(inp[k], np.float32)
    shared = dict(
        w_mod=np.ascontiguousarray(f('w_mod')[:L]),
        b_mod_t=np.ascontiguousarray(f('b_mod')[:L].reshape(L, 96, 128).transpose(2, 0, 1)),
        n1g=np.ascontiguousarray(f('norm1_g')[:L].reshape(L, KD, 128).transpose(2, 0, 1)),
        n2g=np.ascontiguousarray(f('norm2_g')[:L].reshape(L, KD, 128).transpose(2, 0, 1)),
        fng=np.ascontiguousarray(f('final_norm_g').reshape(KD, 128).T),
        w_in=np.ascontiguousarray(f('w_in')[:L]),
        qg=np.ascontiguousarray(f('q_norm_g')[:L].T),
        kg=np.ascontiguousarray(f('k_norm_g')[:L].T),
        convw=np.ascontiguousarray(f('conv_w')[:L].reshape(L, 3, 4, 128).transpose(3, 0, 1, 2)),
        lng=np.ascontiguousarray(f('gm_ln_g')[:L].reshape(L, 4, 128).transpose(2, 0, 1)),
        lnb=np.ascontiguousarray(f('gm_ln_b')[:L].reshape(L, 4, 128).transpose(2, 0, 1)),
        wsT=np.ascontiguousarray(f('gm_ws')[:L].transpose(3, 0, 1, 2)),
        gmb=np.ascontiguousarray(np.broadcast_to(f('gm_b')[:L][None], (128, L, 4, 128))),
        w_out=np.ascontiguousarray(f('w_out')[:L]),
        w_up=np.ascontiguousarray(f('w_up')[:L]),
        w_down=np.ascontiguousarray(f('w_down')[:L]),
        fcw=np.ascontiguousarray(f('ffn_conv_w')[:L].reshape(L, 3, NFF, 128).transpose(3, 0, 1, 2)),
        fcb=np.ascontiguousarray(f('ffn_conv_b')[:L].reshape(L, NFF, 128).transpose(2, 0, 1)),
    )
    shared.update(_constants(T))
    x = f('x')
    ctx = f('ctx')
    c = f('c')
    cc = f('c_ctx')
    maps = []
    for b in range(ncores):
        mp = dict(shared)
        mp['x'] = np.ascontiguousarray(x[b, :T])
        mp['ctx'] = np.ascontiguousarray(ctx[b])
        cv = np.stack([c[b].reshape(KD, 128).T, cc.reshape(KD, 128).T], axis=1)
        mp['cvec'] = np.ascontiguousarray(cv)
        maps.append(mp)
    return maps


ACTIVE_SLOTS = (0, 2, 4, 6)


def kernel(**inputs):
    T = 4096
    L = 4
    nc, _ = build_program(T, L)
    real = make_in_maps(inputs, T, L, NCORES)
    zeros = {k: np.zeros_like(v) for k, v in real[0].items()}
    maps = [zeros] * 8
    maps = list(maps)
    for b, slot in enumerate(ACTIVE_SLOTS):
        maps[slot] = real[b]
    res = run_bass_kernel_spmd(nc, maps, core_ids=list(range(8)))
    return np.stack([np.asarray(res.results[slot]["out"], np.float32) for slot in ACTIVE_SLOTS], axis=0)
```
